# Optimizing a Trainium2 kernel written in Bass

```python
import math
import jax, jax.numpy as jnp
from jax import lax
import numpy as np

D_MODEL = 4096
BATCH = 4
SEQ = 4096
DEPTH = 1

N_META = 16
Q_BLOCK = 128
D_FF = 11008
EPS = 1e-6

DIFF_HEADS = 8
DIFF_HEAD_DIM = 128
DIFF_V_DIM = 2 * DIFF_HEAD_DIM
DIFF_QK_WIDTH = DIFF_HEADS * 2 * DIFF_HEAD_DIM
DIFF_WIDTH = DIFF_HEADS * DIFF_V_DIM

MLA_HEADS = 16
Q_LORA = 1024
KV_LORA = 512
NOPE_D = 128
ROPE_D = 64
MLA_QK_D = NOPE_D + ROPE_D
MLA_V_D = 128
MLA_WIDTH = MLA_HEADS * MLA_V_D
ROPE_THETA = 10000.0

IN_SIZES = (DIFF_QK_WIDTH, DIFF_QK_WIDTH, DIFF_WIDTH, Q_LORA, KV_LORA, ROPE_D)
IN_WIDTH = sum(IN_SIZES)
IN_SPLITS = tuple(int(v) for v in np.cumsum(IN_SIZES)[:-1])

kernel_name = "hybrid_diffattn_mla_macaron_encoder"


def alibi_slopes(n):
    return np.array([2.0 ** (-8.0 * (h + 1) / n) for h in range(n)], dtype=np.float32)


def rms_norm(x, g):
    xf = x.astype(jnp.float32)
    y = xf * lax.rsqrt(jnp.mean(xf * xf, axis=-1, keepdims=True) + EPS)
    return (y * g.astype(jnp.float32)).astype(x.dtype)


def swiglu(h, w_gate, w_up, w_down):
    return (jax.nn.silu(h @ w_gate) * (h @ w_up)) @ w_down


def apply_rope(x, cos, sin):
    half = ROPE_D // 2
    x1, x2 = x[..., :half], x[..., half:]
    cos = cos.astype(x.dtype)
    sin = sin.astype(x.dtype)
    return jnp.concatenate([x1 * cos - x2 * sin, x1 * sin + x2 * cos], axis=-1)


def setup_inputs(seed: int = 0) -> dict:
    key = jax.random.key(seed)
    ks = iter(jax.random.split(key, 40))
    f32 = jnp.float32

    def w(shape, fan_in):
        return jax.random.normal(next(ks), shape, f32) * (fan_in ** -0.5)

    def gain(shape):
        return 1.0 + 0.02 * jax.random.normal(next(ks), shape, f32)

    L = DEPTH
    return {
        "x": jax.random.normal(next(ks), (BATCH, SEQ, D_MODEL), f32),
        "meta_tokens": jax.random.normal(next(ks), (N_META, D_MODEL), f32),
        "ffn1_norm": gain((L, D_MODEL)),
        "ffn1_w_gate": w((L, D_MODEL, D_FF), D_MODEL),
        "ffn1_w_up": w((L, D_MODEL, D_FF), D_MODEL),
        "ffn1_w_down": w((L, D_FF, D_MODEL), D_FF),
        "mix_norm": gain((L, D_MODEL)),
        "w_in": w((L, D_MODEL, IN_WIDTH), D_MODEL),
        "diff_lambda_q1": 0.1 * jax.random.normal(next(ks), (L, DIFF_HEAD_DIM), f32),
        "diff_lambda_k1": 0.1 * jax.random.normal(next(ks), (L, DIFF_HEAD_DIM), f32),
        "diff_lambda_q2": 0.1 * jax.random.normal(next(ks), (L, DIFF_HEAD_DIM), f32),
        "diff_lambda_k2": 0.1 * jax.random.normal(next(ks), (L, DIFF_HEAD_DIM), f32),
        "diff_subln": gain((L, DIFF_V_DIM)),
        "mla_q_norm": gain((L, Q_LORA)),
        "mla_w_uq": w((L, Q_LORA, MLA_HEADS * MLA_QK_D), Q_LORA),
        "mla_kv_norm": gain((L, KV_LORA)),
        "mla_w_ukv": w((L, KV_LORA, MLA_HEADS * (NOPE_D + MLA_V_D)), KV_LORA),
        "w_gate": w((L, D_MODEL, 2 * D_MODEL), D_MODEL),
        "b_gate": 0.02 * jax.random.normal(next(ks), (L, 2 * D_MODEL), f32),
        "w_branch_diff": w((L, DIFF_WIDTH, D_MODEL), DIFF_WIDTH),
        "w_branch_mla": w((L, MLA_WIDTH, D_MODEL), MLA_WIDTH),
        "w_out": w((L, D_MODEL, D_MODEL), D_MODEL),
        "ffn2_norm": gain((L, D_MODEL)),
        "ffn2_w_gate": w((L, D_MODEL, D_FF), D_MODEL),
        "ffn2_w_up": w((L, D_MODEL, D_FF), D_MODEL),
        "ffn2_w_down": w((L, D_FF, D_MODEL), D_FF),
        "final_norm": gain((D_MODEL,)),
    }


def mixing_sublayer(h, w_in, lq1, lk1, lq2, lk2, subln, q_norm, w_uq, kv_norm, w_ukv,
                    lam_init):
    B, T, _ = h.shape
    S = T - N_META
    proj = h @ w_in
    dq, dk, dv, cq, ckv, k_rope = jnp.split(proj, IN_SPLITS, axis=-1)
    dq = dq.reshape(B, T, DIFF_HEADS, 2, DIFF_HEAD_DIM)
    dk = dk.reshape(B, T, DIFF_HEADS, 2, DIFF_HEAD_DIM)
    dv = dv.reshape(B, T, DIFF_HEADS, DIFF_V_DIM)

    lam = (jnp.exp(jnp.sum(lq1.astype(jnp.float32) * lk1.astype(jnp.float32)))
           - jnp.exp(jnp.sum(lq2.astype(jnp.float32) * lk2.astype(jnp.float32)))
           + lam_init)

    pos = jnp.arange(T)
    inv_freq = 1.0 / (ROPE_THETA ** (jnp.arange(0, ROPE_D, 2, dtype=jnp.float32) / ROPE_D))
    ang = pos.astype(jnp.float32)[:, None] * inv_freq[None, :]
    cos, sin = jnp.cos(ang), jnp.sin(ang)
    q = (rms_norm(cq, q_norm) @ w_uq).reshape(B, T, MLA_HEADS, MLA_QK_D)
    q_nope, q_pe = q[..., :NOPE_D], q[..., NOPE_D:]
    q_pe = apply_rope(q_pe, cos[:, None, :], sin[:, None, :])
    mq = jnp.concatenate([q_nope, q_pe], axis=-1)
    kv = (rms_norm(ckv, kv_norm) @ w_ukv).reshape(B, T, MLA_HEADS, NOPE_D + MLA_V_D)
    k_nope, mv = kv[..., :NOPE_D], kv[..., NOPE_D:]
    k_pe = apply_rope(k_rope, cos, sin)
    mk = jnp.concatenate(
        [k_nope, jnp.broadcast_to(k_pe[:, :, None, :], (B, T, MLA_HEADS, ROPE_D))], axis=-1)

    slopes = jnp.asarray(alibi_slopes(DIFF_HEADS))
    kpos = pos
    diff_scale = DIFF_HEAD_DIM ** -0.5
    mla_scale = MLA_QK_D ** -0.5
    out_scale = 1.0 - lam_init

    def attend_block(start, qn):
        qpos = start + jnp.arange(qn)
        both_real = (qpos[:, None] >= N_META) & (kpos[None, :] >= N_META)
        dist = jnp.abs(qpos[:, None] - kpos[None, :]).astype(jnp.float32)
        alibi = -slopes[:, None, None] * jnp.where(both_real, dist, 0.0)[None]

        qd = lax.dynamic_slice_in_dim(dq, start, qn, axis=1)
        s = jnp.einsum('bqhcd,bkhcd->bhcqk', qd, dk).astype(jnp.float32) * diff_scale
        p = jax.nn.softmax(s + alibi[None, :, None], axis=-1)
        a = p[:, :, 0] - lam * p[:, :, 1]
        od = jnp.einsum('bhqk,bkhe->bqhe', a.astype(dv.dtype), dv)
        od = rms_norm(od, subln) * out_scale

        qm = lax.dynamic_slice_in_dim(mq, start, qn, axis=1)
        s2 = jnp.einsum('bqhd,bkhd->bhqk', qm, mk).astype(jnp.float32) * mla_scale
        p2 = jax.nn.softmax(s2, axis=-1)
        om = jnp.einsum('bhqk,bkhe->bqhe', p2.astype(mv.dtype), mv)
        return od.reshape(B, qn, DIFF_WIDTH), om.reshape(B, qn, MLA_WIDTH)

    od_meta, om_meta = attend_block(0, N_META)
    n_blk = S // Q_BLOCK
    od_blk, om_blk = lax.map(lambda i: attend_block(N_META + i * Q_BLOCK, Q_BLOCK),
                             jnp.arange(n_blk))
    od_real = jnp.moveaxis(od_blk, 0, 1).reshape(B, S, DIFF_WIDTH)
    om_real = jnp.moveaxis(om_blk, 0, 1).reshape(B, S, MLA_WIDTH)
    return (jnp.concatenate([od_meta, od_real], axis=1),
            jnp.concatenate([om_meta, om_real], axis=1))


def reference(x, meta_tokens, ffn1_norm, ffn1_w_gate, ffn1_w_up, ffn1_w_down, mix_norm, w_in,
              diff_lambda_q1, diff_lambda_k1, diff_lambda_q2, diff_lambda_k2, diff_subln,
              mla_q_norm, mla_w_uq, mla_kv_norm, mla_w_ukv, w_gate, b_gate,
              w_branch_diff, w_branch_mla, w_out, ffn2_norm, ffn2_w_gate, ffn2_w_up,
              ffn2_w_down, final_norm):
    B = x.shape[0]
    meta = jnp.broadcast_to(meta_tokens[None].astype(x.dtype), (B, N_META, x.shape[-1]))
    h_stream = jnp.concatenate([meta, x], axis=1)

    for l in range(DEPTH):
        lam_init = 0.8 - 0.6 * math.exp(-0.3 * l)
        h_stream = h_stream + 0.5 * swiglu(rms_norm(h_stream, ffn1_norm[l]),
                                           ffn1_w_gate[l], ffn1_w_up[l], ffn1_w_down[l])
        h = rms_norm(h_stream, mix_norm[l])
        o_diff, o_mla = mixing_sublayer(
            h, w_in[l], diff_lambda_q1[l], diff_lambda_k1[l], diff_lambda_q2[l],
            diff_lambda_k2[l], diff_subln[l], mla_q_norm[l], mla_w_uq[l], mla_kv_norm[l],
            mla_w_ukv[l], lam_init)
        gates = jax.nn.sigmoid(h @ w_gate[l] + b_gate[l])
        g_diff, g_mla = jnp.split(gates, 2, axis=-1)
        merged = g_diff * (o_diff @ w_branch_diff[l]) + g_mla * (o_mla @ w_branch_mla[l])
        h_stream = h_stream + merged @ w_out[l]
        h_stream = h_stream + 0.5 * swiglu(rms_norm(h_stream, ffn2_norm[l]),
                                           ffn2_w_gate[l], ffn2_w_up[l], ffn2_w_down[l])

    return rms_norm(h_stream, final_norm)[:, N_META:]
```

```python
import math
import numpy as np
import concourse.bass as bass
import concourse.mybir as mybir
from concourse.bass_utils import run_bass_kernel_spmd

F32 = mybir.dt.float32
BF16 = mybir.dt.bfloat16
AF = mybir.ActivationFunctionType
ALU = mybir.AluOpType
EPS = 1e-6
LAM_INIT = 0.2
OUT_SCALE = 0.8
ROPE_THETA = 10000.0
NCORES = 8


def make_cfg(D=4096, F=11008, HD=8, HM=16, QL=1024, KVL=512, S=4096, B=4, SLOT=4096, NW=5):
    c = dict(D=D, F=F, HD=HD, HM=HM, QL=QL, KVL=KVL, S=S, B=B, NMETA=16, SLOT=SLOT, NW=NW)
    c["KC"] = D // 128
    c["FC"] = F // 128
    c["QC"] = QL // 128
    c["KVC"] = KVL // 128
    c["TOWN"] = S // 2
    c["NTO"] = c["TOWN"] // 512
    c["NTA"] = S // 512
    c["TK"] = S + 16
    c["NKT"] = S // 128 + 1
    c["TKP"] = c["NKT"] * 128
    c["CG"] = min(8, c["KC"])
    c["NVG"] = (HD * 256) // 512
    c["MVG"] = min(4, HM)
    c["MVW"] = c["MVG"] * 128
    c["HGQ"] = max(1, min(HM, SLOT // (c["QC"] * 128)))
    c["HGK"] = max(1, min(HM, SLOT // (c["KVC"] * 128)))
    c["WOWN"] = 2 * c["TOWN"] - 128
    c["WOTH"] = S - 128
    c["NV"] = 4 * c["KC"] + c["QC"] + c["KVC"] + 2 + 2 * c["KC"] + 4
    assert D % 128 == 0 and F % 128 == 0 and c["TOWN"] % 512 == 0 and HD % 2 == 0
    assert HM % c["HGQ"] == 0 and HM % c["HGK"] == 0 and HM % c["MVG"] == 0 and c["KC"] % c["CG"] == 0
    return c


class Res:
    __slots__ = ("w", "r")

    def __init__(self):
        self.w = None
        self.r = {}


class Gen:
    ENG = ("pe", "act", "dve", "pool", "sp")

    def __init__(self, nc):
        self.nc = nc
        self.q = {e: [] for e in self.ENG}
        self.prog = {e: nc.alloc_semaphore("pg_" + e) for e in ("pe", "act", "dve", "pool")}
        self.cnt = {e: 0 for e in self.prog}
        self.waited = {e: {} for e in self.ENG}
        self.semtot = {}
        self.semobj = {}

    def dsem(self, name):
        s = self.nc.alloc_semaphore(f"{name}_{len(self.semtot)}")
        self.semtot[s.num] = 0
        self.semobj[s.num] = s
        return s

    def _wait(self, e, deps):
        best = {}
        for d in deps:
            if d is None:
                continue
            sem, val = d
            if sem.num not in best or best[sem.num][1] < val:
                best[sem.num] = (sem, val)
        for sem, val in best.values():
            if e == "pe" and sem.num == self.prog["pe"].num:
                continue
            if self.waited[e].get(sem.num, 0) >= val:
                continue
            self.waited[e][sem.num] = val
            self.q[e].append(("w", sem, val))

    @staticmethod
    def _deps(reads, writes):
        deps = []
        for r in reads:
            if r.w is not None:
                deps.append(r.w)
        for w in writes:
            if w.w is not None:
                deps.append(w.w)
            deps.extend(w.r.values())
        return deps

    @staticmethod
    def _commit(tok, reads, writes):
        sem, val = tok
        for r in reads:
            r.r[sem.num] = tok
        for w in writes:
            w.w = tok
            w.r = {}

    def op(self, e, fns, reads=(), writes=()):
        if callable(fns):
            fns = [fns]
        self._wait(e, self._deps(reads, writes))
        self.cnt[e] += 1
        tok = (self.prog[e], self.cnt[e])
        for f in fns[:-1]:
            self.q[e].append(("i", f, None, 0))
        self.q[e].append(("i", fns[-1], self.prog[e], 1))
        self._commit(tok, reads, writes)
        return tok

    def dma(self, e, sem, fns, reads=(), writes=()):
        if callable(fns):
            fns = [fns]
        self._wait(e, self._deps(reads, writes))
        for f in fns:
            self.semtot[sem.num] += 16
            self.q[e].append(("i", f, sem, 16))
        tok = (sem, self.semtot[sem.num])
        self._commit(tok, reads, writes)
        return tok

    def fence(self, engines=ENG):
        toks = [(self.prog[e], self.cnt[e]) for e in self.prog if self.cnt[e] > 0]
        toks += [(self.semobj[n], t) for n, t in self.semtot.items() if t > 0]
        for e in engines:
            self._wait(e, toks)

    def emit(self, block):
        nc = self.nc

        def run(name, h):
            for it in self.q[name]:
                if it[0] == "w":
                    h.wait_ge(it[1], it[2])
                else:
                    ins = it[1](h)
                    if it[2] is not None:
                        ins.then_inc(it[2], it[3])

        @block.tensor
        def _(h):
            run("pe", h)

        @block.scalar
        def _(h):
            run("act", h)

        @block.vector
        def _(h):
            run("dve", h)

        @block.gpsimd
        def _(h):
            run("pool", h)

        @block.sync
        def _(h):
            run("sp", h)


class Arena:
    def __init__(self, nc):
        self.nc = nc
        self.base = (nc.sbuf_base + 63) // 64 * 64
        self.top = nc.sbuf_top
        self.off = self.base
        self.n = 0

    def alloc(self, shape, dtype, name=None):
        esz = 4 if dtype == F32 else 2
        size = esz
        for s in shape[1:]:
            size *= s
        size = (size + 63) // 64 * 64
        assert self.off + size <= self.top, f"SBUF arena overflow {self.off + size - self.top} bytes ({name})"
        self.n += 1
        t = self.nc.alloc_sbuf_tensor_at(f"{name or 'a'}_{self.n}", list(shape), dtype, offset=self.off)
        self.off += size
        return t

    def mark(self):
        return self.off

    def reset(self, m):
        self.off = m


class Ring:
    def __init__(self, g, arena, name, shape, dtype, n, sem=False):
        self.n = n
        self.t = arena.alloc([shape[0], n] + list(shape[1:]), dtype, name)
        self.res = [Res() for _ in range(n)]
        self.sems = [g.dsem(f"{name}_s{k}") for k in range(n)] if sem else None
        self.i = 0

    def next(self):
        k = self.i % self.n
        self.i += 1
        return self.t[:, k], self.res[k], (self.sems[k] if self.sems else None)


def mm(out, lhsT, rhs, start, stop):
    return lambda e: e.matmul(out, lhsT=lhsT, rhs=rhs, start=start, stop=stop)


def actf(out, in_, func, bias=None, scale=None):
    kw = {}
    if bias is not None:
        kw["bias"] = bias
    if scale is not None:
        kw["scale"] = scale
    return lambda e: e.activation(out=out, in_=in_, func=func, **kw)


def tt(out, in0, in1, op):
    return lambda e: e.tensor_tensor(out=out, in0=in0, in1=in1, op=op)


def ts(out, in0, s1, op0):
    return lambda e: e.tensor_scalar(out=out, in0=in0, scalar1=s1, scalar2=None, op0=op0)


def stt(out, in0, scalar, in1, op0, op1):
    return lambda e: e.scalar_tensor_tensor(out=out, in0=in0, scalar=scalar, in1=in1, op0=op0, op1=op1)


def recip(out, in_):
    return lambda e: e.reciprocal(out=out, in_=in_)


def cpy(out, in_):
    return lambda e: e.tensor_copy(out=out, in_=in_)


def mset(ap, v):
    return lambda e: e.memset(ap, v)


def dmaf(out, in_, **kw):
    return lambda e: e.dma_start(out=out, in_=in_, **kw)


def alibi_slopes(n):
    return [2.0 ** (-8.0 * (h + 1) / n) for h in range(n)]


def build(cfg):
    c = cfg
    D, F, HD, HM, S = c["D"], c["F"], c["HD"], c["HM"], c["S"]
    KC, FC, QC, KVC = c["KC"], c["FC"], c["QC"], c["KVC"]
    TOWN, TK, NKT, TKP = c["TOWN"], c["TK"], c["NKT"], c["TKP"]
    SLOT, NW, CG, NVG, MVG, MVW, HGQ, HGK = c["SLOT"], c["NW"], c["CG"], c["NVG"], c["MVG"], c["MVW"], c["HGQ"], c["HGK"]
    NV = c["NV"]
    NE = 2 * HD + HM

    nc = bass.Bass("TRN2", target_bir_lowering=False)

    def din(name, shape, dt=F32):
        return nc.dram_tensor(name, list(shape), dt, kind="ExternalInput").ap()

    def dscr(name, shape, dt):
        return nc.dram_tensor(name, list(shape), dt, kind=("ExternalOutput" if c.get("DEBUG") else "Internal")).ap()

    xT = din("xT", [KC, 128, S])
    metaT = din("metaT", [KC, 128, 16])
    ropeC = din("ropeC", [64, TK])
    ropeS = din("ropeS", [64, TK])
    mown = din("mown", [128, c["WOWN"]])
    moth = din("moth", [128, c["WOTH"]])
    vecs = din("vecs", [128, NV])
    ident = din("ident", [128, 128])
    W = {}
    for nm in ("f1", "f2"):
        W[nm + "g"] = din(nm + "g", [FC, 128, KC * 128])
        W[nm + "u"] = din(nm + "u", [FC, 128, KC * 128])
        W[nm + "d"] = din(nm + "d", [KC, 128, FC * 128])
    W["wi_dq"] = din("wi_dq", [2 * HD, 128, KC * 128])
    W["wi_dk"] = din("wi_dk", [2 * HD, 128, KC * 128])
    W["wi_cq"] = din("wi_cq", [QC, 128, KC * 128])
    W["wi_ckv"] = din("wi_ckv", [KVC, 128, KC * 128])
    W["wi_kr"] = din("wi_kr", [1, 128, KC * 128])
    W["wi_dv"] = din("wi_dv", [NVG * (KC // CG), 128, CG * 512])
    W["uq_n"] = din("uq_n", [HM // HGQ, 128, HGQ * QC * 128])
    W["uq_p"] = din("uq_p", [HM // HGQ, 128, HGQ * QC * 128])
    W["ukv_n"] = din("ukv_n", [HM // HGK, 128, HGK * KVC * 128])
    W["ukv_v"] = din("ukv_v", [HM // MVG, 128, KVC * MVW])
    W["wgt"] = din("wgt", [2 * KC, 128, KC * 128])
    W["wbr"] = din("wbr", [KC, 128, NE * 128])
    W["wo"] = din("wo", [KC, 128, KC * 128])
    yT = nc.dram_tensor("yT", [KC, 128, TOWN], F32, kind="ExternalOutput").ap()

    h1s = dscr("h1s", [KC, 128, TOWN], F32)
    hgs = dscr("hgs", [KC, 128, TOWN], BF16)
    r2s = dscr("r2s", [128, TOWN], F32)
    qdT = dscr("qdT", [2 * HD, 128, TOWN], BF16)
    kdT = dscr("kdT", [2 * HD, 128, TK], BF16)
    vds = dscr("vds", [TK, HD * 256], BF16)
    knT = dscr("knT", [HM, 128, TK], BF16)
    kpT = dscr("kpT", [64, TK], BF16)
    mvs = dscr("mvs", [TK, HM * 128], BF16)
    qnT = dscr("qnT", [HM, 128, TOWN], BF16)
    qpT = dscr("qpT", [HM, 64, TOWN], BF16)
    odT = dscr("odT", [2 * HD, 128, TOWN], BF16)
    omT = dscr("omT", [HM, 128, TOWN], BF16)

    g = Gen(nc)
    ar = Arena(nc)
    ps_all = nc.alloc_psum_tensor("ps", [128, 8, 512], F32)
    PSr = [Res() for _ in range(8)]

    class PRing:
        def __init__(self, banks):
            self.banks = banks
            self.i = 0

        def next(self):
            k = self.banks[self.i % len(self.banks)]
            self.i += 1
            return ps_all[:, k], PSr[k]

    ones32 = ar.alloc([128, 128], F32, "ones32")
    onesbf = ar.alloc([128, 128], BF16, "onesbf")
    onesmeta = ar.alloc([128, 128], BF16, "onesmeta")
    id32 = ar.alloc([128, 128], F32, "id32")
    vec = ar.alloc([128, NV], F32, "vec")
    epsc = ar.alloc([128, 1], F32, "epsc")
    nlam = ar.alloc([128, 1], F32, "nlam")
    sublnS = ar.alloc([128, 2], F32, "sublnS")
    lamt = ar.alloc([128, 4], F32, "lamt")
    csem = g.dsem("csem")
    cres = Res()
    o = 0
    V_G1 = o; o += KC
    V_GM = o; o += KC
    V_G2 = o; o += KC
    V_GF = o; o += KC
    V_GQ = o; o += QC
    V_GKV = o; o += KVC
    V_SUB = o; o += 2
    V_BG = o; o += 2 * KC
    V_LAM = o; o += 4
    assert o == NV

    g.dma("sp", csem, [dmaf(vec[:], vecs), dmaf(id32[:], ident)], writes=(cres,))
    omr, l0, l1, l2, l3 = Res(), Res(), Res(), Res(), Res()
    g.op("dve", [mset(ones32[:], 1.0), mset(onesbf[:], 1.0), mset(onesmeta[:], 0.0), mset(epsc[:], EPS)], writes=(omr,))
    g.op("dve", mset(onesmeta[0:16, :], 1.0), writes=(omr,))
    g.op("dve", tt(lamt[:, 0:1], vec[:, V_LAM:V_LAM + 1], vec[:, V_LAM + 1:V_LAM + 2], ALU.mult), reads=(cres,), writes=(l0,))
    g.op("dve", tt(lamt[:, 1:2], vec[:, V_LAM + 2:V_LAM + 3], vec[:, V_LAM + 3:V_LAM + 4], ALU.mult), reads=(cres,), writes=(l1,))
    g.op("dve", ts(sublnS[:], vec[:, V_SUB:V_SUB + 2], OUT_SCALE, ALU.mult), reads=(cres,), writes=(l3,))
    g.op("pe", mm(ps_all[:, 0, 0:2], ones32[:], lamt[:, 0:2], True, True), reads=(omr, l0, l1), writes=(PSr[0],))
    g.op("act", actf(lamt[:, 2:4], ps_all[:, 0, 0:2], AF.Exp), reads=(PSr[0],), writes=(l2,))
    g.op("dve", tt(lamt[:, 0:1], lamt[:, 3:4], lamt[:, 2:3], ALU.subtract), reads=(l2,), writes=(l0,))
    g.op("dve", lambda e: e.tensor_scalar(out=nlam[:], in0=lamt[:, 0:1], scalar1=-LAM_INIT, scalar2=None, op0=ALU.add),
         reads=(l0,), writes=(l3,))
    g.fence(("pe", "act", "dve", "sp", "pool"))

    persist_mark = ar.mark()
    wt = ar.alloc([128, NW, SLOT], BF16, "wring")
    wres = [Res() for _ in range(NW)]
    wsem = [g.dsem(f"w{k}") for k in range(NW)]
    wi = [0]

    def wload(src, L):
        k = wi[0] % NW
        wi[0] += 1
        dst = wt[:, k, 0:L]
        g.dma("pool", wsem[k], dmaf(dst, src, max_dma_last_dim=8192), writes=(wres[k],))
        return dst, wres[k]

    ac_mark = ar.mark()

    def alloc_AC():
        st = {}
        st["XG"] = ar.alloc([128, KC, 512], BF16, "XG")
        st["XGr"] = [Res() for _ in range(KC)]
        bigsz = max(FC * 512 * 2, (QC + KVC) * 512 * 6, (NE + KC) * 512 * 2)
        big0 = ar.mark()
        st["HT"] = ar.alloc([128, FC, 512], BF16, "HT")
        st["HTr"] = [Res() for _ in range(FC)]
        ar.reset(big0)
        st["CQ"] = ar.alloc([128, QC, 512], F32, "CQ")
        st["CKV"] = ar.alloc([128, KVC, 512], F32, "CKV")
        st["CQN"] = ar.alloc([128, QC, 512], BF16, "CQN")
        st["CKVN"] = ar.alloc([128, KVC, 512], BF16, "CKVN")
        st["CQr"] = [Res() for _ in range(QC)]
        st["CKVr"] = [Res() for _ in range(KVC)]
        st["CQNr"] = [Res() for _ in range(QC)]
        st["CKVNr"] = [Res() for _ in range(KVC)]
        ar.reset(big0)
        st["OD"] = ar.alloc([128, 2 * HD, 512], BF16, "OD")
        st["OM"] = ar.alloc([128, HM, 512], BF16, "OM")
        st["MG"] = ar.alloc([128, KC, 512], BF16, "MG")
        st["MGr"] = [Res() for _ in range(KC)]
        st["ODr"] = Res()
        ar.reset(big0 + (bigsz + 63) // 64 * 64)
        st["XIN"] = Ring(g, ar, "xin", [128, 512], F32, 3, sem=True)
        st["SQ"] = Ring(g, ar, "sq", [128, 512], F32, 2)
        st["ACC"] = [ar.alloc([128, 512], F32, "acc0"), ar.alloc([128, 512], F32, "acc1")]
        st["ACCr"] = [Res(), Res()]
        st["TMP"] = Ring(g, ar, "tmp", [128, 512], F32, 5)
        st["H1"] = Ring(g, ar, "h1", [128, 512], F32, 3, sem=True)
        st["OUTB"] = Ring(g, ar, "outb", [128, 512], BF16, 3, sem=True)
        st["R1"] = ar.alloc([128, 512], F32, "R1"); st["R1r"] = Res()
        st["R2"] = ar.alloc([128, 512], F32, "R2"); st["R2r"] = Res()
        st["RQ"] = ar.alloc([128, 512], F32, "RQ"); st["RQr"] = Res()
        st["RKV"] = ar.alloc([128, 512], F32, "RKV"); st["RKVr"] = Res()
        st["R2T"] = ar.alloc([128, 4], F32, "R2T"); st["R2Tr"] = Res()
        st["RF"] = st["RQ"]; st["RFr"] = st["RQr"]
        st["c5"] = []
        st["ROPE"] = ar.alloc([64, 2, 512], F32, "ROPE"); st["ROPEr"] = Res(); st["ROPEs"] = g.dsem("ropes")
        st["bulks"] = g.dsem("bulks")
        st["PS"] = PRing([2, 3, 4, 5, 6, 7])
        return st

    def norm_stream(st, srcf, N, gcol, ssb, ssr):
        XG, XGr, XIN, SQ = st["XG"], st["XGr"], st["XIN"], st["SQ"]
        for cch in range(KC):
            xin, xr, xs = XIN.next()
            g.dma("sp", xs, srcf(cch, xin), writes=(xr,))
            sq, sqr, _ = SQ.next()
            g.op("act", actf(sq[:, :N], xin[:, :N], AF.Square), reads=(xr,), writes=(sqr,))
            g.op("pe", mm(ssb[:, :N], ones32[:], sq[:, :N], cch == 0, cch == KC - 1), reads=(sqr,), writes=(ssr,))
            g.op("dve", ts(XG[:, cch, :N], xin[:, :N], vec[:, gcol + cch:gcol + cch + 1], ALU.mult),
                 reads=(xr,), writes=(XGr[cch],))

    def rstd(st, ssb, ssr, N, dim, out, outr, P=128):
        tmp, tr, _ = st["TMP"].next()
        g.op("act", actf(tmp[:P, :N], ssb[:P, :N], AF.Ln, bias=epsc[:P, 0:1], scale=1.0 / dim), reads=(ssr,), writes=(tr,))
        g.op("act", actf(out[:P, :N], tmp[:P, :N], AF.Exp, scale=-0.5), reads=(tr,), writes=(outr,))

    def ffn(st, nm, N, resid, post):
        XG, XGr, HT, HTr, PS, TMP, H1, XIN = st["XG"], st["XGr"], st["HT"], st["HTr"], st["PS"], st["TMP"], st["H1"], st["XIN"]
        R1, R1r = st["R1"], st["R1r"]
        wg_, wu_, wd_ = W[nm + "g"], W[nm + "u"], W[nm + "d"]
        for f in range(FC):
            wg, wgr = wload(wg_[f], KC * 128)
            wu, wur = wload(wu_[f], KC * 128)
            pg, pgr = PS.next()
            pu, pur = PS.next()
            g.op("pe", [mm(pg[:, :N], wg[:, k * 128:(k + 1) * 128], XG[:, k, :N], k == 0, k == KC - 1) for k in range(KC)],
                 reads=(wgr, *XGr), writes=(pgr,))
            g.op("pe", [mm(pu[:, :N], wu[:, k * 128:(k + 1) * 128], XG[:, k, :N], k == 0, k == KC - 1) for k in range(KC)],
                 reads=(wur, *XGr), writes=(pur,))
            t1, t1r, _ = TMP.next()
            g.op("dve", tt(t1[:, :N], pg[:, :N], R1[:, :N], ALU.mult), reads=(pgr, R1r), writes=(t1r,))
            t2, t2r, _ = TMP.next()
            g.op("act", actf(t2[:, :N], t1[:, :N], AF.Silu), reads=(t1r,), writes=(t2r,))
            t3, t3r, _ = TMP.next()
            g.op("dve", tt(t3[:, :N], pu[:, :N], R1[:, :N], ALU.mult), reads=(pur, R1r), writes=(t3r,))
            g.op("dve", tt(HT[:, f, :N], t3[:, :N], t2[:, :N], ALU.mult), reads=(t3r, t2r), writes=(HTr[f],))
        nt = SLOT // 128
        for j in range(KC):
            xin, xr, xs = XIN.next()
            resid(j, xin, xr, xs)
            pd, pdr = PS.next()
            f0 = 0
            while f0 < FC:
                f1 = min(FC, f0 + nt)
                w, wr = wload(wd_[j][:, f0 * 128:f1 * 128], (f1 - f0) * 128)
                g.op("pe", [mm(pd[:, :N], w[:, (f - f0) * 128:(f - f0 + 1) * 128], HT[:, f, :N], f == 0, f == FC - 1)
                            for f in range(f0, f1)], reads=(wr, *HTr[f0:f1]), writes=(pdr,))
                f0 = f1
            h1, h1r, h1sem = H1.next()
            g.op("dve", stt(h1[:, :N], pd[:, :N], 0.5, xin[:, :N], ALU.mult, ALU.add), reads=(pdr, xr), writes=(h1r,))
            post(j, h1, h1r, h1sem)

    def sq_acc(st, src, srcr, N, ssb, ssr, first, last, which=0):
        acc, accr = st["ACC"][which], st["ACCr"][which]
        if first:
            g.op("act", actf(acc[:, :N], src, AF.Square), reads=(srcr,), writes=(accr,))
        else:
            sq, sqr, _ = st["SQ"].next()
            g.op("act", actf(sq[:, :N], src, AF.Square), reads=(srcr,), writes=(sqr,))
            g.op("dve", tt(acc[:, :N], acc[:, :N], sq[:, :N], ALU.add), reads=(sqr, accr), writes=(accr,))
        if last:
            g.op("pe", mm(ssb[:, :N], ones32[:], acc[:, :N], True, True), reads=(accr,), writes=(ssr,))

    def phaseA(st, segs, own):
        XG, XGr, PS, TMP, OUTB = st["XG"], st["XGr"], st["PS"], st["TMP"], st["OUTB"]
        R2, R2r = st["R2"], st["R2r"]
        cols = []
        c0_ = 0
        for (kind, s0, n, key0) in segs:
            cols.append((c0_, kind, s0, n, key0))
            c0_ += n
        N = c0_
        tok0 = segs[0][1]
        ss0, ss0r = ps_all[:, 0], PSr[0]
        ss1, ss1r = ps_all[:, 1], PSr[1]

        def srcf(cch, dst):
            return [dmaf(dst[:, a:a + n], (metaT[cch] if kind == "meta" else xT[cch][:, s0:s0 + n])) for (a, kind, s0, n, key0) in cols]

        def fm_fns(dram2d, ob, P=128):
            return [dmaf(dram2d[:, key0:key0 + n], ob[:P, a:a + n]) for (a, kind, s0, n, key0) in cols]

        def tm_fns(dram, ob, b, nt_, d0, d1, w):
            fns = []
            lo_b, hi_b = b * 128, b * 128 + nt_
            for (a, kind, s0, n, key0) in cols:
                lo, hi = max(lo_b, a), min(hi_b, a + n)
                if lo < hi:
                    fns.append(dmaf(dram[key0 + lo - a:key0 + hi - a, d0:d1], ob[lo - lo_b:hi - lo_b, 0:w]))
            return fns

        ROPE, ROPEr = st["ROPE"], st["ROPEr"]
        rf = []
        for (a, kind, s0, n, key0) in cols:
            rf.append(dmaf(ROPE[:, 0, a:a + n], ropeC[:, key0:key0 + n]))
            rf.append(dmaf(ROPE[:, 1, a:a + n], ropeS[:, key0:key0 + n]))
        g.dma("sp", st["ROPEs"], rf, writes=(ROPEr,))
        norm_stream(st, srcf, N, V_G1, ss0, ss0r)
        rstd(st, ss0, ss0r, N, D, st["R1"], st["R1r"])

        def resid(j, xin, xr, xs):
            g.dma("sp", xs, srcf(j, xin), writes=(xr,))

        def post(j, h1, h1r, h1sem):
            if own:
                g.dma("sp", h1sem, dmaf(h1s[j][:, tok0:tok0 + N], h1[:, :N]), reads=(h1r,))
            sq_acc(st, h1[:, :N], h1r, N, ss1, ss1r, j == 0, j == KC - 1, which=1)
            g.op("dve", ts(XG[:, j, :N], h1[:, :N], vec[:, V_GM + j:V_GM + j + 1], ALU.mult), reads=(h1r,), writes=(XGr[j],))

        ffn(st, "f1", N, resid, post)
        rstd(st, ss1, ss1r, N, D, R2, R2r)
        if own:
            g.dma("sp", st["bulks"], [dmaf(hgs[:, :, tok0:tok0 + N].rearrange("c p t -> p c t"), XG[:, :, :N]),
                                      dmaf(r2s[:, tok0:tok0 + N], R2[:, :N])], reads=(R2r, *XGr))
        R2T, R2Tr = st["R2T"], st["R2Tr"]
        nb = (N + 127) // 128
        for b in range(nb):
            nt_ = min(128, N - b * 128)
            pt, ptr = PS.next()
            g.op("pe", lambda e, pt=pt, b=b, nt_=nt_: e.transpose(out=pt[:nt_, 0:128], in_=R2[:, b * 128:b * 128 + nt_], identity=id32[:]),
                 reads=(R2r,), writes=(ptr,))
            g.op("dve", cpy(R2T[:nt_, b:b + 1], pt[:nt_, 0:1]), reads=(ptr,), writes=(R2Tr,))

        def proj_fm(w, wr, off, kc, rhs, rhsr, M=128, col0=0):
            p, pr = PS.next()
            g.op("pe", [mm(p[:M, :N], w[:, off + k * 128 + col0:off + k * 128 + col0 + M], rhs[:, k, :N], k == 0, k == kc - 1)
                        for k in range(kc)], reads=(wr, *rhsr), writes=(pr,))
            return p, pr

        def store_bf(p, pr, dst, mul=None, mulr=None, P=128, eng="dve"):
            ob, obr, obs = OUTB.next()
            if mul is not None:
                g.op("dve", tt(ob[:P, :N], p[:P, :N], mul[:P, :N], ALU.mult), reads=(pr, mulr), writes=(obr,))
            elif eng == "act":
                g.op("act", actf(ob[:P, :N], p[:P, :N], AF.Copy), reads=(pr,), writes=(obr,))
            else:
                g.op("dve", cpy(ob[:P, :N], p[:P, :N]), reads=(pr,), writes=(obr,))
            g.dma("sp", obs, dst(ob) if callable(dst) else dmaf(dst, ob[:P, :N]), reads=(obr,))

        if own:
            for hc in range(2 * HD):
                w, wr = wload(W["wi_dq"][hc], KC * 128)
                p, pr = proj_fm(w, wr, 0, KC, XG, XGr)
                store_bf(p, pr, qdT[hc][:, tok0:tok0 + N], R2, R2r)
        for hc in range(2 * HD):
            w, wr = wload(W["wi_dk"][hc], KC * 128)
            p, pr = proj_fm(w, wr, 0, KC, XG, XGr)
            store_bf(p, pr, (lambda ob, hc=hc: fm_fns(kdT[hc], ob)), R2, R2r)
        CKV, CKVr, CKVN, CKVNr = st["CKV"], st["CKVr"], st["CKVN"], st["CKVNr"]
        for k in range(KVC):
            w, wr = wload(W["wi_ckv"][k], KC * 128)
            p, pr = proj_fm(w, wr, 0, KC, XG, XGr)
            g.op("dve", tt(CKV[:, k, :N], p[:, :N], R2[:, :N], ALU.mult), reads=(pr, R2r), writes=(CKVr[k],))
            sq_acc(st, CKV[:, k, :N], CKVr[k], N, ss0, ss0r, k == 0, k == KVC - 1, which=0)
        w, wr = wload(W["wi_kr"][0], KC * 128)
        pa, par = proj_fm(w, wr, 0, KC, XG, XGr, M=64, col0=0)
        pb, pbr = proj_fm(w, wr, 0, KC, XG, XGr, M=64, col0=64)
        ta, tar, _ = TMP.next()
        g.op("dve", tt(ta[:64, :N], pa[:64, :N], R2[:64, :N], ALU.mult), reads=(par, R2r), writes=(tar,))
        tb, tbr, _ = TMP.next()
        g.op("dve", tt(tb[:64, :N], pb[:64, :N], R2[:64, :N], ALU.mult), reads=(pbr, R2r), writes=(tbr,))
        tc_, tcr, _ = TMP.next()
        g.op("dve", tt(tc_[:64, :N], ta[:64, :N], ROPE[:, 0, :N], ALU.mult), reads=(tar, ROPEr), writes=(tcr,))
        td, tdr, _ = TMP.next()
        g.op("dve", tt(td[:64, :N], tb[:64, :N], ROPE[:, 1, :N], ALU.mult), reads=(tbr, ROPEr), writes=(tdr,))
        ob, obr, obs = OUTB.next()
        g.op("dve", tt(ob[:64, :N], tc_[:64, :N], td[:64, :N], ALU.add), reads=(tcr, tdr), writes=(obr,))
        g.dma("sp", obs, fm_fns(kpT, ob, P=64), reads=(obr,))
        for gi in range(NVG):
            banks = [PS.next() for _ in range(nb)]
            for cg in range(KC // CG):
                w, wr = wload(W["wi_dv"][gi * (KC // CG) + cg], CG * 512)
                fns = []
                for cc in range(CG):
                    k = cg * CG + cc
                    for b in range(nb):
                        nt_ = min(128, N - b * 128)
                        fns.append(mm(banks[b][0][:nt_, :512], XG[:, k, b * 128:b * 128 + nt_], w[:, cc * 512:(cc + 1) * 512],
                                      k == 0, k == KC - 1))
                g.op("pe", fns, reads=(wr, *XGr), writes=tuple(bk[1] for bk in banks))
            for b in range(nb):
                nt_ = min(128, N - b * 128)
                ob, obr, obs = OUTB.next()
                g.op("act", actf(ob[:nt_, :512], banks[b][0][:nt_, :512], AF.Copy, scale=R2T[:nt_, b:b + 1]),
                     reads=(banks[b][1], R2Tr), writes=(obr,))
                g.dma("sp", obs, tm_fns(vds, ob, b, nt_, gi * 512, (gi + 1) * 512, 512), reads=(obr,))
        rstd(st, ss0, ss0r, N, c["KVL"], st["RKV"], st["RKVr"])
        for k in range(KVC):
            g.op("dve", stt(CKVN[:, k, :N], CKV[:, k, :N], vec[:, V_GKV + k:V_GKV + k + 1], st["RKV"][:, :N], ALU.mult, ALU.mult),
                 reads=(CKVr[k], st["RKVr"]), writes=(CKVNr[k],))
        for hg_ in range(HM // HGK):
            w, wr = wload(W["ukv_n"][hg_], HGK * KVC * 128)
            for hh in range(HGK):
                h = hg_ * HGK + hh
                p, pr = proj_fm(w, wr, hh * KVC * 128, KVC, CKVN, CKVNr)
                store_bf(p, pr, (lambda ob, h=h: fm_fns(knT[h], ob)), eng=("act" if hh % 2 else "dve"))
        for gi in range(HM // MVG):
            banks = [PS.next() for _ in range(nb)]
            w, wr = wload(W["ukv_v"][gi], KVC * MVW)
            fns = []
            for k in range(KVC):
                for b in range(nb):
                    nt_ = min(128, N - b * 128)
                    fns.append(mm(banks[b][0][:nt_, :MVW], CKVN[:, k, b * 128:b * 128 + nt_], w[:, k * MVW:(k + 1) * MVW],
                                  k == 0, k == KVC - 1))
            g.op("pe", fns, reads=(wr, *CKVNr), writes=tuple(bk[1] for bk in banks))
            for b in range(nb):
                nt_ = min(128, N - b * 128)
                ob, obr, obs = OUTB.next()
                g.op("act" if b % 2 else "dve",
                     (actf(ob[:nt_, :MVW], banks[b][0][:nt_, :MVW], AF.Copy) if b % 2 else cpy(ob[:nt_, :MVW], banks[b][0][:nt_, :MVW])),
                     reads=(banks[b][1],), writes=(obr,))
                g.dma("sp", obs, tm_fns(mvs, ob, b, nt_, gi * MVW, (gi + 1) * MVW, MVW), reads=(obr,))
        if own:
            CQ, CQr, CQN, CQNr = st["CQ"], st["CQr"], st["CQN"], st["CQNr"]
            for k in range(QC):
                w, wr = wload(W["wi_cq"][k], KC * 128)
                p, pr = proj_fm(w, wr, 0, KC, XG, XGr)
                g.op("dve", tt(CQ[:, k, :N], p[:, :N], R2[:, :N], ALU.mult), reads=(pr, R2r), writes=(CQr[k],))
                sq_acc(st, CQ[:, k, :N], CQr[k], N, ss1, ss1r, k == 0, k == QC - 1, which=1)
            rstd(st, ss1, ss1r, N, c["QL"], st["RQ"], st["RQr"])
            for k in range(QC):
                g.op("dve", stt(CQN[:, k, :N], CQ[:, k, :N], vec[:, V_GQ + k:V_GQ + k + 1], st["RQ"][:, :N], ALU.mult, ALU.mult),
                     reads=(CQr[k], st["RQr"]), writes=(CQNr[k],))
            for hg_ in range(HM // HGQ):
                w, wr = wload(W["uq_n"][hg_], HGQ * QC * 128)
                for hh in range(HGQ):
                    h = hg_ * HGQ + hh
                    p, pr = proj_fm(w, wr, hh * QC * 128, QC, CQN, CQNr)
                    store_bf(p, pr, qnT[h][:, tok0:tok0 + N], eng=("act" if hh % 2 else "dve"))
            for hg_ in range(HM // HGQ):
                w, wr = wload(W["uq_p"][hg_], HGQ * QC * 128)
                for hh in range(HGQ):
                    h = hg_ * HGQ + hh
                    pa, par = proj_fm(w, wr, hh * QC * 128, QC, CQN, CQNr, M=64, col0=0)
                    pb, pbr = proj_fm(w, wr, hh * QC * 128, QC, CQN, CQNr, M=64, col0=64)
                    tc_, tcr, _ = TMP.next()
                    g.op("dve", tt(tc_[:64, :N], pa[:64, :N], ROPE[:, 0, :N], ALU.mult), reads=(par, ROPEr), writes=(tcr,))
                    td, tdr, _ = TMP.next()
                    g.op("dve", tt(td[:64, :N], pb[:64, :N], ROPE[:, 1, :N], ALU.mult), reads=(pbr, ROPEr), writes=(tdr,))
                    ob, obr, obs = OUTB.next()
                    g.op("dve", tt(ob[:64, :N], tc_[:64, :N], td[:64, :N], ALU.add), reads=(tcr, tdr), writes=(obr,))
                    g.dma("sp", obs, dmaf(qpT[h][:, tok0:tok0 + N], ob[:64, :N]), reads=(obr,))
        g.fence(("pe", "act", "dve"))

    def phaseB():
        m0 = ar.mark()
        MOWN = ar.alloc([128, c["WOWN"]], F32, "MOWN")
        MOTH = ar.alloc([128, c["WOTH"]], F32, "MOTH")
        KPE = ar.alloc([128, TKP], BF16, "KPE")
        tabr = Res()
        tabs = g.dsem("tabs")
        HB = []
        for i in range(2):
            hb = dict(KT=ar.alloc([128, 2, TKP], BF16, f"KT{i}"), V=ar.alloc([128, NKT, 256], BF16, f"V{i}"),
                      QT=ar.alloc([128, 2, TOWN], BF16, f"QT{i}"), QP=ar.alloc([128, TOWN], BF16, f"QP{i}"),
                      res=Res(), sem=g.dsem(f"hb{i}"))
            HB.append(hb)
        LOOK = 4
        E = Ring(g, ar, "E", [128, 512], BF16, 4)
        T = Ring(g, ar, "T", [128, 512], F32, 3)
        TMP = Ring(g, ar, "tmpB", [128, 512], F32, 6)
        SQ = Ring(g, ar, "sqB", [128, 512], F32, 2)
        OUTB = Ring(g, ar, "outbB", [128, 512], BF16, 3, sem=True)
        OC = ar.alloc([128, 2, 2, 512], F32, "OC")
        OCr = [[Res(), Res()], [Res(), Res()]]
        ODt = ar.alloc([128, 2, 512], F32, "ODt")
        ODr = Res()
        SB = PRing([0, 1, 2, 3])

        g.op("dve", mset(KPE[:], 0.0), writes=(tabr,))
        for hb in HB:
            g.op("dve", [mset(hb["KT"][:], 0.0), mset(hb["QP"][:], 0.0)], writes=(hb["res"],))
            g.op("pool", mset(hb["V"][:, NKT - 1, :], 0.0), writes=(hb["res"],))
        g.dma("sp", tabs, [dmaf(MOWN[:], mown), dmaf(MOTH[:], moth), dmaf(KPE[0:64, 0:TK], kpT)], writes=(tabr,))

        NOT = TOWN // 128
        NJ = TOWN // 512

        def load_diff(h, hb):
            fns = []
            for cc in range(2):
                fns.append(dmaf(hb["KT"][:, cc, 0:TK], kdT[2 * h + cc]))
                fns.append(dmaf(hb["QT"][:, cc, :], qdT[2 * h + cc]))
            t0 = 0
            while t0 < NKT - 1:
                t1 = min(NKT - 1, t0 + 8)
                fns.append(dmaf(hb["V"][:, t0:t1, :],
                                vds[t0 * 128:t1 * 128, h * 256:(h + 1) * 256].rearrange("(t p) e -> p t e", p=128)))
                t0 = t1
            fns.append(dmaf(hb["V"][0:16, NKT - 1, :], vds[S:S + 16, h * 256:(h + 1) * 256]))
            g.dma("sp", hb["sem"], fns, writes=(hb["res"],))

        def load_mla(h, hb):
            fns = [dmaf(hb["KT"][:, 0, 0:TK], knT[h]), dmaf(hb["QT"][:, 0, :], qnT[h]), dmaf(hb["QP"][0:64, :], qpT[h])]
            t0 = 0
            while t0 < NKT - 1:
                t1 = min(NKT - 1, t0 + 8)
                fns.append(dmaf(hb["V"][:, t0:t1, 0:128],
                                mvs[t0 * 128:t1 * 128, h * 128:(h + 1) * 128].rearrange("(t p) e -> p t e", p=128)))
                t0 = t1
            fns.append(dmaf(hb["V"][0:16, NKT - 1, 0:128], mvs[S:S + 16, h * 128:(h + 1) * 128]))
            g.dma("sp", hb["sem"], fns, writes=(hb["res"],))

        slopes = alibi_slopes(HD)
        dscale = 128 ** -0.5
        mscale = 192 ** -0.5
        total_heads = HD + HM
        items = []
        deferred = []

        def recip_act(src_ap, src_res, scale_in=None, bias_in=None, power=-1.0):
            lz, lzr, _ = TMP.next()
            g.op("act", actf(lz[:], src_ap, AF.Ln, bias=bias_in, scale=scale_in), reads=(src_res,), writes=(lzr,))
            rz, rzr, _ = TMP.next()
            g.op("act", actf(rz[:], lz[:], AF.Exp, scale=power), reads=(lzr,), writes=(rzr,))
            return rz, rzr

        def mk_diff_epi(h, j, cc, Ob, Zb):
            def epi(pidx):
                g.op("dve", cpy(OC[:, cc, 0], ps_all[:, Ob[0]]), reads=(PSr[Ob[0]],), writes=(OCr[cc][0],))
                g.op("act", actf(OC[:, cc, 1], ps_all[:, Ob[1]], AF.Copy), reads=(PSr[Ob[1]],), writes=(OCr[cc][1],))
                zc, zcr, _ = TMP.next()
                g.op("dve", cpy(zc[:], ps_all[:, Zb]), reads=(PSr[Zb],), writes=(zcr,))

                def part1():
                    rz, rzr = recip_act(zc[:], zcr)
                    for x in range(2):
                        g.op("dve", tt(OC[:, cc, x], OC[:, cc, x], rz[:], ALU.mult), reads=(rzr, OCr[cc][x]), writes=(OCr[cc][x],))
                    if cc == 0:
                        return
                    g.op("dve", stt(ODt[:].rearrange("p a b -> p (a b)"), OC[:, 1].rearrange("p a b -> p (a b)"), nlam[:, 0:1],
                                    OC[:, 0].rearrange("p a b -> p (a b)"), ALU.mult, ALU.add),
                         reads=(OCr[0][0], OCr[0][1], OCr[1][0], OCr[1][1]), writes=(ODr,))
                    sqs = []
                    for x in range(2):
                        sq, sqr, _ = SQ.next()
                        g.op("act", actf(sq[:], ODt[:, x], AF.Square), reads=(ODr,), writes=(sqr,))
                        sqs.append((sq, sqr))

                    def part2():
                        ssb, ssr = SB.next()
                        for x in range(2):
                            g.op("pe", mm(ssb[:], ones32[:], sqs[x][0][:], x == 0, x == 1), reads=(sqs[x][1],), writes=(ssr,))
                        rd, rdr = recip_act(ssb[:], ssr, scale_in=1.0 / 256, bias_in=epsc[:, 0:1], power=-0.5)
                        for x in range(2):
                            ob, obr, obs = OUTB.next()
                            g.op("dve", stt(ob[:], ODt[:, x], sublnS[:, x:x + 1], rd[:], ALU.mult, ALU.mult), reads=(ODr, rdr), writes=(obr,))
                            g.dma("sp", obs, dmaf(odT[2 * h + x][:, j * 512:(j + 1) * 512], ob[:]), reads=(obr,))
                    deferred.append([pidx + 5, part2])
                deferred.append([pidx + 2, part1])
            return epi

        def mk_mla_epi(h, j, Ob, Zb):
            def epi(pidx):
                raw, rawr, _ = TMP.next()
                g.op("dve", cpy(raw[:], ps_all[:, Ob[0]]), reads=(PSr[Ob[0]],), writes=(rawr,))
                zc, zcr, _ = TMP.next()
                g.op("dve", cpy(zc[:], ps_all[:, Zb]), reads=(PSr[Zb],), writes=(zcr,))

                def part1():
                    rz, rzr = recip_act(zc[:], zcr)
                    ob, obr, obs = OUTB.next()
                    g.op("dve", tt(ob[:], raw[:], rz[:], ALU.mult), reads=(rawr, rzr), writes=(obr,))
                    g.dma("sp", obs, dmaf(omT[h][:, j * 512:(j + 1) * 512], ob[:]), reads=(obr,))
                deferred.append([pidx + 2, part1])
            return epi

        mla_blk = 0
        for hh in range(total_heads):
            hb = HB[hh % 2]
            first_of_head = True
            if hh < HD:
                h = hh
                ch = -slopes[h] / dscale
                for j in range(NJ):
                    for cc in range(2):
                        Ob, Zb = [4, 5], 6
                        for i in range(NKT):
                            if i == NKT - 1:
                                bias = None
                            elif i < NOT:
                                s0 = 512 * j - 128 * i + (TOWN - 128)
                                bias = MOWN[:, s0:s0 + 512]
                            else:
                                s0 = 512 * j - 128 * i + (S - 128)
                                bias = MOTH[:, s0:s0 + 512]
                            it = dict(hb=hb, hh=hh, i=i, bias=bias, ch=ch, scale=dscale, ne=2, Ob=Ob, Zb=Zb,
                                      kq=[(hb["KT"][:, cc, i * 128:(i + 1) * 128], hb["QT"][:, cc, j * 512:(j + 1) * 512])],
                                      epi=(mk_diff_epi(h, j, cc, Ob, Zb) if i == NKT - 1 else None), pre=first_of_head)
                            first_of_head = False
                            items.append(it)
            else:
                h = hh - HD
                for j in range(NJ):
                    Ob, Zb = ([4], 5) if mla_blk % 2 == 0 else ([6], 7)
                    mla_blk += 1
                    for i in range(NKT):
                        it = dict(hb=hb, hh=hh, i=i, bias=None, ch=0.0, scale=mscale, ne=1, Ob=Ob, Zb=Zb,
                                  kq=[(hb["KT"][:, 0, i * 128:(i + 1) * 128], hb["QT"][:, 0, j * 512:(j + 1) * 512]),
                                      (KPE[:, i * 128:(i + 1) * 128], hb["QP"][:, j * 512:(j + 1) * 512])],
                                  epi=(mk_mla_epi(h, j, Ob, Zb) if i == NKT - 1 else None), pre=first_of_head)
                        first_of_head = False
                        items.append(it)

        def issue_S(it):
            p, pr = SB.next()
            nk = len(it["kq"])
            g.op("pe", [mm(p[:], k_, q_, x == 0, x == nk - 1) for x, (k_, q_) in enumerate(it["kq"])],
                 reads=(it["hb"]["res"], tabr), writes=(pr,))
            it["S"] = (p, pr)

        def prefetch(hh):
            if hh >= total_heads:
                return
            if hh < HD:
                load_diff(hh, HB[hh % 2])
            else:
                load_mla(hh - HD, HB[hh % 2])

        prefetch(0)
        n_items = len(items)
        AHEAD = 2

        def stage1(it):
            p, pr = it["S"]
            if it["bias"] is not None:
                t, tr, _ = T.next()
                g.op("dve", stt(t[:], it["bias"], it["ch"], p[:], ALU.mult, ALU.add), reads=(pr, tabr), writes=(tr,))
                srcp, srcr = t, tr
            else:
                srcp, srcr = p, pr
            e_, er, _ = E.next()
            g.op("act", actf(e_[:], srcp[:], AF.Exp, scale=it["scale"]), reads=(srcr,), writes=(er,))
            it["E"] = (e_, er)

        def run_deferred(pidx):
            k = 0
            while k < len(deferred):
                if deferred[k][0] <= pidx:
                    deferred.pop(k)[1]()
                    k = 0
                else:
                    k += 1

        for pidx in range(min(LOOK, n_items)):
            issue_S(items[pidx])
        for pidx in range(min(AHEAD, n_items)):
            stage1(items[pidx])
        for pidx in range(n_items):
            it = items[pidx]
            if it["pre"]:
                prefetch(it["hh"] + 1)
            hb = it["hb"]
            i = it["i"]
            e_, er = it["E"]
            first, last = (i == 0), (i == NKT - 1)
            fns = [mm(ps_all[:, it["Ob"][x]], hb["V"][:, i, x * 128:(x + 1) * 128], e_[:], first, last) for x in range(it["ne"])]
            fns.append(mm(ps_all[:, it["Zb"]], (onesmeta if last else onesbf)[:], e_[:], first, last))
            g.op("pe", fns, reads=(er, hb["res"]), writes=tuple(PSr[k] for k in it["Ob"][:it["ne"]] + [it["Zb"]]))
            if pidx + LOOK < n_items:
                issue_S(items[pidx + LOOK])
            if it["epi"] is not None:
                it["epi"](pidx)
            if pidx + AHEAD < n_items:
                stage1(items[pidx + AHEAD])
            run_deferred(pidx)
        run_deferred(10 ** 9)
        ar.reset(m0)

    def phaseC(st, tok0):
        N = 512
        XG, XGr, PS, TMP, H1, XIN = st["XG"], st["XGr"], st["PS"], st["TMP"], st["H1"], st["XIN"]
        OD, OM, MG, MGr, ODr = st["OD"], st["OM"], st["MG"], st["MGr"], st["ODr"]
        R2, R2r = st["R2"], st["R2r"]
        ss0, ss0r = ps_all[:, 0], PSr[0]
        ss1, ss1r = ps_all[:, 1], PSr[1]
        hres = [Res() for _ in range(KC)]
        c5_prev = st["c5"]
        st["c5"] = []
        g.dma("sp", st["bulks"], [dmaf(OD[:], odT[:, :, tok0:tok0 + N].rearrange("c p t -> p c t")),
                                  dmaf(OM[:], omT[:, :, tok0:tok0 + N].rearrange("c p t -> p c t")),
                                  dmaf(XG[:], hgs[:, :, tok0:tok0 + N].rearrange("c p t -> p c t")),
                                  dmaf(R2[:], r2s[:, tok0:tok0 + N])], writes=(ODr, R2r, *XGr))
        for j in range(KC):
            wa, war = wload(W["wgt"][j], KC * 128)
            wb, wbr_ = wload(W["wgt"][KC + j], KC * 128)
            wc, wcr = wload(W["wbr"][j], NE * 128)
            pgd, pgdr = PS.next()
            g.op("pe", [mm(pgd[:], wa[:, k * 128:(k + 1) * 128], XG[:, k], k == 0, k == KC - 1) for k in range(KC)],
                 reads=(war, *XGr), writes=(pgdr,))
            pgm, pgmr = PS.next()
            g.op("pe", [mm(pgm[:], wb[:, k * 128:(k + 1) * 128], XG[:, k], k == 0, k == KC - 1) for k in range(KC)],
                 reads=(wbr_, *XGr), writes=(pgmr,))
            pbd, pbdr = PS.next()
            g.op("pe", [mm(pbd[:], wc[:, k * 128:(k + 1) * 128], OD[:, k], k == 0, k == 2 * HD - 1) for k in range(2 * HD)],
                 reads=(wcr, ODr), writes=(pbdr,))
            pbm, pbmr = PS.next()
            g.op("pe", [mm(pbm[:], wc[:, (2 * HD + k) * 128:(2 * HD + k + 1) * 128], OM[:, k], k == 0, k == HM - 1) for k in range(HM)],
                 reads=(wcr, ODr), writes=(pbmr,))
            ms = []
            for (pgx, pgxr, pbx, pbxr, bcol) in ((pgd, pgdr, pbd, pbdr, V_BG + j), (pgm, pgmr, pbm, pbmr, V_BG + KC + j)):
                t1, t1r, _ = TMP.next()
                g.op("dve", tt(t1[:], pgx[:], R2[:], ALU.mult), reads=(pgxr, R2r), writes=(t1r,))
                t2, t2r, _ = TMP.next()
                g.op("act", actf(t2[:], t1[:], AF.Sigmoid, bias=vec[:, bcol:bcol + 1]), reads=(t1r,), writes=(t2r,))
                t3, t3r, _ = TMP.next()
                g.op("dve", tt(t3[:], pbx[:], t2[:], ALU.mult), reads=(pbxr, t2r), writes=(t3r,))
                ms.append((t3, t3r))
            g.op("dve", tt(MG[:, j], ms[0][0][:], ms[1][0][:], ALU.add), reads=(ms[0][1], ms[1][1]), writes=(MGr[j],))
            if c5_prev:
                c5_prev.pop(0)()
        while c5_prev:
            c5_prev.pop(0)()
        for j in range(KC):
            w, wr = wload(W["wo"][j], KC * 128)
            xin, xr, xs = XIN.next()
            g.dma("sp", xs, dmaf(xin[:], h1s[j][:, tok0:tok0 + N]), reads=(hres[j],), writes=(xr,))
            p, pr = PS.next()
            g.op("pe", [mm(p[:], w[:, k * 128:(k + 1) * 128], MG[:, k], k == 0, k == KC - 1) for k in range(KC)],
                 reads=(wr, *MGr), writes=(pr,))
            h2, h2r, h2s = H1.next()
            g.op("dve", tt(h2[:], p[:], xin[:], ALU.add), reads=(pr, xr), writes=(h2r,))
            g.dma("sp", h2s, dmaf(h1s[j][:, tok0:tok0 + N], h2[:]), reads=(h2r,), writes=(hres[j],))
            sq_acc(st, h2[:], h2r, N, ss0, ss0r, j == 0, j == KC - 1, which=0)
            g.op("dve", ts(XG[:, j], h2[:], vec[:, V_G2 + j:V_G2 + j + 1], ALU.mult), reads=(h2r,), writes=(XGr[j],))
        rstd(st, ss0, ss0r, N, D, st["R1"], st["R1r"])
        g.fence(("pe", "act", "dve"))

        def resid(j, xin, xr, xs):
            g.dma("sp", xs, dmaf(xin[:], h1s[j][:, tok0:tok0 + N]), reads=(hres[j],), writes=(xr,))

        def post(j, h3, h3r, h3s):
            g.dma("sp", h3s, dmaf(h1s[j][:, tok0:tok0 + N], h3[:]), reads=(h3r,), writes=(hres[j],))
            sq_acc(st, h3[:], h3r, N, ss1, ss1r, j == 0, j == KC - 1, which=1)

        ffn(st, "f2", N, resid, post)
        rstd(st, ss1, ss1r, N, D, st["RF"], st["RFr"])
        g.fence(("pe", "act", "dve", "sp"))

        def c5_step(j):
            xin, xr, xs = XIN.next()
            g.dma("sp", xs, dmaf(xin[:], h1s[j][:, tok0:tok0 + N]), reads=(hres[j],), writes=(xr,))
            y, yr, ys = H1.next()
            g.op("dve", stt(y[:], xin[:], vec[:, V_GF + j:V_GF + j + 1], st["RF"][:], ALU.mult, ALU.mult), reads=(xr, st["RFr"]), writes=(yr,))
            g.dma("sp", ys, dmaf(yT[j][:, tok0:tok0 + N], y[:]), reads=(yr,))
        for j in range(KC):
            st["c5"].append(lambda j=j: c5_step(j))

    st = alloc_AC()
    for t in range(c["NTO"]):
        phaseA(st, [("x", t * 512, 512, t * 512)], True)
    rest = TOWN + 16
    nto = -(-rest // 512)
    base = -(-(-(-rest // nto)) // 32) * 32
    pos = TOWN
    for k in range(nto):
        n = base if k < nto - 1 else (S - pos)
        seg = [("x", pos, n, pos)]
        if k == nto - 1:
            seg.append(("meta", 0, 16, S))
        assert 0 < sum(x[2] for x in seg) <= 512
        phaseA(st, seg, False)
        pos += n
    g.fence()
    if c.get("STOP") != "A":
        ar.reset(persist_mark)
        phaseB()
        g.fence()
        if c.get("STOP") != "B":
            ar.reset(ac_mark)
            st = alloc_AC()
            for t in range(c["NTO"]):
                phaseC(st, t * 512)
            while st["c5"]:
                st["c5"].pop(0)()
            g.fence()

    with nc.Block() as block:
        g.emit(block)
    return nc


def lay_cols_fm(Wm, kc):
    n = Wm.shape[1] // 128
    return np.ascontiguousarray(Wm.reshape(kc, 128, n, 128).transpose(2, 1, 0, 3).reshape(n, 128, kc * 128))


def vec_cols(v):
    v = np.asarray(v, np.float32).reshape(-1)
    return v.reshape(-1, 128).T


def swap_halves(Wp):
    h = Wp.shape[-1] // 2
    return np.concatenate([Wp[..., h:], Wp[..., :h]], axis=-1)


def prep_shared(cfg, inp):
    c = cfg
    D, F, HD, HM, QL, KVL = c["D"], c["F"], c["HD"], c["HM"], c["QL"], c["KVL"]
    KC, FC, QC, KVC, CG, NVG, MVG, MVW, HGQ, HGK = (c[k] for k in ("KC", "FC", "QC", "KVC", "CG", "NVG", "MVG", "MVW", "HGQ", "HGK"))
    f32 = np.float32
    sh = {}
    for nm, pre in (("f1", "ffn1"), ("f2", "ffn2")):
        sh[nm + "g"] = lay_cols_fm(np.asarray(inp[pre + "_w_gate"][0], f32), KC)
        sh[nm + "u"] = lay_cols_fm(np.asarray(inp[pre + "_w_up"][0], f32), KC)
        sh[nm + "d"] = lay_cols_fm(np.asarray(inp[pre + "_w_down"][0], f32), FC)
    w_in = np.asarray(inp["w_in"][0], f32)
    QKW = HD * 256
    o_dq, o_dk, o_dv = 0, QKW, 2 * QKW
    o_cq = 3 * QKW
    o_ckv = o_cq + QL
    o_kr = o_ckv + KVL
    sh["wi_dq"] = lay_cols_fm(w_in[:, o_dq:o_dq + QKW], KC)
    sh["wi_dk"] = lay_cols_fm(w_in[:, o_dk:o_dk + QKW], KC)
    sh["wi_cq"] = lay_cols_fm(w_in[:, o_cq:o_cq + QL], KC)
    sh["wi_ckv"] = lay_cols_fm(w_in[:, o_ckv:o_ckv + KVL], KC)
    wkr = w_in[:, o_kr:o_kr + 64]
    sh["wi_kr"] = lay_cols_fm(np.concatenate([wkr, swap_halves(wkr)], axis=1), KC)
    wv = w_in[:, o_dv:o_dv + QKW]
    sh["wi_dv"] = np.ascontiguousarray(
        wv.reshape(KC // CG, CG, 128, NVG, 512).transpose(3, 0, 2, 1, 4).reshape(NVG * (KC // CG), 128, CG * 512))
    wuq = np.asarray(inp["mla_w_uq"][0], f32).reshape(QL, HM, 192)
    wn = np.ascontiguousarray(wuq[:, :, :128]).reshape(QL, HM * 128)
    pn = lay_cols_fm(wn, QC)
    sh["uq_n"] = np.ascontiguousarray(pn.reshape(HM // HGQ, HGQ, 128, QC * 128).transpose(0, 2, 1, 3).reshape(HM // HGQ, 128, HGQ * QC * 128))
    wp = wuq[:, :, 128:192]
    wpp = np.concatenate([wp, swap_halves(wp)], axis=2).reshape(QL, HM * 128)
    pp = lay_cols_fm(np.ascontiguousarray(wpp), QC)
    sh["uq_p"] = np.ascontiguousarray(pp.reshape(HM // HGQ, HGQ, 128, QC * 128).transpose(0, 2, 1, 3).reshape(HM // HGQ, 128, HGQ * QC * 128))
    wukv = np.asarray(inp["mla_w_ukv"][0], f32).reshape(KVL, HM, 256)
    wkn = np.ascontiguousarray(wukv[:, :, :128]).reshape(KVL, HM * 128)
    pk = lay_cols_fm(wkn, KVC)
    sh["ukv_n"] = np.ascontiguousarray(pk.reshape(HM // HGK, HGK, 128, KVC * 128).transpose(0, 2, 1, 3).reshape(HM // HGK, 128, HGK * KVC * 128))
    wvv = np.ascontiguousarray(wukv[:, :, 128:]).reshape(KVC, 128, HM // MVG, MVW)
    sh["ukv_v"] = np.ascontiguousarray(wvv.transpose(2, 1, 0, 3).reshape(HM // MVG, 128, KVC * MVW))
    sh["wgt"] = lay_cols_fm(np.asarray(inp["w_gate"][0], f32), KC)
    bd = lay_cols_fm(np.asarray(inp["w_branch_diff"][0], f32), 2 * HD)
    bm = lay_cols_fm(np.asarray(inp["w_branch_mla"][0], f32), HM)
    sh["wbr"] = np.ascontiguousarray(np.concatenate([bd, bm], axis=2))
    sh["wo"] = lay_cols_fm(np.asarray(inp["w_out"][0], f32), KC)
    cols = [vec_cols(inp["ffn1_norm"][0]), vec_cols(inp["mix_norm"][0]), vec_cols(inp["ffn2_norm"][0]), vec_cols(inp["final_norm"]),
            vec_cols(inp["mla_q_norm"][0]), vec_cols(inp["mla_kv_norm"][0]), vec_cols(inp["diff_subln"][0]), vec_cols(inp["b_gate"][0]),
            vec_cols(inp["diff_lambda_q1"][0]), vec_cols(inp["diff_lambda_k1"][0]), vec_cols(inp["diff_lambda_q2"][0]),
            vec_cols(inp["diff_lambda_k2"][0])]
    sh["vecs"] = np.ascontiguousarray(np.concatenate(cols, axis=1).astype(f32))
    assert sh["vecs"].shape[1] == c["NV"]
    sh["ident"] = np.eye(128, dtype=f32)
    return sh


def prep_core(cfg, inp, core):
    c = cfg
    S, TOWN, KC, TK = c["S"], c["TOWN"], c["KC"], c["TK"]
    f32 = np.float32
    b, half = core // 2, core % 2
    x = np.asarray(inp["x"][b], f32)
    order = np.concatenate([np.arange(half * TOWN, (half + 1) * TOWN), np.arange((1 - half) * TOWN, (2 - half) * TOWN)])
    pc = {}
    pc["xT"] = np.ascontiguousarray(x[order].T.reshape(KC, 128, S))
    pc["metaT"] = np.ascontiguousarray(np.asarray(inp["meta_tokens"], f32).T.reshape(KC, 128, 16))
    pos = np.concatenate([16 + order, np.arange(16)]).astype(f32)
    inv_freq = (1.0 / (ROPE_THETA ** (np.arange(0, 64, 2, dtype=f32) / f32(64)))).astype(f32)
    ang = (pos[None, :] * inv_freq[:, None]).astype(f32)
    cs, sn = np.cos(ang).astype(f32), np.sin(ang).astype(f32)
    pc["ropeC"] = np.ascontiguousarray(np.concatenate([cs, cs], axis=0))
    pc["ropeS"] = np.ascontiguousarray(np.concatenate([-sn, sn], axis=0))
    kk = np.arange(128, dtype=np.int64)[:, None]
    u = np.arange(c["WOWN"], dtype=np.int64)[None, :] - (TOWN - 128)
    pc["mown"] = np.abs(u - kk).astype(f32)
    u = np.arange(c["WOTH"], dtype=np.int64)[None, :] - (S - 128)
    pc["moth"] = ((kk - u) if half == 0 else (S + u - kk)).astype(f32)
    return pc


def run_cfg(cfg, inp, trace=False):
    nc = build(cfg)
    sh = prep_shared(cfg, inp)
    in_maps = []
    for core in range(NCORES):
        m = dict(sh)
        m.update(prep_core(cfg, inp, core))
        in_maps.append(m)
    res = run_bass_kernel_spmd(nc, in_maps, core_ids=list(range(NCORES)), trace=trace)
    S, TOWN, D = cfg["S"], cfg["TOWN"], cfg["D"]
    out = np.empty((cfg["B"], S, D), np.float32)
    for core in range(NCORES):
        b, half = core // 2, core % 2
        y = np.asarray(res.results[core]["yT"]).reshape(D, TOWN)
        out[b, half * TOWN:(half + 1) * TOWN, :] = y.T
    return out, res


def kernel(**inputs):
    cfg = make_cfg()
    out, _ = run_cfg(cfg, inputs)
    return out
```

```python
import math
import numpy as np
import concourse.bass as bass
import concourse.mybir as mybir
from concourse.bass_utils import run_bass_kernel_spmd

F32 = mybir.dt.float32
BF16 = mybir.dt.bfloat16
AF = mybir.ActivationFunctionType
ALU = mybir.AluOpType
EPS = 1e-6
LAM_INIT = 0.2
OUT_SCALE = 0.8
ROPE_THETA = 10000.0
NCORES = 8


def make_cfg(D=4096, F=11008, HD=8, HM=16, QL=1024, KVL=512, S=4096, B=4, SLOT=4096, NW=5):
    c = dict(D=D, F=F, HD=HD, HM=HM, QL=QL, KVL=KVL, S=S, B=B, NMETA=16, SLOT=SLOT, NW=NW)
    c["KC"] = D // 128
    c["FC"] = F // 128
    c["QC"] = QL // 128
    c["KVC"] = KVL // 128
    c["TOWN"] = S // 2
    c["NTO"] = c["TOWN"] // 512
    c["NTA"] = S // 512
    c["TK"] = S + 16
    c["NKT"] = S // 128 + 1
    c["TKP"] = c["NKT"] * 128
    c["CG"] = min(8, c["KC"])
    c["NVG"] = (HD * 256) // 512
    c["MVG"] = min(4, HM)
    c["MVW"] = c["MVG"] * 128
    c["HGQ"] = max(1, min(HM, SLOT // (c["QC"] * 128)))
    c["HGK"] = max(1, min(HM, SLOT // (c["KVC"] * 128)))
    c["WOWN"] = 2 * c["TOWN"] - 128
    c["WOTH"] = S - 128
    c["NV"] = 4 * c["KC"] + c["QC"] + c["KVC"] + 2 + 2 * c["KC"] + 4
    assert D % 128 == 0 and F % 128 == 0 and c["TOWN"] % 512 == 0 and HD % 2 == 0
    assert HM % c["HGQ"] == 0 and HM % c["HGK"] == 0 and HM % c["MVG"] == 0 and c["KC"] % c["CG"] == 0
    return c


class Res:
    __slots__ = ("w", "r")

    def __init__(self):
        self.w = None
        self.r = {}


class Gen:
    ENG = ("pe", "act", "dve", "pool", "sp")

    def __init__(self, nc):
        self.nc = nc
        self.q = {e: [] for e in self.ENG}
        self.prog = {e: nc.alloc_semaphore("pg_" + e) for e in ("pe", "act", "dve", "pool")}
        self.cnt = {e: 0 for e in self.prog}
        self.waited = {e: {} for e in self.ENG}
        self.semtot = {}
        self.semobj = {}

    def dsem(self, name):
        s = self.nc.alloc_semaphore(f"{name}_{len(self.semtot)}")
        self.semtot[s.num] = 0
        self.semobj[s.num] = s
        return s

    def _wait(self, e, deps):
        best = {}
        for d in deps:
            if d is None:
                continue
            sem, val = d
            if sem.num not in best or best[sem.num][1] < val:
                best[sem.num] = (sem, val)
        for sem, val in best.values():
            if e == "pe" and sem.num == self.prog["pe"].num:
                continue
            if self.waited[e].get(sem.num, 0) >= val:
                continue
            self.waited[e][sem.num] = val
            self.q[e].append(("w", sem, val))

    @staticmethod
    def _deps(reads, writes):
        deps = []
        for r in reads:
            if r.w is not None:
                deps.append(r.w)
        for w in writes:
            if w.w is not None:
                deps.append(w.w)
            deps.extend(w.r.values())
        return deps

    @staticmethod
    def _commit(tok, reads, writes):
        sem, val = tok
        for r in reads:
            r.r[sem.num] = tok
        for w in writes:
            w.w = tok
            w.r = {}

    def op(self, e, fns, reads=(), writes=()):
        if callable(fns):
            fns = [fns]
        self._wait(e, self._deps(reads, writes))
        self.cnt[e] += 1
        tok = (self.prog[e], self.cnt[e])
        for f in fns[:-1]:
            self.q[e].append(("i", f, None, 0))
        self.q[e].append(("i", fns[-1], self.prog[e], 1))
        self._commit(tok, reads, writes)
        return tok

    def dma(self, e, sem, fns, reads=(), writes=()):
        if callable(fns):
            fns = [fns]
        self._wait(e, self._deps(reads, writes))
        for f in fns:
            self.semtot[sem.num] += 16
            self.q[e].append(("i", f, sem, 16))
        tok = (sem, self.semtot[sem.num])
        self._commit(tok, reads, writes)
        return tok

    def fence(self, engines=ENG):
        toks = [(self.prog[e], self.cnt[e]) for e in self.prog if self.cnt[e] > 0]
        toks += [(self.semobj[n], t) for n, t in self.semtot.items() if t > 0]
        for e in engines:
            self._wait(e, toks)

    def emit(self, block):
        nc = self.nc

        def run(name, h):
            for it in self.q[name]:
                if it[0] == "w":
                    h.wait_ge(it[1], it[2])
                else:
                    ins = it[1](h)
                    if it[2] is not None:
                        ins.then_inc(it[2], it[3])

        @block.tensor
        def _(h):
            run("pe", h)

        @block.scalar
        def _(h):
            run("act", h)

        @block.vector
        def _(h):
            run("dve", h)

        @block.gpsimd
        def _(h):
            run("pool", h)

        @block.sync
        def _(h):
            run("sp", h)


class Arena:
    def __init__(self, nc):
        self.nc = nc
        self.base = (nc.sbuf_base + 63) // 64 * 64
        self.top = nc.sbuf_top
        self.off = self.base
        self.n = 0

    def alloc(self, shape, dtype, name=None):
        esz = 4 if dtype == F32 else 2
        size = esz
        for s in shape[1:]:
            size *= s
        size = (size + 63) // 64 * 64
        assert self.off + size <= self.top, f"SBUF arena overflow {self.off + size - self.top} bytes ({name})"
        self.n += 1
        t = self.nc.alloc_sbuf_tensor_at(f"{name or 'a'}_{self.n}", list(shape), dtype, offset=self.off)
        self.off += size
        return t

    def mark(self):
        return self.off

    def reset(self, m):
        self.off = m


class Ring:
    def __init__(self, g, arena, name, shape, dtype, n, sem=False):
        self.n = n
        self.t = arena.alloc([shape[0], n] + list(shape[1:]), dtype, name)
        self.res = [Res() for _ in range(n)]
        self.sems = [g.dsem(f"{name}_s{k}") for k in range(n)] if sem else None
        self.i = 0

    def next(self):
        k = self.i % self.n
        self.i += 1
        return self.t[:, k], self.res[k], (self.sems[k] if self.sems else None)


def mm(out, lhsT, rhs, start, stop):
    return lambda e: e.matmul(out, lhsT=lhsT, rhs=rhs, start=start, stop=stop)


def actf(out, in_, func, bias=None, scale=None):
    kw = {}
    if bias is not None:
        kw["bias"] = bias
    if scale is not None:
        kw["scale"] = scale
    return lambda e: e.activation(out=out, in_=in_, func=func, **kw)


def tt(out, in0, in1, op):
    return lambda e: e.tensor_tensor(out=out, in0=in0, in1=in1, op=op)


def ts(out, in0, s1, op0):
    return lambda e: e.tensor_scalar(out=out, in0=in0, scalar1=s1, scalar2=None, op0=op0)


def stt(out, in0, scalar, in1, op0, op1):
    return lambda e: e.scalar_tensor_tensor(out=out, in0=in0, scalar=scalar, in1=in1, op0=op0, op1=op1)


def recip(out, in_):
    return lambda e: e.reciprocal(out=out, in_=in_)


def cpy(out, in_):
    return lambda e: e.tensor_copy(out=out, in_=in_)


def mset(ap, v):
    return lambda e: e.memset(ap, v)


def dmaf(out, in_, **kw):
    return lambda e: e.dma_start(out=out, in_=in_, **kw)


def alibi_slopes(n):
    return [2.0 ** (-8.0 * (h + 1) / n) for h in range(n)]


def build(cfg):
    c = cfg
    D, F, HD, HM, S = c["D"], c["F"], c["HD"], c["HM"], c["S"]
    KC, FC, QC, KVC = c["KC"], c["FC"], c["QC"], c["KVC"]
    TOWN, TK, NKT, TKP = c["TOWN"], c["TK"], c["NKT"], c["TKP"]
    SLOT, NW, CG, NVG, MVG, MVW, HGQ, HGK = c["SLOT"], c["NW"], c["CG"], c["NVG"], c["MVG"], c["MVW"], c["HGQ"], c["HGK"]
    NV = c["NV"]
    NE = 2 * HD + HM

    nc = bass.Bass("TRN2", target_bir_lowering=False)

    def din(name, shape, dt=F32):
        return nc.dram_tensor(name, list(shape), dt, kind="ExternalInput").ap()

    def dscr(name, shape, dt):
        return nc.dram_tensor(name, list(shape), dt, kind=("ExternalOutput" if c.get("DEBUG") else "Internal")).ap()

    xT = din("xT", [KC, 128, S])
    metaT = din("metaT", [KC, 128, 16])
    ropeC = din("ropeC", [64, TK])
    ropeS = din("ropeS", [64, TK])
    mown = din("mown", [128, c["WOWN"]])
    moth = din("moth", [128, c["WOTH"]])
    vecs = din("vecs", [128, NV])
    ident = din("ident", [128, 128])
    W = {}
    for nm in ("f1", "f2"):
        W[nm + "g"] = din(nm + "g", [FC, 128, KC * 128])
        W[nm + "u"] = din(nm + "u", [FC, 128, KC * 128])
        W[nm + "d"] = din(nm + "d", [KC, 128, FC * 128])
    W["wi_dq"] = din("wi_dq", [2 * HD, 128, KC * 128])
    W["wi_dk"] = din("wi_dk", [2 * HD, 128, KC * 128])
    W["wi_cq"] = din("wi_cq", [QC, 128, KC * 128])
    W["wi_ckv"] = din("wi_ckv", [KVC, 128, KC * 128])
    W["wi_kr"] = din("wi_kr", [1, 128, KC * 128])
    W["wi_dv"] = din("wi_dv", [NVG * (KC // CG), 128, CG * 512])
    W["uq_n"] = din("uq_n", [HM // HGQ, 128, HGQ * QC * 128])
    W["uq_p"] = din("uq_p", [HM // HGQ, 128, HGQ * QC * 128])
    W["ukv_n"] = din("ukv_n", [HM // HGK, 128, HGK * KVC * 128])
    W["ukv_v"] = din("ukv_v", [HM // MVG, 128, KVC * MVW])
    W["wgt"] = din("wgt", [2 * KC, 128, KC * 128])
    W["wbr"] = din("wbr", [KC, 128, NE * 128])
    W["wo"] = din("wo", [KC, 128, KC * 128])
    yT = nc.dram_tensor("yT", [KC, 128, TOWN], F32, kind="ExternalOutput").ap()

    h1s = dscr("h1s", [KC, 128, TOWN], F32)
    hgs = dscr("hgs", [KC, 128, TOWN], BF16)
    r2s = dscr("r2s", [128, TOWN], F32)
    qdT = dscr("qdT", [2 * HD, 128, TOWN], BF16)
    kdT = dscr("kdT", [2 * HD, 128, TK], BF16)
    vds = dscr("vds", [TK, HD * 256], BF16)
    knT = dscr("knT", [HM, 128, TK], BF16)
    kpT = dscr("kpT", [64, TK], BF16)
    mvs = dscr("mvs", [TK, HM * 128], BF16)
    qnT = dscr("qnT", [HM, 128, TOWN], BF16)
    qpT = dscr("qpT", [HM, 64, TOWN], BF16)
    odT = dscr("odT", [2 * HD, 128, TOWN], BF16)
    omT = dscr("omT", [HM, 128, TOWN], BF16)

    g = Gen(nc)
    ar = Arena(nc)
    ps_all = nc.alloc_psum_tensor("ps", [128, 8, 512], F32)
    PSr = [Res() for _ in range(8)]

    class PRing:
        def __init__(self, banks):
            self.banks = banks
            self.i = 0

        def next(self):
            k = self.banks[self.i % len(self.banks)]
            self.i += 1
            return ps_all[:, k], PSr[k]

    ones32 = ar.alloc([128, 128], F32, "ones32")
    onesbf = ar.alloc([128, 128], BF16, "onesbf")
    onesmeta = ar.alloc([128, 128], BF16, "onesmeta")
    id32 = ar.alloc([128, 128], F32, "id32")
    vec = ar.alloc([128, NV], F32, "vec")
    epsc = ar.alloc([128, 1], F32, "epsc")
    nlam = ar.alloc([128, 1], F32, "nlam")
    sublnS = ar.alloc([128, 2], F32, "sublnS")
    lamt = ar.alloc([128, 4], F32, "lamt")
    csem = g.dsem("csem")
    cres = Res()
    o = 0
    V_G1 = o; o += KC
    V_GM = o; o += KC
    V_G2 = o; o += KC
    V_GF = o; o += KC
    V_GQ = o; o += QC
    V_GKV = o; o += KVC
    V_SUB = o; o += 2
    V_BG = o; o += 2 * KC
    V_LAM = o; o += 4
    assert o == NV

    g.dma("sp", csem, [dmaf(vec[:], vecs), dmaf(id32[:], ident)], writes=(cres,))
    omr, l0, l1, l2, l3 = Res(), Res(), Res(), Res(), Res()
    g.op("dve", [mset(ones32[:], 1.0), mset(onesbf[:], 1.0), mset(onesmeta[:], 0.0), mset(epsc[:], EPS)], writes=(omr,))
    g.op("dve", mset(onesmeta[0:16, :], 1.0), writes=(omr,))
    g.op("dve", tt(lamt[:, 0:1], vec[:, V_LAM:V_LAM + 1], vec[:, V_LAM + 1:V_LAM + 2], ALU.mult), reads=(cres,), writes=(l0,))
    g.op("dve", tt(lamt[:, 1:2], vec[:, V_LAM + 2:V_LAM + 3], vec[:, V_LAM + 3:V_LAM + 4], ALU.mult), reads=(cres,), writes=(l1,))
    g.op("dve", ts(sublnS[:], vec[:, V_SUB:V_SUB + 2], OUT_SCALE, ALU.mult), reads=(cres,), writes=(l3,))
    g.op("pe", mm(ps_all[:, 0, 0:2], ones32[:], lamt[:, 0:2], True, True), reads=(omr, l0, l1), writes=(PSr[0],))
    g.op("act", actf(lamt[:, 2:4], ps_all[:, 0, 0:2], AF.Exp), reads=(PSr[0],), writes=(l2,))
    g.op("dve", tt(lamt[:, 0:1], lamt[:, 3:4], lamt[:, 2:3], ALU.subtract), reads=(l2,), writes=(l0,))
    g.op("dve", lambda e: e.tensor_scalar(out=nlam[:], in0=lamt[:, 0:1], scalar1=-LAM_INIT, scalar2=None, op0=ALU.add),
         reads=(l0,), writes=(l3,))
    g.fence(("pe", "act", "dve", "sp", "pool"))

    persist_mark = ar.mark()
    wt = ar.alloc([128, NW, SLOT], BF16, "wring")
    wres = [Res() for _ in range(NW)]
    wsem = [g.dsem(f"w{k}") for k in range(NW)]
    wi = [0]

    def wload(src, L):
        k = wi[0] % NW
        wi[0] += 1
        dst = wt[:, k, 0:L]
        g.dma("pool", wsem[k], dmaf(dst, src, max_dma_last_dim=8192), writes=(wres[k],))
        return dst, wres[k]

    ac_mark = ar.mark()

    def alloc_AC():
        st = {}
        st["XG"] = ar.alloc([128, KC, 512], BF16, "XG")
        st["XGr"] = [Res() for _ in range(KC)]
        bigsz = max(FC * 512 * 2, (QC + KVC) * 512 * 6, (NE + KC) * 512 * 2)
        big0 = ar.mark()
        st["HT"] = ar.alloc([128, FC, 512], BF16, "HT")
        st["HTr"] = [Res() for _ in range(FC)]
        ar.reset(big0)
        st["CQ"] = ar.alloc([128, QC, 512], F32, "CQ")
        st["CKV"] = ar.alloc([128, KVC, 512], F32, "CKV")
        st["CQN"] = ar.alloc([128, QC, 512], BF16, "CQN")
        st["CKVN"] = ar.alloc([128, KVC, 512], BF16, "CKVN")
        st["CQr"] = [Res() for _ in range(QC)]
        st["CKVr"] = [Res() for _ in range(KVC)]
        st["CQNr"] = [Res() for _ in range(QC)]
        st["CKVNr"] = [Res() for _ in range(KVC)]
        ar.reset(big0)
        st["OD"] = ar.alloc([128, 2 * HD, 512], BF16, "OD")
        st["OM"] = ar.alloc([128, HM, 512], BF16, "OM")
        st["MG"] = ar.alloc([128, KC, 512], BF16, "MG")
        st["MGr"] = [Res() for _ in range(KC)]
        st["ODr"] = Res()
        ar.reset(big0 + (bigsz + 63) // 64 * 64)
        st["XIN"] = Ring(g, ar, "xin", [128, 512], F32, 3, sem=True)
        st["SQ"] = Ring(g, ar, "sq", [128, 512], F32, 2)
        st["ACC"] = [ar.alloc([128, 512], F32, "acc0"), ar.alloc([128, 512], F32, "acc1")]
        st["ACCr"] = [Res(), Res()]
        st["TMP"] = Ring(g, ar, "tmp", [128, 512], F32, 5)
        st["H1"] = Ring(g, ar, "h1", [128, 512], F32, 3, sem=True)
        st["OUTB"] = Ring(g, ar, "outb", [128, 512], BF16, 3, sem=True)
        st["R1"] = ar.alloc([128, 512], F32, "R1"); st["R1r"] = Res()
        st["R2"] = ar.alloc([128, 512], F32, "R2"); st["R2r"] = Res()
        st["RQ"] = ar.alloc([128, 512], F32, "RQ"); st["RQr"] = Res()
        st["RKV"] = ar.alloc([128, 512], F32, "RKV"); st["RKVr"] = Res()
        st["R2T"] = ar.alloc([128, 4], F32, "R2T"); st["R2Tr"] = Res()
        st["RF"] = st["RQ"]; st["RFr"] = st["RQr"]
        st["c5"] = []
        st["ROPE"] = ar.alloc([64, 2, 512], F32, "ROPE"); st["ROPEr"] = Res(); st["ROPEs"] = g.dsem("ropes")
        st["bulks"] = g.dsem("bulks")
        st["PS"] = PRing([2, 3, 4, 5, 6, 7])
        return st

    def norm_stream(st, srcf, N, gcol, ssb, ssr):
        XG, XGr, XIN, SQ = st["XG"], st["XGr"], st["XIN"], st["SQ"]
        for cch in range(KC):
            xin, xr, xs = XIN.next()
            g.dma("sp", xs, srcf(cch, xin), writes=(xr,))
            sq, sqr, _ = SQ.next()
            g.op("act", actf(sq[:, :N], xin[:, :N], AF.Square), reads=(xr,), writes=(sqr,))
            g.op("pe", mm(ssb[:, :N], ones32[:], sq[:, :N], cch == 0, cch == KC - 1), reads=(sqr,), writes=(ssr,))
            g.op("dve", ts(XG[:, cch, :N], xin[:, :N], vec[:, gcol + cch:gcol + cch + 1], ALU.mult),
                 reads=(xr,), writes=(XGr[cch],))

    def rstd(st, ssb, ssr, N, dim, out, outr, P=128):
        tmp, tr, _ = st["TMP"].next()
        g.op("act", actf(tmp[:P, :N], ssb[:P, :N], AF.Ln, bias=epsc[:P, 0:1], scale=1.0 / dim), reads=(ssr,), writes=(tr,))
        g.op("act", actf(out[:P, :N], tmp[:P, :N], AF.Exp, scale=-0.5), reads=(tr,), writes=(outr,))

    def ffn(st, nm, N, resid, post):
        XG, XGr, HT, HTr, PS, TMP, H1, XIN = st["XG"], st["XGr"], st["HT"], st["HTr"], st["PS"], st["TMP"], st["H1"], st["XIN"]
        R1, R1r = st["R1"], st["R1r"]
        wg_, wu_, wd_ = W[nm + "g"], W[nm + "u"], W[nm + "d"]
        for f in range(FC):
            wg, wgr = wload(wg_[f], KC * 128)
            wu, wur = wload(wu_[f], KC * 128)
            pg, pgr = PS.next()
            pu, pur = PS.next()
            g.op("pe", [mm(pg[:, :N], wg[:, k * 128:(k + 1) * 128], XG[:, k, :N], k == 0, k == KC - 1) for k in range(KC)],
                 reads=(wgr, *XGr), writes=(pgr,))
            g.op("pe", [mm(pu[:, :N], wu[:, k * 128:(k + 1) * 128], XG[:, k, :N], k == 0, k == KC - 1) for k in range(KC)],
                 reads=(wur, *XGr), writes=(pur,))
            t1, t1r, _ = TMP.next()
            g.op("dve", tt(t1[:, :N], pg[:, :N], R1[:, :N], ALU.mult), reads=(pgr, R1r), writes=(t1r,))
            t2, t2r, _ = TMP.next()
            g.op("act", actf(t2[:, :N], t1[:, :N], AF.Silu), reads=(t1r,), writes=(t2r,))
            t3, t3r, _ = TMP.next()
            g.op("dve", tt(t3[:, :N], pu[:, :N], R1[:, :N], ALU.mult), reads=(pur, R1r), writes=(t3r,))
            g.op("dve", tt(HT[:, f, :N], t3[:, :N], t2[:, :N], ALU.mult), reads=(t3r, t2r), writes=(HTr[f],))
        nt = SLOT // 128
        for j in range(KC):
            xin, xr, xs = XIN.next()
            resid(j, xin, xr, xs)
            pd, pdr = PS.next()
            f0 = 0
            while f0 < FC:
                f1 = min(FC, f0 + nt)
                w, wr = wload(wd_[j][:, f0 * 128:f1 * 128], (f1 - f0) * 128)
                g.op("pe", [mm(pd[:, :N], w[:, (f - f0) * 128:(f - f0 + 1) * 128], HT[:, f, :N], f == 0, f == FC - 1)
                            for f in range(f0, f1)], reads=(wr, *HTr[f0:f1]), writes=(pdr,))
                f0 = f1
            h1, h1r, h1sem = H1.next()
            g.op("dve", stt(h1[:, :N], pd[:, :N], 0.5, xin[:, :N], ALU.mult, ALU.add), reads=(pdr, xr), writes=(h1r,))
            post(j, h1, h1r, h1sem)

    def sq_acc(st, src, srcr, N, ssb, ssr, first, last, which=0):
        acc, accr = st["ACC"][which], st["ACCr"][which]
        if first:
            g.op("act", actf(acc[:, :N], src, AF.Square), reads=(srcr,), writes=(accr,))
        else:
            sq, sqr, _ = st["SQ"].next()
            g.op("act", actf(sq[:, :N], src, AF.Square), reads=(srcr,), writes=(sqr,))
            g.op("dve", tt(acc[:, :N], acc[:, :N], sq[:, :N], ALU.add), reads=(sqr, accr), writes=(accr,))
        if last:
            g.op("pe", mm(ssb[:, :N], ones32[:], acc[:, :N], True, True), reads=(accr,), writes=(ssr,))

    def phaseA(st, segs, own):
        XG, XGr, PS, TMP, OUTB = st["XG"], st["XGr"], st["PS"], st["TMP"], st["OUTB"]
        R2, R2r = st["R2"], st["R2r"]
        cols = []
        c0_ = 0
        for (kind, s0, n, key0) in segs:
            cols.append((c0_, kind, s0, n, key0))
            c0_ += n
        N = c0_
        tok0 = segs[0][1]
        ss0, ss0r = ps_all[:, 0], PSr[0]
        ss1, ss1r = ps_all[:, 1], PSr[1]

        def srcf(cch, dst):
            return [dmaf(dst[:, a:a + n], (metaT[cch] if kind == "meta" else xT[cch][:, s0:s0 + n])) for (a, kind, s0, n, key0) in cols]

        def fm_fns(dram2d, ob, P=128):
            return [dmaf(dram2d[:, key0:key0 + n], ob[:P, a:a + n]) for (a, kind, s0, n, key0) in cols]

        def tm_fns(dram, ob, b, nt_, d0, d1, w):
            fns = []
            lo_b, hi_b = b * 128, b * 128 + nt_
            for (a, kind, s0, n, key0) in cols:
                lo, hi = max(lo_b, a), min(hi_b, a + n)
                if lo < hi:
                    fns.append(dmaf(dram[key0 + lo - a:key0 + hi - a, d0:d1], ob[lo - lo_b:hi - lo_b, 0:w]))
            return fns

        ROPE, ROPEr = st["ROPE"], st["ROPEr"]
        rf = []
        for (a, kind, s0, n, key0) in cols:
            rf.append(dmaf(ROPE[:, 0, a:a + n], ropeC[:, key0:key0 + n]))
            rf.append(dmaf(ROPE[:, 1, a:a + n], ropeS[:, key0:key0 + n]))
        g.dma("sp", st["ROPEs"], rf, writes=(ROPEr,))
        norm_stream(st, srcf, N, V_G1, ss0, ss0r)
        rstd(st, ss0, ss0r, N, D, st["R1"], st["R1r"])

        def resid(j, xin, xr, xs):
            g.dma("sp", xs, srcf(j, xin), writes=(xr,))

        def post(j, h1, h1r, h1sem):
            if own:
                g.dma("sp", h1sem, dmaf(h1s[j][:, tok0:tok0 + N], h1[:, :N]), reads=(h1r,))
            sq_acc(st, h1[:, :N], h1r, N, ss1, ss1r, j == 0, j == KC - 1, which=1)
            g.op("dve", ts(XG[:, j, :N], h1[:, :N], vec[:, V_GM + j:V_GM + j + 1], ALU.mult), reads=(h1r,), writes=(XGr[j],))

        ffn(st, "f1", N, resid, post)
        rstd(st, ss1, ss1r, N, D, R2, R2r)
        if own:
            bf = [dmaf(hgs[k0:min(KC, k0 + 8), :, tok0:tok0 + N].rearrange("c p t -> p c t"), XG[:, k0:min(KC, k0 + 8), :N])
                  for k0 in range(0, KC, 8)]
            bf.append(dmaf(r2s[:, tok0:tok0 + N], R2[:, :N]))
            g.dma("sp", st["bulks"], bf, reads=(R2r, *XGr))
        R2T, R2Tr = st["R2T"], st["R2Tr"]
        nb = (N + 127) // 128
        for b in range(nb):
            nt_ = min(128, N - b * 128)
            pt, ptr = PS.next()
            g.op("pe", lambda e, pt=pt, b=b, nt_=nt_: e.transpose(out=pt[:nt_, 0:128], in_=R2[:, b * 128:b * 128 + nt_], identity=id32[:]),
                 reads=(R2r,), writes=(ptr,))
            g.op("dve", cpy(R2T[:nt_, b:b + 1], pt[:nt_, 0:1]), reads=(ptr,), writes=(R2Tr,))

        def proj_fm(w, wr, off, kc, rhs, rhsr, M=128, col0=0):
            p, pr = PS.next()
            g.op("pe", [mm(p[:M, :N], w[:, off + k * 128 + col0:off + k * 128 + col0 + M], rhs[:, k, :N], k == 0, k == kc - 1)
                        for k in range(kc)], reads=(wr, *rhsr), writes=(pr,))
            return p, pr

        def store_bf(p, pr, dst, mul=None, mulr=None, P=128, eng="dve"):
            ob, obr, obs = OUTB.next()
            if mul is not None:
                g.op("dve", tt(ob[:P, :N], p[:P, :N], mul[:P, :N], ALU.mult), reads=(pr, mulr), writes=(obr,))
            elif eng == "act":
                g.op("act", actf(ob[:P, :N], p[:P, :N], AF.Copy), reads=(pr,), writes=(obr,))
            else:
                g.op("dve", cpy(ob[:P, :N], p[:P, :N]), reads=(pr,), writes=(obr,))
            g.dma("sp", obs, dst(ob) if callable(dst) else dmaf(dst, ob[:P, :N]), reads=(obr,))

        if own:
            for hc in range(2 * HD):
                w, wr = wload(W["wi_dq"][hc], KC * 128)
                p, pr = proj_fm(w, wr, 0, KC, XG, XGr)
                store_bf(p, pr, qdT[hc][:, tok0:tok0 + N], R2, R2r)
        for hc in range(2 * HD):
            w, wr = wload(W["wi_dk"][hc], KC * 128)
            p, pr = proj_fm(w, wr, 0, KC, XG, XGr)
            store_bf(p, pr, (lambda ob, hc=hc: fm_fns(kdT[hc], ob)), R2, R2r)
        CKV, CKVr, CKVN, CKVNr = st["CKV"], st["CKVr"], st["CKVN"], st["CKVNr"]
        for k in range(KVC):
            w, wr = wload(W["wi_ckv"][k], KC * 128)
            p, pr = proj_fm(w, wr, 0, KC, XG, XGr)
            g.op("dve", tt(CKV[:, k, :N], p[:, :N], R2[:, :N], ALU.mult), reads=(pr, R2r), writes=(CKVr[k],))
            sq_acc(st, CKV[:, k, :N], CKVr[k], N, ss0, ss0r, k == 0, k == KVC - 1, which=0)
        w, wr = wload(W["wi_kr"][0], KC * 128)
        pa, par = proj_fm(w, wr, 0, KC, XG, XGr, M=64, col0=0)
        pb, pbr = proj_fm(w, wr, 0, KC, XG, XGr, M=64, col0=64)
        ta, tar, _ = TMP.next()
        g.op("dve", tt(ta[:64, :N], pa[:64, :N], R2[:64, :N], ALU.mult), reads=(par, R2r), writes=(tar,))
        tb, tbr, _ = TMP.next()
        g.op("dve", tt(tb[:64, :N], pb[:64, :N], R2[:64, :N], ALU.mult), reads=(pbr, R2r), writes=(tbr,))
        tc_, tcr, _ = TMP.next()
        g.op("dve", tt(tc_[:64, :N], ta[:64, :N], ROPE[:, 0, :N], ALU.mult), reads=(tar, ROPEr), writes=(tcr,))
        td, tdr, _ = TMP.next()
        g.op("dve", tt(td[:64, :N], tb[:64, :N], ROPE[:, 1, :N], ALU.mult), reads=(tbr, ROPEr), writes=(tdr,))
        ob, obr, obs = OUTB.next()
        g.op("dve", tt(ob[:64, :N], tc_[:64, :N], td[:64, :N], ALU.add), reads=(tcr, tdr), writes=(obr,))
        g.dma("sp", obs, fm_fns(kpT, ob, P=64), reads=(obr,))
        for gi in range(NVG):
            banks = [PS.next() for _ in range(nb)]
            for cg in range(KC // CG):
                w, wr = wload(W["wi_dv"][gi * (KC // CG) + cg], CG * 512)
                fns = []
                for cc in range(CG):
                    k = cg * CG + cc
                    for b in range(nb):
                        nt_ = min(128, N - b * 128)
                        fns.append(mm(banks[b][0][:nt_, :512], XG[:, k, b * 128:b * 128 + nt_], w[:, cc * 512:(cc + 1) * 512],
                                      k == 0, k == KC - 1))
                g.op("pe", fns, reads=(wr, *XGr), writes=tuple(bk[1] for bk in banks))
            for b in range(nb):
                nt_ = min(128, N - b * 128)
                ob, obr, obs = OUTB.next()
                g.op("act", actf(ob[:nt_, :512], banks[b][0][:nt_, :512], AF.Copy, scale=R2T[:nt_, b:b + 1]),
                     reads=(banks[b][1], R2Tr), writes=(obr,))
                g.dma("sp", obs, tm_fns(vds, ob, b, nt_, gi * 512, (gi + 1) * 512, 512), reads=(obr,))
        rstd(st, ss0, ss0r, N, c["KVL"], st["RKV"], st["RKVr"])
        for k in range(KVC):
            g.op("dve", stt(CKVN[:, k, :N], CKV[:, k, :N], vec[:, V_GKV + k:V_GKV + k + 1], st["RKV"][:, :N], ALU.mult, ALU.mult),
                 reads=(CKVr[k], st["RKVr"]), writes=(CKVNr[k],))
        for hg_ in range(HM // HGK):
            w, wr = wload(W["ukv_n"][hg_], HGK * KVC * 128)
            for hh in range(HGK):
                h = hg_ * HGK + hh
                p, pr = proj_fm(w, wr, hh * KVC * 128, KVC, CKVN, CKVNr)
                store_bf(p, pr, (lambda ob, h=h: fm_fns(knT[h], ob)), eng=("act" if hh % 2 else "dve"))
        for gi in range(HM // MVG):
            banks = [PS.next() for _ in range(nb)]
            w, wr = wload(W["ukv_v"][gi], KVC * MVW)
            fns = []
            for k in range(KVC):
                for b in range(nb):
                    nt_ = min(128, N - b * 128)
                    fns.append(mm(banks[b][0][:nt_, :MVW], CKVN[:, k, b * 128:b * 128 + nt_], w[:, k * MVW:(k + 1) * MVW],
                                  k == 0, k == KVC - 1))
            g.op("pe", fns, reads=(wr, *CKVNr), writes=tuple(bk[1] for bk in banks))
            for b in range(nb):
                nt_ = min(128, N - b * 128)
                ob, obr, obs = OUTB.next()
                g.op("act" if b % 2 else "dve",
                     (actf(ob[:nt_, :MVW], banks[b][0][:nt_, :MVW], AF.Copy) if b % 2 else cpy(ob[:nt_, :MVW], banks[b][0][:nt_, :MVW])),
                     reads=(banks[b][1],), writes=(obr,))
                g.dma("sp", obs, tm_fns(mvs, ob, b, nt_, gi * MVW, (gi + 1) * MVW, MVW), reads=(obr,))
        if own:
            CQ, CQr, CQN, CQNr = st["CQ"], st["CQr"], st["CQN"], st["CQNr"]
            for k in range(QC):
                w, wr = wload(W["wi_cq"][k], KC * 128)
                p, pr = proj_fm(w, wr, 0, KC, XG, XGr)
                g.op("dve", tt(CQ[:, k, :N], p[:, :N], R2[:, :N], ALU.mult), reads=(pr, R2r), writes=(CQr[k],))
                sq_acc(st, CQ[:, k, :N], CQr[k], N, ss1, ss1r, k == 0, k == QC - 1, which=1)
            rstd(st, ss1, ss1r, N, c["QL"], st["RQ"], st["RQr"])
            for k in range(QC):
                g.op("dve", stt(CQN[:, k, :N], CQ[:, k, :N], vec[:, V_GQ + k:V_GQ + k + 1], st["RQ"][:, :N], ALU.mult, ALU.mult),
                     reads=(CQr[k], st["RQr"]), writes=(CQNr[k],))
            for hg_ in range(HM // HGQ):
                w, wr = wload(W["uq_n"][hg_], HGQ * QC * 128)
                for hh in range(HGQ):
                    h = hg_ * HGQ + hh
                    p, pr = proj_fm(w, wr, hh * QC * 128, QC, CQN, CQNr)
                    store_bf(p, pr, qnT[h][:, tok0:tok0 + N], eng=("act" if hh % 2 else "dve"))
            for hg_ in range(HM // HGQ):
                w, wr = wload(W["uq_p"][hg_], HGQ * QC * 128)
                for hh in range(HGQ):
                    h = hg_ * HGQ + hh
                    pa, par = proj_fm(w, wr, hh * QC * 128, QC, CQN, CQNr, M=64, col0=0)
                    pb, pbr = proj_fm(w, wr, hh * QC * 128, QC, CQN, CQNr, M=64, col0=64)
                    tc_, tcr, _ = TMP.next()
                    g.op("dve", tt(tc_[:64, :N], pa[:64, :N], ROPE[:, 0, :N], ALU.mult), reads=(par, ROPEr), writes=(tcr,))
                    td, tdr, _ = TMP.next()
                    g.op("dve", tt(td[:64, :N], pb[:64, :N], ROPE[:, 1, :N], ALU.mult), reads=(pbr, ROPEr), writes=(tdr,))
                    ob, obr, obs = OUTB.next()
                    g.op("dve", tt(ob[:64, :N], tc_[:64, :N], td[:64, :N], ALU.add), reads=(tcr, tdr), writes=(obr,))
                    g.dma("sp", obs, dmaf(qpT[h][:, tok0:tok0 + N], ob[:64, :N]), reads=(obr,))
        g.fence(("pe", "act", "dve"))

    def phaseB():
        m0 = ar.mark()
        MOWN = ar.alloc([128, c["WOWN"]], F32, "MOWN")
        MOTH = ar.alloc([128, c["WOTH"]], F32, "MOTH")
        KPE = ar.alloc([128, TKP], BF16, "KPE")
        tabr = Res()
        tabs = g.dsem("tabs")
        HB = []
        for i in range(2):
            hb = dict(KT=ar.alloc([128, 2, TKP], BF16, f"KT{i}"), V=ar.alloc([128, NKT, 256], BF16, f"V{i}"),
                      QT=ar.alloc([128, 2, TOWN], BF16, f"QT{i}"), QP=ar.alloc([128, TOWN], BF16, f"QP{i}"),
                      res=Res(), sem=g.dsem(f"hb{i}"))
            HB.append(hb)
        LOOK = 4
        E = Ring(g, ar, "E", [128, 512], BF16, 4)
        T = Ring(g, ar, "T", [128, 512], F32, 3)
        TMP = Ring(g, ar, "tmpB", [128, 512], F32, 6)
        SQ = Ring(g, ar, "sqB", [128, 512], F32, 2)
        OUTB = Ring(g, ar, "outbB", [128, 512], BF16, 3, sem=True)
        OC = ar.alloc([128, 2, 2, 512], F32, "OC")
        OCr = [[Res(), Res()], [Res(), Res()]]
        ODt = ar.alloc([128, 2, 512], F32, "ODt")
        ODr = Res()
        SB = PRing([0, 1, 2, 3])

        g.op("dve", mset(KPE[:], 0.0), writes=(tabr,))
        for hb in HB:
            g.op("dve", [mset(hb["KT"][:], 0.0), mset(hb["QP"][:], 0.0)], writes=(hb["res"],))
            g.op("pool", mset(hb["V"][:, NKT - 1, :], 0.0), writes=(hb["res"],))
        g.dma("sp", tabs, [dmaf(MOWN[:], mown), dmaf(MOTH[:], moth), dmaf(KPE[0:64, 0:TK], kpT)], writes=(tabr,))

        NOT = TOWN // 128
        NJ = TOWN // 512

        def load_diff(h, hb):
            fns = []
            for cc in range(2):
                fns.append(dmaf(hb["KT"][:, cc, 0:TK], kdT[2 * h + cc]))
                fns.append(dmaf(hb["QT"][:, cc, :], qdT[2 * h + cc]))
            t0 = 0
            while t0 < NKT - 1:
                t1 = min(NKT - 1, t0 + 8)
                fns.append(dmaf(hb["V"][:, t0:t1, :],
                                vds[t0 * 128:t1 * 128, h * 256:(h + 1) * 256].rearrange("(t p) e -> p t e", p=128)))
                t0 = t1
            fns.append(dmaf(hb["V"][0:16, NKT - 1, :], vds[S:S + 16, h * 256:(h + 1) * 256]))
            g.dma("sp", hb["sem"], fns, writes=(hb["res"],))

        def load_mla(h, hb):
            fns = [dmaf(hb["KT"][:, 0, 0:TK], knT[h]), dmaf(hb["QT"][:, 0, :], qnT[h]), dmaf(hb["QP"][0:64, :], qpT[h])]
            t0 = 0
            while t0 < NKT - 1:
                t1 = min(NKT - 1, t0 + 8)
                fns.append(dmaf(hb["V"][:, t0:t1, 0:128],
                                mvs[t0 * 128:t1 * 128, h * 128:(h + 1) * 128].rearrange("(t p) e -> p t e", p=128)))
                t0 = t1
            fns.append(dmaf(hb["V"][0:16, NKT - 1, 0:128], mvs[S:S + 16, h * 128:(h + 1) * 128]))
            g.dma("sp", hb["sem"], fns, writes=(hb["res"],))

        slopes = alibi_slopes(HD)
        dscale = 128 ** -0.5
        mscale = 192 ** -0.5
        total_heads = HD + HM
        items = []
        deferred = []

        def recip_act(src_ap, src_res, scale_in=None, bias_in=None, power=-1.0):
            lz, lzr, _ = TMP.next()
            g.op("act", actf(lz[:], src_ap, AF.Ln, bias=bias_in, scale=scale_in), reads=(src_res,), writes=(lzr,))
            rz, rzr, _ = TMP.next()
            g.op("act", actf(rz[:], lz[:], AF.Exp, scale=power), reads=(lzr,), writes=(rzr,))
            return rz, rzr

        def mk_diff_epi(h, j, cc, Ob, Zb):
            def epi(pidx):
                g.op("dve", cpy(OC[:, cc, 0], ps_all[:, Ob[0]]), reads=(PSr[Ob[0]],), writes=(OCr[cc][0],))
                g.op("act", actf(OC[:, cc, 1], ps_all[:, Ob[1]], AF.Copy), reads=(PSr[Ob[1]],), writes=(OCr[cc][1],))
                zc, zcr, _ = TMP.next()
                g.op("dve", cpy(zc[:], ps_all[:, Zb]), reads=(PSr[Zb],), writes=(zcr,))

                def part1():
                    rz, rzr = recip_act(zc[:], zcr)
                    for x in range(2):
                        g.op("dve", tt(OC[:, cc, x], OC[:, cc, x], rz[:], ALU.mult), reads=(rzr, OCr[cc][x]), writes=(OCr[cc][x],))
                    if cc == 0:
                        return
                    g.op("dve", stt(ODt[:].rearrange("p a b -> p (a b)"), OC[:, 1].rearrange("p a b -> p (a b)"), nlam[:, 0:1],
                                    OC[:, 0].rearrange("p a b -> p (a b)"), ALU.mult, ALU.add),
                         reads=(OCr[0][0], OCr[0][1], OCr[1][0], OCr[1][1]), writes=(ODr,))
                    sqs = []
                    for x in range(2):
                        sq, sqr, _ = SQ.next()
                        g.op("act", actf(sq[:], ODt[:, x], AF.Square), reads=(ODr,), writes=(sqr,))
                        sqs.append((sq, sqr))

                    def part2():
                        ssb, ssr = SB.next()
                        for x in range(2):
                            g.op("pe", mm(ssb[:], ones32[:], sqs[x][0][:], x == 0, x == 1), reads=(sqs[x][1],), writes=(ssr,))
                        rd, rdr = recip_act(ssb[:], ssr, scale_in=1.0 / 256, bias_in=epsc[:, 0:1], power=-0.5)
                        for x in range(2):
                            ob, obr, obs = OUTB.next()
                            g.op("dve", stt(ob[:], ODt[:, x], sublnS[:, x:x + 1], rd[:], ALU.mult, ALU.mult), reads=(ODr, rdr), writes=(obr,))
                            g.dma("sp", obs, dmaf(odT[2 * h + x][:, j * 512:(j + 1) * 512], ob[:]), reads=(obr,))
                    deferred.append([pidx + 5, part2])
                deferred.append([pidx + 2, part1])
            return epi

        def mk_mla_epi(h, j, Ob, Zb):
            def epi(pidx):
                raw, rawr, _ = TMP.next()
                g.op("dve", cpy(raw[:], ps_all[:, Ob[0]]), reads=(PSr[Ob[0]],), writes=(rawr,))
                zc, zcr, _ = TMP.next()
                g.op("dve", cpy(zc[:], ps_all[:, Zb]), reads=(PSr[Zb],), writes=(zcr,))

                def part1():
                    rz, rzr = recip_act(zc[:], zcr)
                    ob, obr, obs = OUTB.next()
                    g.op("dve", tt(ob[:], raw[:], rz[:], ALU.mult), reads=(rawr, rzr), writes=(obr,))
                    g.dma("sp", obs, dmaf(omT[h][:, j * 512:(j + 1) * 512], ob[:]), reads=(obr,))
                deferred.append([pidx + 2, part1])
            return epi

        mla_blk = 0
        for hh in range(total_heads):
            hb = HB[hh % 2]
            first_of_head = True
            if hh < HD:
                h = hh
                ch = -slopes[h] / dscale
                for j in range(NJ):
                    for cc in range(2):
                        Ob, Zb = [4, 5], 6
                        for i in range(NKT):
                            if i == NKT - 1:
                                bias = None
                            elif i < NOT:
                                s0 = 512 * j - 128 * i + (TOWN - 128)
                                bias = MOWN[:, s0:s0 + 512]
                            else:
                                s0 = 512 * j - 128 * i + (S - 128)
                                bias = MOTH[:, s0:s0 + 512]
                            it = dict(hb=hb, hh=hh, i=i, bias=bias, ch=ch, scale=dscale, ne=2, Ob=Ob, Zb=Zb,
                                      kq=[(hb["KT"][:, cc, i * 128:(i + 1) * 128], hb["QT"][:, cc, j * 512:(j + 1) * 512])],
                                      epi=(mk_diff_epi(h, j, cc, Ob, Zb) if i == NKT - 1 else None), pre=first_of_head)
                            first_of_head = False
                            items.append(it)
            else:
                h = hh - HD
                for j in range(NJ):
                    Ob, Zb = ([4], 5) if mla_blk % 2 == 0 else ([6], 7)
                    mla_blk += 1
                    for i in range(NKT):
                        it = dict(hb=hb, hh=hh, i=i, bias=None, ch=0.0, scale=mscale, ne=1, Ob=Ob, Zb=Zb,
                                  kq=[(hb["KT"][:, 0, i * 128:(i + 1) * 128], hb["QT"][:, 0, j * 512:(j + 1) * 512]),
                                      (KPE[:, i * 128:(i + 1) * 128], hb["QP"][:, j * 512:(j + 1) * 512])],
                                  epi=(mk_mla_epi(h, j, Ob, Zb) if i == NKT - 1 else None), pre=first_of_head)
                        first_of_head = False
                        items.append(it)

        def issue_S(it):
            p, pr = SB.next()
            nk = len(it["kq"])
            g.op("pe", [mm(p[:], k_, q_, x == 0, x == nk - 1) for x, (k_, q_) in enumerate(it["kq"])],
                 reads=(it["hb"]["res"], tabr), writes=(pr,))
            it["S"] = (p, pr)

        def prefetch(hh):
            if hh >= total_heads:
                return
            if hh < HD:
                load_diff(hh, HB[hh % 2])
            else:
                load_mla(hh - HD, HB[hh % 2])

        prefetch(0)
        n_items = len(items)
        AHEAD = 2

        def stage1(it):
            p, pr = it["S"]
            if it["bias"] is not None:
                t, tr, _ = T.next()
                g.op("dve", stt(t[:], it["bias"], it["ch"], p[:], ALU.mult, ALU.add), reads=(pr, tabr), writes=(tr,))
                srcp, srcr = t, tr
            else:
                srcp, srcr = p, pr
            e_, er, _ = E.next()
            g.op("act", actf(e_[:], srcp[:], AF.Exp, scale=it["scale"]), reads=(srcr,), writes=(er,))
            it["E"] = (e_, er)

        def run_deferred(pidx):
            k = 0
            while k < len(deferred):
                if deferred[k][0] <= pidx:
                    deferred.pop(k)[1]()
                    k = 0
                else:
                    k += 1

        for pidx in range(min(LOOK, n_items)):
            issue_S(items[pidx])
        for pidx in range(min(AHEAD, n_items)):
            stage1(items[pidx])
        for pidx in range(n_items):
            it = items[pidx]
            if it["pre"]:
                prefetch(it["hh"] + 1)
            hb = it["hb"]
            i = it["i"]
            e_, er = it["E"]
            first, last = (i == 0), (i == NKT - 1)
            fns = [mm(ps_all[:, it["Ob"][x]], hb["V"][:, i, x * 128:(x + 1) * 128], e_[:], first, last) for x in range(it["ne"])]
            fns.append(mm(ps_all[:, it["Zb"]], (onesmeta if last else onesbf)[:], e_[:], first, last))
            g.op("pe", fns, reads=(er, hb["res"]), writes=tuple(PSr[k] for k in it["Ob"][:it["ne"]] + [it["Zb"]]))
            if pidx + LOOK < n_items:
                issue_S(items[pidx + LOOK])
            if it["epi"] is not None:
                it["epi"](pidx)
            if pidx + AHEAD < n_items:
                stage1(items[pidx + AHEAD])
            run_deferred(pidx)
        run_deferred(10 ** 9)
        ar.reset(m0)

    def phaseC(st, tok0):
        N = 512
        XG, XGr, PS, TMP, H1, XIN = st["XG"], st["XGr"], st["PS"], st["TMP"], st["H1"], st["XIN"]
        OD, OM, MG, MGr, ODr = st["OD"], st["OM"], st["MG"], st["MGr"], st["ODr"]
        R2, R2r = st["R2"], st["R2r"]
        ss0, ss0r = ps_all[:, 0], PSr[0]
        ss1, ss1r = ps_all[:, 1], PSr[1]
        hres = [Res() for _ in range(KC)]
        c5_prev = st["c5"]
        st["c5"] = []
        bf = []
        for k0 in range(0, 2 * HD, 8):
            k1 = min(2 * HD, k0 + 8)
            bf.append(dmaf(OD[:, k0:k1, :], odT[k0:k1, :, tok0:tok0 + N].rearrange("c p t -> p c t")))
        for k0 in range(0, HM, 8):
            k1 = min(HM, k0 + 8)
            bf.append(dmaf(OM[:, k0:k1, :], omT[k0:k1, :, tok0:tok0 + N].rearrange("c p t -> p c t")))
        for k0 in range(0, KC, 8):
            k1 = min(KC, k0 + 8)
            bf.append(dmaf(XG[:, k0:k1, :], hgs[k0:k1, :, tok0:tok0 + N].rearrange("c p t -> p c t")))
        bf.append(dmaf(R2[:], r2s[:, tok0:tok0 + N]))
        g.dma("sp", st["bulks"], bf, writes=(ODr, R2r, *XGr))
        for j in range(KC):
            wa, war = wload(W["wgt"][j], KC * 128)
            wb, wbr_ = wload(W["wgt"][KC + j], KC * 128)
            wc, wcr = wload(W["wbr"][j], NE * 128)
            pgd, pgdr = PS.next()
            g.op("pe", [mm(pgd[:], wa[:, k * 128:(k + 1) * 128], XG[:, k], k == 0, k == KC - 1) for k in range(KC)],
                 reads=(war, *XGr), writes=(pgdr,))
            pgm, pgmr = PS.next()
            g.op("pe", [mm(pgm[:], wb[:, k * 128:(k + 1) * 128], XG[:, k], k == 0, k == KC - 1) for k in range(KC)],
                 reads=(wbr_, *XGr), writes=(pgmr,))
            pbd, pbdr = PS.next()
            g.op("pe", [mm(pbd[:], wc[:, k * 128:(k + 1) * 128], OD[:, k], k == 0, k == 2 * HD - 1) for k in range(2 * HD)],
                 reads=(wcr, ODr), writes=(pbdr,))
            pbm, pbmr = PS.next()
            g.op("pe", [mm(pbm[:], wc[:, (2 * HD + k) * 128:(2 * HD + k + 1) * 128], OM[:, k], k == 0, k == HM - 1) for k in range(HM)],
                 reads=(wcr, ODr), writes=(pbmr,))
            ms = []
            for (pgx, pgxr, pbx, pbxr, bcol) in ((pgd, pgdr, pbd, pbdr, V_BG + j), (pgm, pgmr, pbm, pbmr, V_BG + KC + j)):
                t1, t1r, _ = TMP.next()
                g.op("dve", tt(t1[:], pgx[:], R2[:], ALU.mult), reads=(pgxr, R2r), writes=(t1r,))
                t2, t2r, _ = TMP.next()
                g.op("act", actf(t2[:], t1[:], AF.Sigmoid, bias=vec[:, bcol:bcol + 1]), reads=(t1r,), writes=(t2r,))
                t3, t3r, _ = TMP.next()
                g.op("dve", tt(t3[:], pbx[:], t2[:], ALU.mult), reads=(pbxr, t2r), writes=(t3r,))
                ms.append((t3, t3r))
            g.op("dve", tt(MG[:, j], ms[0][0][:], ms[1][0][:], ALU.add), reads=(ms[0][1], ms[1][1]), writes=(MGr[j],))
            if c5_prev:
                c5_prev.pop(0)()
        while c5_prev:
            c5_prev.pop(0)()
        for j in range(KC):
            w, wr = wload(W["wo"][j], KC * 128)
            xin, xr, xs = XIN.next()
            g.dma("sp", xs, dmaf(xin[:], h1s[j][:, tok0:tok0 + N]), reads=(hres[j],), writes=(xr,))
            p, pr = PS.next()
            g.op("pe", [mm(p[:], w[:, k * 128:(k + 1) * 128], MG[:, k], k == 0, k == KC - 1) for k in range(KC)],
                 reads=(wr, *MGr), writes=(pr,))
            h2, h2r, h2s = H1.next()
            g.op("dve", tt(h2[:], p[:], xin[:], ALU.add), reads=(pr, xr), writes=(h2r,))
            g.dma("sp", h2s, dmaf(h1s[j][:, tok0:tok0 + N], h2[:]), reads=(h2r,), writes=(hres[j],))
            sq_acc(st, h2[:], h2r, N, ss0, ss0r, j == 0, j == KC - 1, which=0)
            g.op("dve", ts(XG[:, j], h2[:], vec[:, V_G2 + j:V_G2 + j + 1], ALU.mult), reads=(h2r,), writes=(XGr[j],))
        rstd(st, ss0, ss0r, N, D, st["R1"], st["R1r"])
        g.fence(("pe", "act", "dve"))

        def resid(j, xin, xr, xs):
            g.dma("sp", xs, dmaf(xin[:], h1s[j][:, tok0:tok0 + N]), reads=(hres[j],), writes=(xr,))

        def post(j, h3, h3r, h3s):
            g.dma("sp", h3s, dmaf(h1s[j][:, tok0:tok0 + N], h3[:]), reads=(h3r,), writes=(hres[j],))
            sq_acc(st, h3[:], h3r, N, ss1, ss1r, j == 0, j == KC - 1, which=1)

        ffn(st, "f2", N, resid, post)
        rstd(st, ss1, ss1r, N, D, st["RF"], st["RFr"])
        g.fence(("pe", "act", "dve", "sp"))

        def c5_step(j):
            xin, xr, xs = XIN.next()
            g.dma("sp", xs, dmaf(xin[:], h1s[j][:, tok0:tok0 + N]), reads=(hres[j],), writes=(xr,))
            y, yr, ys = H1.next()
            g.op("dve", stt(y[:], xin[:], vec[:, V_GF + j:V_GF + j + 1], st["RF"][:], ALU.mult, ALU.mult), reads=(xr, st["RFr"]), writes=(yr,))
            g.dma("sp", ys, dmaf(yT[j][:, tok0:tok0 + N], y[:]), reads=(yr,))
        for j in range(KC):
            st["c5"].append(lambda j=j: c5_step(j))

    st = alloc_AC()
    for t in range(c["NTO"]):
        phaseA(st, [("x", t * 512, 512, t * 512)], True)
    rest = TOWN + 16
    nto = -(-rest // 512)
    base = -(-(-(-rest // nto)) // 32) * 32
    pos = TOWN
    for k in range(nto):
        n = base if k < nto - 1 else (S - pos)
        seg = [("x", pos, n, pos)]
        if k == nto - 1:
            seg.append(("meta", 0, 16, S))
        assert 0 < sum(x[2] for x in seg) <= 512
        phaseA(st, seg, False)
        pos += n
    g.fence()
    if c.get("STOP") != "A":
        ar.reset(persist_mark)
        phaseB()
        g.fence()
        if c.get("STOP") != "B":
            ar.reset(ac_mark)
            st = alloc_AC()
            for t in range(c["NTO"]):
                phaseC(st, t * 512)
            while st["c5"]:
                st["c5"].pop(0)()
            g.fence()

    with nc.Block() as block:
        g.emit(block)
    return nc


def lay_cols_fm(Wm, kc):
    n = Wm.shape[1] // 128
    return np.ascontiguousarray(Wm.reshape(kc, 128, n, 128).transpose(2, 1, 0, 3).reshape(n, 128, kc * 128))


def vec_cols(v):
    v = np.asarray(v, np.float32).reshape(-1)
    return v.reshape(-1, 128).T


def swap_halves(Wp):
    h = Wp.shape[-1] // 2
    return np.concatenate([Wp[..., h:], Wp[..., :h]], axis=-1)


def prep_shared(cfg, inp):
    c = cfg
    D, F, HD, HM, QL, KVL = c["D"], c["F"], c["HD"], c["HM"], c["QL"], c["KVL"]
    KC, FC, QC, KVC, CG, NVG, MVG, MVW, HGQ, HGK = (c[k] for k in ("KC", "FC", "QC", "KVC", "CG", "NVG", "MVG", "MVW", "HGQ", "HGK"))
    f32 = np.float32
    sh = {}
    for nm, pre in (("f1", "ffn1"), ("f2", "ffn2")):
        sh[nm + "g"] = lay_cols_fm(np.asarray(inp[pre + "_w_gate"][0], f32), KC)
        sh[nm + "u"] = lay_cols_fm(np.asarray(inp[pre + "_w_up"][0], f32), KC)
        sh[nm + "d"] = lay_cols_fm(np.asarray(inp[pre + "_w_down"][0], f32), FC)
    w_in = np.asarray(inp["w_in"][0], f32)
    QKW = HD * 256
    o_dq, o_dk, o_dv = 0, QKW, 2 * QKW
    o_cq = 3 * QKW
    o_ckv = o_cq + QL
    o_kr = o_ckv + KVL
    sh["wi_dq"] = lay_cols_fm(w_in[:, o_dq:o_dq + QKW], KC)
    sh["wi_dk"] = lay_cols_fm(w_in[:, o_dk:o_dk + QKW], KC)
    sh["wi_cq"] = lay_cols_fm(w_in[:, o_cq:o_cq + QL], KC)
    sh["wi_ckv"] = lay_cols_fm(w_in[:, o_ckv:o_ckv + KVL], KC)
    wkr = w_in[:, o_kr:o_kr + 64]
    sh["wi_kr"] = lay_cols_fm(np.concatenate([wkr, swap_halves(wkr)], axis=1), KC)
    wv = w_in[:, o_dv:o_dv + QKW]
    sh["wi_dv"] = np.ascontiguousarray(
        wv.reshape(KC // CG, CG, 128, NVG, 512).transpose(3, 0, 2, 1, 4).reshape(NVG * (KC // CG), 128, CG * 512))
    wuq = np.asarray(inp["mla_w_uq"][0], f32).reshape(QL, HM, 192)
    wn = np.ascontiguousarray(wuq[:, :, :128]).reshape(QL, HM * 128)
    pn = lay_cols_fm(wn, QC)
    sh["uq_n"] = np.ascontiguousarray(pn.reshape(HM // HGQ, HGQ, 128, QC * 128).transpose(0, 2, 1, 3).reshape(HM // HGQ, 128, HGQ * QC * 128))
    wp = wuq[:, :, 128:192]
    wpp = np.concatenate([wp, swap_halves(wp)], axis=2).reshape(QL, HM * 128)
    pp = lay_cols_fm(np.ascontiguousarray(wpp), QC)
    sh["uq_p"] = np.ascontiguousarray(pp.reshape(HM // HGQ, HGQ, 128, QC * 128).transpose(0, 2, 1, 3).reshape(HM // HGQ, 128, HGQ * QC * 128))
    wukv = np.asarray(inp["mla_w_ukv"][0], f32).reshape(KVL, HM, 256)
    wkn = np.ascontiguousarray(wukv[:, :, :128]).reshape(KVL, HM * 128)
    pk = lay_cols_fm(wkn, KVC)
    sh["ukv_n"] = np.ascontiguousarray(pk.reshape(HM // HGK, HGK, 128, KVC * 128).transpose(0, 2, 1, 3).reshape(HM // HGK, 128, HGK * KVC * 128))
    wvv = np.ascontiguousarray(wukv[:, :, 128:]).reshape(KVC, 128, HM // MVG, MVW)
    sh["ukv_v"] = np.ascontiguousarray(wvv.transpose(2, 1, 0, 3).reshape(HM // MVG, 128, KVC * MVW))
    sh["wgt"] = lay_cols_fm(np.asarray(inp["w_gate"][0], f32), KC)
    bd = lay_cols_fm(np.asarray(inp["w_branch_diff"][0], f32), 2 * HD)
    bm = lay_cols_fm(np.asarray(inp["w_branch_mla"][0], f32), HM)
    sh["wbr"] = np.ascontiguousarray(np.concatenate([bd, bm], axis=2))
    sh["wo"] = lay_cols_fm(np.asarray(inp["w_out"][0], f32), KC)
    cols = [vec_cols(inp["ffn1_norm"][0]), vec_cols(inp["mix_norm"][0]), vec_cols(inp["ffn2_norm"][0]), vec_cols(inp["final_norm"]),
            vec_cols(inp["mla_q_norm"][0]), vec_cols(inp["mla_kv_norm"][0]), vec_cols(inp["diff_subln"][0]), vec_cols(inp["b_gate"][0]),
            vec_cols(inp["diff_lambda_q1"][0]), vec_cols(inp["diff_lambda_k1"][0]), vec_cols(inp["diff_lambda_q2"][0]),
            vec_cols(inp["diff_lambda_k2"][0])]
    sh["vecs"] = np.ascontiguousarray(np.concatenate(cols, axis=1).astype(f32))
    assert sh["vecs"].shape[1] == c["NV"]
    sh["ident"] = np.eye(128, dtype=f32)
    return sh


def prep_core(cfg, inp, core):
    c = cfg
    S, TOWN, KC, TK = c["S"], c["TOWN"], c["KC"], c["TK"]
    f32 = np.float32
    b, half = core // 2, core % 2
    x = np.asarray(inp["x"][b], f32)
    order = np.concatenate([np.arange(half * TOWN, (half + 1) * TOWN), np.arange((1 - half) * TOWN, (2 - half) * TOWN)])
    pc = {}
    pc["xT"] = np.ascontiguousarray(x[order].T.reshape(KC, 128, S))
    pc["metaT"] = np.ascontiguousarray(np.asarray(inp["meta_tokens"], f32).T.reshape(KC, 128, 16))
    pos = np.concatenate([16 + order, np.arange(16)]).astype(f32)
    inv_freq = (1.0 / (ROPE_THETA ** (np.arange(0, 64, 2, dtype=f32) / f32(64)))).astype(f32)
    ang = (pos[None, :] * inv_freq[:, None]).astype(f32)
    cs, sn = np.cos(ang).astype(f32), np.sin(ang).astype(f32)
    pc["ropeC"] = np.ascontiguousarray(np.concatenate([cs, cs], axis=0))
    pc["ropeS"] = np.ascontiguousarray(np.concatenate([-sn, sn], axis=0))
    kk = np.arange(128, dtype=np.int64)[:, None]
    u = np.arange(c["WOWN"], dtype=np.int64)[None, :] - (TOWN - 128)
    pc["mown"] = np.abs(u - kk).astype(f32)
    u = np.arange(c["WOTH"], dtype=np.int64)[None, :] - (S - 128)
    pc["moth"] = ((kk - u) if half == 0 else (S + u - kk)).astype(f32)
    return pc


def run_cfg(cfg, inp, trace=False):
    nc = build(cfg)
    sh = prep_shared(cfg, inp)
    in_maps = []
    for core in range(NCORES):
        m = dict(sh)
        m.update(prep_core(cfg, inp, core))
        in_maps.append(m)
    res = run_bass_kernel_spmd(nc, in_maps, core_ids=list(range(NCORES)), trace=trace)
    S, TOWN, D = cfg["S"], cfg["TOWN"], cfg["D"]
    out = np.empty((cfg["B"], S, D), np.float32)
    for core in range(NCORES):
        b, half = core // 2, core % 2
        y = np.asarray(res.results[core]["yT"]).reshape(D, TOWN)
        out[b, half * TOWN:(half + 1) * TOWN, :] = y.T
    return out, res


def kernel(**inputs):
    cfg = make_cfg()
    out, _ = run_cfg(cfg, inputs)
    return out
```

```python
import math
import numpy as np
import concourse.bass as bass
import concourse.mybir as mybir
from concourse.bass_utils import run_bass_kernel_spmd

F32 = mybir.dt.float32
BF16 = mybir.dt.bfloat16
AF = mybir.ActivationFunctionType
ALU = mybir.AluOpType
EPS = 1e-6
LAM_INIT = 0.2
OUT_SCALE = 0.8
ROPE_THETA = 10000.0
NCORES = 8


def make_cfg(D=4096, F=11008, HD=8, HM=16, QL=1024, KVL=512, S=4096, B=4, SLOT=4096, NW=5):
    c = dict(D=D, F=F, HD=HD, HM=HM, QL=QL, KVL=KVL, S=S, B=B, NMETA=16, SLOT=SLOT, NW=NW)
    c["KC"] = D // 128
    c["FC"] = F // 128
    c["QC"] = QL // 128
    c["KVC"] = KVL // 128
    c["TOWN"] = S // 2
    c["NTO"] = c["TOWN"] // 512
    c["NTA"] = S // 512
    c["TK"] = S + 16
    c["NKT"] = S // 128 + 1
    c["TKP"] = c["NKT"] * 128
    c["CG"] = min(8, c["KC"])
    c["NVG"] = (HD * 256) // 512
    c["MVG"] = min(4, HM)
    c["MVW"] = c["MVG"] * 128
    c["HGQ"] = max(1, min(HM, SLOT // (c["QC"] * 128)))
    c["HGK"] = max(1, min(HM, SLOT // (c["KVC"] * 128)))
    c["WOWN"] = 2 * c["TOWN"] - 128
    c["WOTH"] = S - 128
    c["NV"] = 4 * c["KC"] + c["QC"] + c["KVC"] + 2 + 2 * c["KC"] + 4
    assert D % 128 == 0 and F % 128 == 0 and c["TOWN"] % 512 == 0 and HD % 2 == 0
    assert HM % c["HGQ"] == 0 and HM % c["HGK"] == 0 and HM % c["MVG"] == 0 and c["KC"] % c["CG"] == 0
    return c


class Res:
    __slots__ = ("w", "r")

    def __init__(self):
        self.w = None
        self.r = {}


class Gen:
    ENG = ("pe", "act", "dve", "pool", "sp")

    def __init__(self, nc):
        self.nc = nc
        self.q = {e: [] for e in self.ENG}
        self.prog = {e: nc.alloc_semaphore("pg_" + e) for e in ("pe", "act", "dve", "pool")}
        self.cnt = {e: 0 for e in self.prog}
        self.waited = {e: {} for e in self.ENG}
        self.semtot = {}
        self.semobj = {}

    def dsem(self, name):
        s = self.nc.alloc_semaphore(f"{name}_{len(self.semtot)}")
        self.semtot[s.num] = 0
        self.semobj[s.num] = s
        return s

    def _wait(self, e, deps):
        best = {}
        for d in deps:
            if d is None:
                continue
            sem, val = d
            if sem.num not in best or best[sem.num][1] < val:
                best[sem.num] = (sem, val)
        for sem, val in best.values():
            if e == "pe" and sem.num == self.prog["pe"].num:
                continue
            if self.waited[e].get(sem.num, 0) >= val:
                continue
            self.waited[e][sem.num] = val
            self.q[e].append(("w", sem, val))

    @staticmethod
    def _deps(reads, writes):
        deps = []
        for r in reads:
            if r.w is not None:
                deps.append(r.w)
        for w in writes:
            if w.w is not None:
                deps.append(w.w)
            deps.extend(w.r.values())
        return deps

    @staticmethod
    def _commit(tok, reads, writes):
        sem, val = tok
        for r in reads:
            r.r[sem.num] = tok
        for w in writes:
            w.w = tok
            w.r = {}

    def op(self, e, fns, reads=(), writes=()):
        if callable(fns):
            fns = [fns]
        self._wait(e, self._deps(reads, writes))
        self.cnt[e] += 1
        tok = (self.prog[e], self.cnt[e])
        for f in fns[:-1]:
            self.q[e].append(("i", f, None, 0))
        self.q[e].append(("i", fns[-1], self.prog[e], 1))
        self._commit(tok, reads, writes)
        return tok

    def dma(self, e, sem, fns, reads=(), writes=()):
        if callable(fns):
            fns = [fns]
        self._wait(e, self._deps(reads, writes))
        for f in fns:
            self.semtot[sem.num] += 16
            self.q[e].append(("i", f, sem, 16))
        tok = (sem, self.semtot[sem.num])
        self._commit(tok, reads, writes)
        return tok

    def fence(self, engines=ENG):
        toks = [(self.prog[e], self.cnt[e]) for e in self.prog if self.cnt[e] > 0]
        toks += [(self.semobj[n], t) for n, t in self.semtot.items() if t > 0]
        for e in engines:
            self._wait(e, toks)

    def emit(self, block):
        nc = self.nc

        def run(name, h):
            for it in self.q[name]:
                if it[0] == "w":
                    h.wait_ge(it[1], it[2])
                else:
                    ins = it[1](h)
                    if it[2] is not None:
                        ins.then_inc(it[2], it[3])

        @block.tensor
        def _(h):
            run("pe", h)

        @block.scalar
        def _(h):
            run("act", h)

        @block.vector
        def _(h):
            run("dve", h)

        @block.gpsimd
        def _(h):
            run("pool", h)

        @block.sync
        def _(h):
            run("sp", h)


class Arena:
    def __init__(self, nc):
        self.nc = nc
        self.base = (nc.sbuf_base + 63) // 64 * 64
        self.top = nc.sbuf_top
        self.off = self.base
        self.n = 0

    def alloc(self, shape, dtype, name=None):
        esz = 4 if dtype == F32 else 2
        size = esz
        for s in shape[1:]:
            size *= s
        size = (size + 63) // 64 * 64
        assert self.off + size <= self.top, f"SBUF arena overflow {self.off + size - self.top} bytes ({name})"
        self.n += 1
        t = self.nc.alloc_sbuf_tensor_at(f"{name or 'a'}_{self.n}", list(shape), dtype, offset=self.off)
        self.off += size
        return t

    def mark(self):
        return self.off

    def reset(self, m):
        self.off = m


class Ring:
    def __init__(self, g, arena, name, shape, dtype, n, sem=False):
        self.n = n
        self.t = arena.alloc([shape[0], n] + list(shape[1:]), dtype, name)
        self.res = [Res() for _ in range(n)]
        self.sems = [g.dsem(f"{name}_s{k}") for k in range(n)] if sem else None
        self.i = 0

    def next(self):
        k = self.i % self.n
        self.i += 1
        return self.t[:, k], self.res[k], (self.sems[k] if self.sems else None)


def mm(out, lhsT, rhs, start, stop):
    return lambda e: e.matmul(out, lhsT=lhsT, rhs=rhs, start=start, stop=stop)


def actf(out, in_, func, bias=None, scale=None):
    kw = {}
    if bias is not None:
        kw["bias"] = bias
    if scale is not None:
        kw["scale"] = scale
    return lambda e: e.activation(out=out, in_=in_, func=func, **kw)


def tt(out, in0, in1, op):
    return lambda e: e.tensor_tensor(out=out, in0=in0, in1=in1, op=op)


def ts(out, in0, s1, op0):
    return lambda e: e.tensor_scalar(out=out, in0=in0, scalar1=s1, scalar2=None, op0=op0)


def stt(out, in0, scalar, in1, op0, op1):
    return lambda e: e.scalar_tensor_tensor(out=out, in0=in0, scalar=scalar, in1=in1, op0=op0, op1=op1)


def recip(out, in_):
    return lambda e: e.reciprocal(out=out, in_=in_)


def cpy(out, in_):
    return lambda e: e.tensor_copy(out=out, in_=in_)


def mset(ap, v):
    return lambda e: e.memset(ap, v)


def dmaf(out, in_, **kw):
    return lambda e: e.dma_start(out=out, in_=in_, **kw)


def alibi_slopes(n):
    return [2.0 ** (-8.0 * (h + 1) / n) for h in range(n)]


def build(cfg):
    c = cfg
    D, F, HD, HM, S = c["D"], c["F"], c["HD"], c["HM"], c["S"]
    KC, FC, QC, KVC = c["KC"], c["FC"], c["QC"], c["KVC"]
    TOWN, TK, NKT, TKP = c["TOWN"], c["TK"], c["NKT"], c["TKP"]
    SLOT, NW, CG, NVG, MVG, MVW, HGQ, HGK = c["SLOT"], c["NW"], c["CG"], c["NVG"], c["MVG"], c["MVW"], c["HGQ"], c["HGK"]
    NV = c["NV"]
    NE = 2 * HD + HM

    nc = bass.Bass("TRN2", target_bir_lowering=False)

    def din(name, shape, dt=F32):
        return nc.dram_tensor(name, list(shape), dt, kind="ExternalInput").ap()

    def dscr(name, shape, dt):
        return nc.dram_tensor(name, list(shape), dt, kind=("ExternalOutput" if c.get("DEBUG") else "Internal")).ap()

    xT = din("xT", [KC, 128, S])
    metaT = din("metaT", [KC, 128, 16])
    ropeC = din("ropeC", [64, TK])
    ropeS = din("ropeS", [64, TK])
    mown = din("mown", [128, c["WOWN"]])
    moth = din("moth", [128, c["WOTH"]])
    vecs = din("vecs", [128, NV])
    ident = din("ident", [128, 128])
    W = {}
    for nm in ("f1", "f2"):
        W[nm + "g"] = din(nm + "g", [FC, 128, KC * 128])
        W[nm + "u"] = din(nm + "u", [FC, 128, KC * 128])
        W[nm + "d"] = din(nm + "d", [KC, 128, FC * 128])
    W["wi_dq"] = din("wi_dq", [2 * HD, 128, KC * 128])
    W["wi_dk"] = din("wi_dk", [2 * HD, 128, KC * 128])
    W["wi_cq"] = din("wi_cq", [QC, 128, KC * 128])
    W["wi_ckv"] = din("wi_ckv", [KVC, 128, KC * 128])
    W["wi_kr"] = din("wi_kr", [1, 128, KC * 128])
    W["wi_dv"] = din("wi_dv", [NVG * (KC // CG), 128, CG * 512])
    W["uq_n"] = din("uq_n", [HM // HGQ, 128, HGQ * QC * 128])
    W["uq_p"] = din("uq_p", [HM // HGQ, 128, HGQ * QC * 128])
    W["ukv_n"] = din("ukv_n", [HM // HGK, 128, HGK * KVC * 128])
    W["ukv_v"] = din("ukv_v", [HM // MVG, 128, KVC * MVW])
    W["wgt"] = din("wgt", [2 * KC, 128, KC * 128])
    W["wbr"] = din("wbr", [KC, 128, NE * 128])
    W["wo"] = din("wo", [KC, 128, KC * 128])
    yT = nc.dram_tensor("yT", [KC, 128, TOWN], F32, kind="ExternalOutput").ap()

    h1s = dscr("h1s", [KC, 128, TOWN], F32)
    hgs = dscr("hgs", [KC, 128, TOWN], BF16)
    r2s = dscr("r2s", [128, TOWN], F32)
    qdT = dscr("qdT", [2 * HD, 128, TOWN], BF16)
    kdT = dscr("kdT", [2 * HD, 128, TK], BF16)
    vds = dscr("vds", [TK, HD * 256], BF16)
    knT = dscr("knT", [HM, 128, TK], BF16)
    kpT = dscr("kpT", [64, TK], BF16)
    mvs = dscr("mvs", [TK, HM * 128], BF16)
    qnT = dscr("qnT", [HM, 128, TOWN], BF16)
    qpT = dscr("qpT", [HM, 64, TOWN], BF16)
    odT = dscr("odT", [2 * HD, 128, TOWN], BF16)
    omT = dscr("omT", [HM, 128, TOWN], BF16)

    g = Gen(nc)
    ar = Arena(nc)
    ps_all = nc.alloc_psum_tensor("ps", [128, 8, 512], F32)
    PSr = [Res() for _ in range(8)]

    class PRing:
        def __init__(self, banks):
            self.banks = banks
            self.i = 0

        def next(self):
            k = self.banks[self.i % len(self.banks)]
            self.i += 1
            return ps_all[:, k], PSr[k]

    ones32 = ar.alloc([128, 128], F32, "ones32")
    onesbf = ar.alloc([128, 128], BF16, "onesbf")
    onesmeta = ar.alloc([128, 128], BF16, "onesmeta")
    id32 = ar.alloc([128, 128], F32, "id32")
    vec = ar.alloc([128, NV], F32, "vec")
    epsc = ar.alloc([128, 1], F32, "epsc")
    nlam = ar.alloc([128, 1], F32, "nlam")
    sublnS = ar.alloc([128, 2], F32, "sublnS")
    lamt = ar.alloc([128, 4], F32, "lamt")
    csem = g.dsem("csem")
    cres = Res()
    o = 0
    V_G1 = o; o += KC
    V_GM = o; o += KC
    V_G2 = o; o += KC
    V_GF = o; o += KC
    V_GQ = o; o += QC
    V_GKV = o; o += KVC
    V_SUB = o; o += 2
    V_BG = o; o += 2 * KC
    V_LAM = o; o += 4
    assert o == NV

    g.dma("sp", csem, [dmaf(vec[:], vecs), dmaf(id32[:], ident)], writes=(cres,))
    omr, l0, l1, l2, l3 = Res(), Res(), Res(), Res(), Res()
    g.op("dve", [mset(ones32[:], 1.0), mset(onesbf[:], 1.0), mset(onesmeta[:], 0.0), mset(epsc[:], EPS)], writes=(omr,))
    g.op("dve", mset(onesmeta[0:16, :], 1.0), writes=(omr,))
    g.op("dve", tt(lamt[:, 0:1], vec[:, V_LAM:V_LAM + 1], vec[:, V_LAM + 1:V_LAM + 2], ALU.mult), reads=(cres,), writes=(l0,))
    g.op("dve", tt(lamt[:, 1:2], vec[:, V_LAM + 2:V_LAM + 3], vec[:, V_LAM + 3:V_LAM + 4], ALU.mult), reads=(cres,), writes=(l1,))
    g.op("dve", ts(sublnS[:], vec[:, V_SUB:V_SUB + 2], OUT_SCALE, ALU.mult), reads=(cres,), writes=(l3,))
    g.op("pe", mm(ps_all[:, 0, 0:2], ones32[:], lamt[:, 0:2], True, True), reads=(omr, l0, l1), writes=(PSr[0],))
    g.op("act", actf(lamt[:, 2:4], ps_all[:, 0, 0:2], AF.Exp), reads=(PSr[0],), writes=(l2,))
    g.op("dve", tt(lamt[:, 0:1], lamt[:, 3:4], lamt[:, 2:3], ALU.subtract), reads=(l2,), writes=(l0,))
    g.op("dve", lambda e: e.tensor_scalar(out=nlam[:], in0=lamt[:, 0:1], scalar1=-LAM_INIT, scalar2=None, op0=ALU.add),
         reads=(l0,), writes=(l3,))
    g.fence(("pe", "act", "dve", "sp", "pool"))

    persist_mark = ar.mark()
    wt = ar.alloc([128, NW, SLOT], BF16, "wring")
    wres = [Res() for _ in range(NW)]
    wsem = [g.dsem(f"w{k}") for k in range(NW)]
    wi = [0]

    def wload(src, L):
        k = wi[0] % NW
        wi[0] += 1
        dst = wt[:, k, 0:L]
        g.dma("pool", wsem[k], dmaf(dst, src, max_dma_last_dim=8192), writes=(wres[k],))
        return dst, wres[k]

    ac_mark = ar.mark()

    def alloc_AC():
        st = {}
        st["XG"] = ar.alloc([128, KC, 512], BF16, "XG")
        st["XGr"] = [Res() for _ in range(KC)]
        bigsz = max(FC * 512 * 2, (QC + KVC) * 512 * 6, (NE + KC) * 512 * 2)
        big0 = ar.mark()
        st["HT"] = ar.alloc([128, FC, 512], BF16, "HT")
        st["HTr"] = [Res() for _ in range(FC)]
        ar.reset(big0)
        st["CQ"] = ar.alloc([128, QC, 512], F32, "CQ")
        st["CKV"] = ar.alloc([128, KVC, 512], F32, "CKV")
        st["CQN"] = ar.alloc([128, QC, 512], BF16, "CQN")
        st["CKVN"] = ar.alloc([128, KVC, 512], BF16, "CKVN")
        st["CQr"] = [Res() for _ in range(QC)]
        st["CKVr"] = [Res() for _ in range(KVC)]
        st["CQNr"] = [Res() for _ in range(QC)]
        st["CKVNr"] = [Res() for _ in range(KVC)]
        ar.reset(big0)
        st["OD"] = ar.alloc([128, 2 * HD, 512], BF16, "OD")
        st["OM"] = ar.alloc([128, HM, 512], BF16, "OM")
        st["MG"] = ar.alloc([128, KC, 512], BF16, "MG")
        st["MGr"] = [Res() for _ in range(KC)]
        st["ODr"] = Res()
        ar.reset(big0 + (bigsz + 63) // 64 * 64)
        st["XIN"] = Ring(g, ar, "xin", [128, 512], F32, 3, sem=True)
        st["SQ"] = Ring(g, ar, "sq", [128, 512], F32, 2)
        st["ACC"] = [ar.alloc([128, 512], F32, "acc0"), ar.alloc([128, 512], F32, "acc1")]
        st["ACCr"] = [Res(), Res()]
        st["TMP"] = Ring(g, ar, "tmp", [128, 512], F32, 5)
        st["H1"] = Ring(g, ar, "h1", [128, 512], F32, 3, sem=True)
        st["OUTB"] = Ring(g, ar, "outb", [128, 512], BF16, 3, sem=True)
        st["R1"] = ar.alloc([128, 512], F32, "R1"); st["R1r"] = Res()
        st["R2"] = ar.alloc([128, 512], F32, "R2"); st["R2r"] = Res()
        st["RQ"] = ar.alloc([128, 512], F32, "RQ"); st["RQr"] = Res()
        st["RKV"] = ar.alloc([128, 512], F32, "RKV"); st["RKVr"] = Res()
        st["R2T"] = ar.alloc([128, 4], F32, "R2T"); st["R2Tr"] = Res()
        st["RF"] = st["RQ"]; st["RFr"] = st["RQr"]
        st["c5"] = []
        st["ROPE"] = ar.alloc([64, 2, 512], F32, "ROPE"); st["ROPEr"] = Res(); st["ROPEs"] = g.dsem("ropes")
        st["bulks"] = g.dsem("bulks")
        st["PS"] = PRing([2, 3, 4, 5, 6, 7])
        return st

    def norm_stream(st, srcf, N, gcol, ssb, ssr):
        XG, XGr, XIN, SQ = st["XG"], st["XGr"], st["XIN"], st["SQ"]
        for cch in range(KC):
            xin, xr, xs = XIN.next()
            g.dma("sp", xs, srcf(cch, xin), writes=(xr,))
            sq, sqr, _ = SQ.next()
            g.op("act", actf(sq[:, :N], xin[:, :N], AF.Square), reads=(xr,), writes=(sqr,))
            g.op("pe", mm(ssb[:, :N], ones32[:], sq[:, :N], cch == 0, cch == KC - 1), reads=(sqr,), writes=(ssr,))
            g.op("dve", ts(XG[:, cch, :N], xin[:, :N], vec[:, gcol + cch:gcol + cch + 1], ALU.mult),
                 reads=(xr,), writes=(XGr[cch],))

    def rstd(st, ssb, ssr, N, dim, out, outr, P=128):
        tmp, tr, _ = st["TMP"].next()
        g.op("act", actf(tmp[:P, :N], ssb[:P, :N], AF.Ln, bias=epsc[:P, 0:1], scale=1.0 / dim), reads=(ssr,), writes=(tr,))
        g.op("act", actf(out[:P, :N], tmp[:P, :N], AF.Exp, scale=-0.5), reads=(tr,), writes=(outr,))

    def ffn(st, nm, N, resid, post):
        XG, XGr, HT, HTr, PS, TMP, H1, XIN = st["XG"], st["XGr"], st["HT"], st["HTr"], st["PS"], st["TMP"], st["H1"], st["XIN"]
        R1, R1r = st["R1"], st["R1r"]
        wg_, wu_, wd_ = W[nm + "g"], W[nm + "u"], W[nm + "d"]
        for f in range(FC):
            wg, wgr = wload(wg_[f], KC * 128)
            wu, wur = wload(wu_[f], KC * 128)
            pg, pgr = PS.next()
            pu, pur = PS.next()
            g.op("pe", [mm(pg[:, :N], wg[:, k * 128:(k + 1) * 128], XG[:, k, :N], k == 0, k == KC - 1) for k in range(KC)],
                 reads=(wgr, *XGr), writes=(pgr,))
            g.op("pe", [mm(pu[:, :N], wu[:, k * 128:(k + 1) * 128], XG[:, k, :N], k == 0, k == KC - 1) for k in range(KC)],
                 reads=(wur, *XGr), writes=(pur,))
            t1, t1r, _ = TMP.next()
            g.op("dve", tt(t1[:, :N], pg[:, :N], R1[:, :N], ALU.mult), reads=(pgr, R1r), writes=(t1r,))
            t2, t2r, _ = TMP.next()
            g.op("act", actf(t2[:, :N], t1[:, :N], AF.Silu), reads=(t1r,), writes=(t2r,))
            t3, t3r, _ = TMP.next()
            g.op("dve", tt(t3[:, :N], pu[:, :N], R1[:, :N], ALU.mult), reads=(pur, R1r), writes=(t3r,))
            g.op("dve", tt(HT[:, f, :N], t3[:, :N], t2[:, :N], ALU.mult), reads=(t3r, t2r), writes=(HTr[f],))
        nt = SLOT // 128
        for j in range(KC):
            xin, xr, xs = XIN.next()
            resid(j, xin, xr, xs)
            pd, pdr = PS.next()
            f0 = 0
            while f0 < FC:
                f1 = min(FC, f0 + nt)
                w, wr = wload(wd_[j][:, f0 * 128:f1 * 128], (f1 - f0) * 128)
                g.op("pe", [mm(pd[:, :N], w[:, (f - f0) * 128:(f - f0 + 1) * 128], HT[:, f, :N], f == 0, f == FC - 1)
                            for f in range(f0, f1)], reads=(wr, *HTr[f0:f1]), writes=(pdr,))
                f0 = f1
            h1, h1r, h1sem = H1.next()
            g.op("dve", stt(h1[:, :N], pd[:, :N], 0.5, xin[:, :N], ALU.mult, ALU.add), reads=(pdr, xr), writes=(h1r,))
            post(j, h1, h1r, h1sem)

    def sq_acc(st, src, srcr, N, ssb, ssr, first, last, which=0):
        acc, accr = st["ACC"][which], st["ACCr"][which]
        if first:
            g.op("act", actf(acc[:, :N], src, AF.Square), reads=(srcr,), writes=(accr,))
        else:
            sq, sqr, _ = st["SQ"].next()
            g.op("act", actf(sq[:, :N], src, AF.Square), reads=(srcr,), writes=(sqr,))
            g.op("dve", tt(acc[:, :N], acc[:, :N], sq[:, :N], ALU.add), reads=(sqr, accr), writes=(accr,))
        if last:
            g.op("pe", mm(ssb[:, :N], ones32[:], acc[:, :N], True, True), reads=(accr,), writes=(ssr,))

    def phaseA(st, segs, own):
        XG, XGr, PS, TMP, OUTB = st["XG"], st["XGr"], st["PS"], st["TMP"], st["OUTB"]
        R2, R2r = st["R2"], st["R2r"]
        cols = []
        c0_ = 0
        for (kind, s0, n, key0) in segs:
            cols.append((c0_, kind, s0, n, key0))
            c0_ += n
        N = c0_
        tok0 = segs[0][1]
        ss0, ss0r = ps_all[:, 0], PSr[0]
        ss1, ss1r = ps_all[:, 1], PSr[1]

        def srcf(cch, dst):
            return [dmaf(dst[:, a:a + n], (metaT[cch] if kind == "meta" else xT[cch][:, s0:s0 + n])) for (a, kind, s0, n, key0) in cols]

        def fm_fns(dram2d, ob, P=128):
            return [dmaf(dram2d[:, key0:key0 + n], ob[:P, a:a + n]) for (a, kind, s0, n, key0) in cols]

        def tm_fns(dram, ob, b, nt_, d0, d1, w):
            fns = []
            lo_b, hi_b = b * 128, b * 128 + nt_
            for (a, kind, s0, n, key0) in cols:
                lo, hi = max(lo_b, a), min(hi_b, a + n)
                if lo < hi:
                    fns.append(dmaf(dram[key0 + lo - a:key0 + hi - a, d0:d1], ob[lo - lo_b:hi - lo_b, 0:w]))
            return fns

        ROPE, ROPEr = st["ROPE"], st["ROPEr"]
        rf = []
        for (a, kind, s0, n, key0) in cols:
            rf.append(dmaf(ROPE[:, 0, a:a + n], ropeC[:, key0:key0 + n]))
            rf.append(dmaf(ROPE[:, 1, a:a + n], ropeS[:, key0:key0 + n]))
        g.dma("sp", st["ROPEs"], rf, writes=(ROPEr,))
        norm_stream(st, srcf, N, V_G1, ss0, ss0r)
        rstd(st, ss0, ss0r, N, D, st["R1"], st["R1r"])

        def resid(j, xin, xr, xs):
            g.dma("sp", xs, srcf(j, xin), writes=(xr,))

        def post(j, h1, h1r, h1sem):
            if own:
                g.dma("sp", h1sem, dmaf(h1s[j][:, tok0:tok0 + N], h1[:, :N]), reads=(h1r,))
            sq_acc(st, h1[:, :N], h1r, N, ss1, ss1r, j == 0, j == KC - 1, which=1)
            g.op("dve", ts(XG[:, j, :N], h1[:, :N], vec[:, V_GM + j:V_GM + j + 1], ALU.mult), reads=(h1r,), writes=(XGr[j],))

        ffn(st, "f1", N, resid, post)
        rstd(st, ss1, ss1r, N, D, R2, R2r)
        if own:
            bf = [dmaf(hgs[k0:min(KC, k0 + 8), :, tok0:tok0 + N].rearrange("c p t -> p c t"), XG[:, k0:min(KC, k0 + 8), :N])
                  for k0 in range(0, KC, 8)]
            bf.append(dmaf(r2s[:, tok0:tok0 + N], R2[:, :N]))
            g.dma("sp", st["bulks"], bf, reads=(R2r, *XGr))
        R2T, R2Tr = st["R2T"], st["R2Tr"]
        nb = (N + 127) // 128
        for b in range(nb):
            nt_ = min(128, N - b * 128)
            pt, ptr = PS.next()
            g.op("pe", lambda e, pt=pt, b=b, nt_=nt_: e.transpose(out=pt[:nt_, 0:128], in_=R2[:, b * 128:b * 128 + nt_], identity=id32[:]),
                 reads=(R2r,), writes=(ptr,))
            g.op("dve", cpy(R2T[:nt_, b:b + 1], pt[:nt_, 0:1]), reads=(ptr,), writes=(R2Tr,))

        def proj_fm(w, wr, off, kc, rhs, rhsr, M=128, col0=0):
            p, pr = PS.next()
            g.op("pe", [mm(p[:M, :N], w[:, off + k * 128 + col0:off + k * 128 + col0 + M], rhs[:, k, :N], k == 0, k == kc - 1)
                        for k in range(kc)], reads=(wr, *rhsr), writes=(pr,))
            return p, pr

        def store_bf(p, pr, dst, mul=None, mulr=None, P=128, eng="dve"):
            ob, obr, obs = OUTB.next()
            if mul is not None:
                g.op("dve", tt(ob[:P, :N], p[:P, :N], mul[:P, :N], ALU.mult), reads=(pr, mulr), writes=(obr,))
            elif eng == "act":
                g.op("act", actf(ob[:P, :N], p[:P, :N], AF.Copy), reads=(pr,), writes=(obr,))
            else:
                g.op("dve", cpy(ob[:P, :N], p[:P, :N]), reads=(pr,), writes=(obr,))
            g.dma("sp", obs, dst(ob) if callable(dst) else dmaf(dst, ob[:P, :N]), reads=(obr,))

        if own:
            for hc in range(2 * HD):
                w, wr = wload(W["wi_dq"][hc], KC * 128)
                p, pr = proj_fm(w, wr, 0, KC, XG, XGr)
                store_bf(p, pr, qdT[hc][:, tok0:tok0 + N], R2, R2r)
        for hc in range(2 * HD):
            w, wr = wload(W["wi_dk"][hc], KC * 128)
            p, pr = proj_fm(w, wr, 0, KC, XG, XGr)
            store_bf(p, pr, (lambda ob, hc=hc: fm_fns(kdT[hc], ob)), R2, R2r)
        CKV, CKVr, CKVN, CKVNr = st["CKV"], st["CKVr"], st["CKVN"], st["CKVNr"]
        for k in range(KVC):
            w, wr = wload(W["wi_ckv"][k], KC * 128)
            p, pr = proj_fm(w, wr, 0, KC, XG, XGr)
            g.op("dve", tt(CKV[:, k, :N], p[:, :N], R2[:, :N], ALU.mult), reads=(pr, R2r), writes=(CKVr[k],))
            sq_acc(st, CKV[:, k, :N], CKVr[k], N, ss0, ss0r, k == 0, k == KVC - 1, which=0)
        w, wr = wload(W["wi_kr"][0], KC * 128)
        pa, par = proj_fm(w, wr, 0, KC, XG, XGr, M=64, col0=0)
        pb, pbr = proj_fm(w, wr, 0, KC, XG, XGr, M=64, col0=64)
        ta, tar, _ = TMP.next()
        g.op("dve", tt(ta[:64, :N], pa[:64, :N], R2[:64, :N], ALU.mult), reads=(par, R2r), writes=(tar,))
        tb, tbr, _ = TMP.next()
        g.op("dve", tt(tb[:64, :N], pb[:64, :N], R2[:64, :N], ALU.mult), reads=(pbr, R2r), writes=(tbr,))
        tc_, tcr, _ = TMP.next()
        g.op("dve", tt(tc_[:64, :N], ta[:64, :N], ROPE[:, 0, :N], ALU.mult), reads=(tar, ROPEr), writes=(tcr,))
        td, tdr, _ = TMP.next()
        g.op("dve", tt(td[:64, :N], tb[:64, :N], ROPE[:, 1, :N], ALU.mult), reads=(tbr, ROPEr), writes=(tdr,))
        ob, obr, obs = OUTB.next()
        g.op("dve", tt(ob[:64, :N], tc_[:64, :N], td[:64, :N], ALU.add), reads=(tcr, tdr), writes=(obr,))
        g.dma("sp", obs, fm_fns(kpT, ob, P=64), reads=(obr,))
        for gi in range(NVG):
            banks = [PS.next() for _ in range(nb)]
            for cg in range(KC // CG):
                w, wr = wload(W["wi_dv"][gi * (KC // CG) + cg], CG * 512)
                fns = []
                for cc in range(CG):
                    k = cg * CG + cc
                    for b in range(nb):
                        nt_ = min(128, N - b * 128)
                        fns.append(mm(banks[b][0][:nt_, :512], XG[:, k, b * 128:b * 128 + nt_], w[:, cc * 512:(cc + 1) * 512],
                                      k == 0, k == KC - 1))
                g.op("pe", fns, reads=(wr, *XGr), writes=tuple(bk[1] for bk in banks))
            for b in range(nb):
                nt_ = min(128, N - b * 128)
                ob, obr, obs = OUTB.next()
                g.op("act", actf(ob[:nt_, :512], banks[b][0][:nt_, :512], AF.Copy, scale=R2T[:nt_, b:b + 1]),
                     reads=(banks[b][1], R2Tr), writes=(obr,))
                g.dma("sp", obs, tm_fns(vds, ob, b, nt_, gi * 512, (gi + 1) * 512, 512), reads=(obr,))
        rstd(st, ss0, ss0r, N, c["KVL"], st["RKV"], st["RKVr"])
        for k in range(KVC):
            g.op("dve", stt(CKVN[:, k, :N], CKV[:, k, :N], vec[:, V_GKV + k:V_GKV + k + 1], st["RKV"][:, :N], ALU.mult, ALU.mult),
                 reads=(CKVr[k], st["RKVr"]), writes=(CKVNr[k],))
        for hg_ in range(HM // HGK):
            w, wr = wload(W["ukv_n"][hg_], HGK * KVC * 128)
            for hh in range(HGK):
                h = hg_ * HGK + hh
                p, pr = proj_fm(w, wr, hh * KVC * 128, KVC, CKVN, CKVNr)
                store_bf(p, pr, (lambda ob, h=h: fm_fns(knT[h], ob)), eng=("act" if hh % 2 else "dve"))
        for gi in range(HM // MVG):
            banks = [PS.next() for _ in range(nb)]
            w, wr = wload(W["ukv_v"][gi], KVC * MVW)
            fns = []
            for k in range(KVC):
                for b in range(nb):
                    nt_ = min(128, N - b * 128)
                    fns.append(mm(banks[b][0][:nt_, :MVW], CKVN[:, k, b * 128:b * 128 + nt_], w[:, k * MVW:(k + 1) * MVW],
                                  k == 0, k == KVC - 1))
            g.op("pe", fns, reads=(wr, *CKVNr), writes=tuple(bk[1] for bk in banks))
            for b in range(nb):
                nt_ = min(128, N - b * 128)
                ob, obr, obs = OUTB.next()
                g.op("act" if b % 2 else "dve",
                     (actf(ob[:nt_, :MVW], banks[b][0][:nt_, :MVW], AF.Copy) if b % 2 else cpy(ob[:nt_, :MVW], banks[b][0][:nt_, :MVW])),
                     reads=(banks[b][1],), writes=(obr,))
                g.dma("sp", obs, tm_fns(mvs, ob, b, nt_, gi * MVW, (gi + 1) * MVW, MVW), reads=(obr,))
        if own:
            CQ, CQr, CQN, CQNr = st["CQ"], st["CQr"], st["CQN"], st["CQNr"]
            for k in range(QC):
                w, wr = wload(W["wi_cq"][k], KC * 128)
                p, pr = proj_fm(w, wr, 0, KC, XG, XGr)
                g.op("dve", tt(CQ[:, k, :N], p[:, :N], R2[:, :N], ALU.mult), reads=(pr, R2r), writes=(CQr[k],))
                sq_acc(st, CQ[:, k, :N], CQr[k], N, ss1, ss1r, k == 0, k == QC - 1, which=1)
            rstd(st, ss1, ss1r, N, c["QL"], st["RQ"], st["RQr"])
            for k in range(QC):
                g.op("dve", stt(CQN[:, k, :N], CQ[:, k, :N], vec[:, V_GQ + k:V_GQ + k + 1], st["RQ"][:, :N], ALU.mult, ALU.mult),
                     reads=(CQr[k], st["RQr"]), writes=(CQNr[k],))
            for hg_ in range(HM // HGQ):
                w, wr = wload(W["uq_n"][hg_], HGQ * QC * 128)
                for hh in range(HGQ):
                    h = hg_ * HGQ + hh
                    p, pr = proj_fm(w, wr, hh * QC * 128, QC, CQN, CQNr)
                    store_bf(p, pr, qnT[h][:, tok0:tok0 + N], eng=("act" if hh % 2 else "dve"))
            for hg_ in range(HM // HGQ):
                w, wr = wload(W["uq_p"][hg_], HGQ * QC * 128)
                for hh in range(HGQ):
                    h = hg_ * HGQ + hh
                    pa, par = proj_fm(w, wr, hh * QC * 128, QC, CQN, CQNr, M=64, col0=0)
                    pb, pbr = proj_fm(w, wr, hh * QC * 128, QC, CQN, CQNr, M=64, col0=64)
                    tc_, tcr, _ = TMP.next()
                    g.op("dve", tt(tc_[:64, :N], pa[:64, :N], ROPE[:, 0, :N], ALU.mult), reads=(par, ROPEr), writes=(tcr,))
                    td, tdr, _ = TMP.next()
                    g.op("dve", tt(td[:64, :N], pb[:64, :N], ROPE[:, 1, :N], ALU.mult), reads=(pbr, ROPEr), writes=(tdr,))
                    ob, obr, obs = OUTB.next()
                    g.op("dve", tt(ob[:64, :N], tc_[:64, :N], td[:64, :N], ALU.add), reads=(tcr, tdr), writes=(obr,))
                    g.dma("sp", obs, dmaf(qpT[h][:, tok0:tok0 + N], ob[:64, :N]), reads=(obr,))
        g.fence(("pe", "act", "dve"))

    def phaseB():
        m0 = ar.mark()
        MOWN = ar.alloc([128, c["WOWN"]], F32, "MOWN")
        MOTH = ar.alloc([128, c["WOTH"]], F32, "MOTH")
        KPE = ar.alloc([128, TKP], BF16, "KPE")
        tabr = Res()
        tabs = g.dsem("tabs")
        HB = []
        for i in range(2):
            hb = dict(KT=ar.alloc([128, 2, TKP], BF16, f"KT{i}"), V=ar.alloc([128, NKT, 256], BF16, f"V{i}"),
                      QT=ar.alloc([128, 2, TOWN], BF16, f"QT{i}"), QP=ar.alloc([128, TOWN], BF16, f"QP{i}"),
                      res=Res(), sem=g.dsem(f"hb{i}"))
            HB.append(hb)
        LOOK = 4
        E = Ring(g, ar, "E", [128, 512], BF16, 4)
        T = Ring(g, ar, "T", [128, 512], F32, 3)
        TMP = Ring(g, ar, "tmpB", [128, 512], F32, 6)
        SQ = Ring(g, ar, "sqB", [128, 512], F32, 2)
        OUTB = Ring(g, ar, "outbB", [128, 512], BF16, 3, sem=True)
        OC = ar.alloc([128, 2, 2, 512], F32, "OC")
        OCr = [[Res(), Res()], [Res(), Res()]]
        ODt = ar.alloc([128, 2, 512], F32, "ODt")
        ODr = Res()
        SB = PRing([0, 1, 2, 3])
        ZA = ar.alloc([128, 2, 2, 512], F32, "ZA")
        ZAr = [[Res(), Res()], [Res(), Res()]]

        g.op("dve", mset(KPE[:], 0.0), writes=(tabr,))
        for hb in HB:
            g.op("dve", [mset(hb["KT"][:], 0.0), mset(hb["QP"][:], 0.0)], writes=(hb["res"],))
            g.op("pool", mset(hb["V"][:, NKT - 1, :], 0.0), writes=(hb["res"],))
        g.dma("sp", tabs, [dmaf(MOWN[:], mown), dmaf(MOTH[:], moth), dmaf(KPE[0:64, 0:TK], kpT)], writes=(tabr,))

        NOT = TOWN // 128
        NJ = TOWN // 512

        def load_diff(h, hb):
            fns = []
            for cc in range(2):
                fns.append(dmaf(hb["KT"][:, cc, 0:TK], kdT[2 * h + cc]))
                fns.append(dmaf(hb["QT"][:, cc, :], qdT[2 * h + cc]))
            t0 = 0
            while t0 < NKT - 1:
                t1 = min(NKT - 1, t0 + 8)
                fns.append(dmaf(hb["V"][:, t0:t1, :],
                                vds[t0 * 128:t1 * 128, h * 256:(h + 1) * 256].rearrange("(t p) e -> p t e", p=128)))
                t0 = t1
            fns.append(dmaf(hb["V"][0:16, NKT - 1, :], vds[S:S + 16, h * 256:(h + 1) * 256]))
            g.dma("sp", hb["sem"], fns, writes=(hb["res"],))

        def load_mla(h, hb):
            fns = [dmaf(hb["KT"][:, 0, 0:TK], knT[h]), dmaf(hb["QT"][:, 0, :], qnT[h]), dmaf(hb["QP"][0:64, :], qpT[h])]
            t0 = 0
            while t0 < NKT - 1:
                t1 = min(NKT - 1, t0 + 8)
                fns.append(dmaf(hb["V"][:, t0:t1, 0:128],
                                mvs[t0 * 128:t1 * 128, h * 128:(h + 1) * 128].rearrange("(t p) e -> p t e", p=128)))
                t0 = t1
            fns.append(dmaf(hb["V"][0:16, NKT - 1, 0:128], mvs[S:S + 16, h * 128:(h + 1) * 128]))
            g.dma("sp", hb["sem"], fns, writes=(hb["res"],))

        slopes = alibi_slopes(HD)
        dscale = 128 ** -0.5
        mscale = 192 ** -0.5
        total_heads = HD + HM
        items = []
        deferred = []

        def recip_act(src_ap, src_res, scale_in=None, bias_in=None, power=-1.0):
            lz, lzr, _ = TMP.next()
            g.op("act", actf(lz[:], src_ap, AF.Ln, bias=bias_in, scale=scale_in), reads=(src_res,), writes=(lzr,))
            rz, rzr, _ = TMP.next()
            g.op("act", actf(rz[:], lz[:], AF.Exp, scale=power), reads=(lzr,), writes=(rzr,))
            return rz, rzr

        def mk_diff_epi(h, j, cc, Ob, Zb):
            def epi(pidx):
                g.op("dve", cpy(OC[:, cc, 0], ps_all[:, Ob[0]]), reads=(PSr[Ob[0]],), writes=(OCr[cc][0],))
                g.op("act", actf(OC[:, cc, 1], ps_all[:, Ob[1]], AF.Copy), reads=(PSr[Ob[1]],), writes=(OCr[cc][1],))
                zc, zcr, _ = TMP.next()
                g.op("dve", cpy(zc[:], ps_all[:, Zb]), reads=(PSr[Zb],), writes=(zcr,))

                def part1():
                    rz, rzr = recip_act(zc[:], zcr)
                    for x in range(2):
                        g.op("dve", tt(OC[:, cc, x], OC[:, cc, x], rz[:], ALU.mult), reads=(rzr, OCr[cc][x]), writes=(OCr[cc][x],))
                    if cc == 0:
                        return
                    g.op("dve", stt(ODt[:].rearrange("p a b -> p (a b)"), OC[:, 1].rearrange("p a b -> p (a b)"), nlam[:, 0:1],
                                    OC[:, 0].rearrange("p a b -> p (a b)"), ALU.mult, ALU.add),
                         reads=(OCr[0][0], OCr[0][1], OCr[1][0], OCr[1][1]), writes=(ODr,))
                    sqs = []
                    for x in range(2):
                        sq, sqr, _ = SQ.next()
                        g.op("act", actf(sq[:], ODt[:, x], AF.Square), reads=(ODr,), writes=(sqr,))
                        sqs.append((sq, sqr))

                    def part2():
                        ssb, ssr = SB.next()
                        for x in range(2):
                            g.op("pe", mm(ssb[:], ones32[:], sqs[x][0][:], x == 0, x == 1), reads=(sqs[x][1],), writes=(ssr,))
                        rd, rdr = recip_act(ssb[:], ssr, scale_in=1.0 / 256, bias_in=epsc[:, 0:1], power=-0.5)
                        for x in range(2):
                            ob, obr, obs = OUTB.next()
                            g.op("dve", stt(ob[:], ODt[:, x], sublnS[:, x:x + 1], rd[:], ALU.mult, ALU.mult), reads=(ODr, rdr), writes=(obr,))
                            g.dma("sp", obs, dmaf(odT[2 * h + x][:, j * 512:(j + 1) * 512], ob[:]), reads=(obr,))
                    deferred.append([pidx + 5, part2])
                deferred.append([pidx + 2, part1])
            return epi

        def mk_mla_epi(h, j, Ob, par):
            def epi(pidx):
                raw, rawr, _ = TMP.next()
                g.op("dve", cpy(raw[:], ps_all[:, Ob[0]]), reads=(PSr[Ob[0]],), writes=(rawr,))

                def part1():
                    zb, zbr = SB.next()
                    g.op("pe", [mm(zb[:], ones32[:], ZA[:, par, 0], True, False), mm(zb[:], ones32[:], ZA[:, par, 1], False, True)],
                         reads=(ZAr[par][0], ZAr[par][1]), writes=(zbr,))
                    rz, rzr = recip_act(zb[:], zbr)
                    ob, obr, obs = OUTB.next()
                    g.op("dve", tt(ob[:], raw[:], rz[:], ALU.mult), reads=(rawr, rzr), writes=(obr,))
                    g.dma("sp", obs, dmaf(omT[h][:, j * 512:(j + 1) * 512], ob[:]), reads=(obr,))
                deferred.append([pidx + 2, part1])
            return epi

        mla_blk = 0
        for hh in range(total_heads):
            hb = HB[hh % 2]
            first_of_head = True
            if hh < HD:
                h = hh
                ch = -slopes[h] / dscale
                for j in range(NJ):
                    for cc in range(2):
                        Ob, Zb = [4, 5], 6
                        for i in range(NKT):
                            if i == NKT - 1:
                                bias = None
                            elif i < NOT:
                                s0 = 512 * j - 128 * i + (TOWN - 128)
                                bias = MOWN[:, s0:s0 + 512]
                            else:
                                s0 = 512 * j - 128 * i + (S - 128)
                                bias = MOTH[:, s0:s0 + 512]
                            it = dict(hb=hb, hh=hh, i=i, bias=bias, ch=ch, scale=dscale, ne=2, Ob=Ob, Zb=Zb,
                                      kq=[(hb["KT"][:, cc, i * 128:(i + 1) * 128], hb["QT"][:, cc, j * 512:(j + 1) * 512])],
                                      epi=(mk_diff_epi(h, j, cc, Ob, Zb) if i == NKT - 1 else None), pre=first_of_head)
                            first_of_head = False
                            items.append(it)
            else:
                h = hh - HD
                for j in range(NJ):
                    Ob, par = [4 + (mla_blk % 4)], mla_blk % 2
                    mla_blk += 1
                    for i in range(NKT):
                        it = dict(hb=hb, hh=hh, i=i, bias=None, ch=0.0, scale=mscale, ne=1, Ob=Ob, Zb=None, zpar=par,
                                  kq=[(hb["KT"][:, 0, i * 128:(i + 1) * 128], hb["QT"][:, 0, j * 512:(j + 1) * 512]),
                                      (KPE[:, i * 128:(i + 1) * 128], hb["QP"][:, j * 512:(j + 1) * 512])],
                                  epi=(mk_mla_epi(h, j, Ob, par) if i == NKT - 1 else None), pre=first_of_head)
                        first_of_head = False
                        items.append(it)

        def issue_S(it):
            p, pr = SB.next()
            nk = len(it["kq"])
            g.op("pe", [mm(p[:], k_, q_, x == 0, x == nk - 1) for x, (k_, q_) in enumerate(it["kq"])],
                 reads=(it["hb"]["res"], tabr), writes=(pr,))
            it["S"] = (p, pr)

        def prefetch(hh):
            if hh >= total_heads:
                return
            if hh < HD:
                load_diff(hh, HB[hh % 2])
            else:
                load_mla(hh - HD, HB[hh % 2])

        prefetch(0)
        n_items = len(items)
        AHEAD = 2

        def stage1(it):
            p, pr = it["S"]
            if it["bias"] is not None:
                t, tr, _ = T.next()
                g.op("dve", stt(t[:], it["bias"], it["ch"], p[:], ALU.mult, ALU.add), reads=(pr, tabr), writes=(tr,))
                srcp, srcr = t, tr
            else:
                srcp, srcr = p, pr
            e_, er, _ = E.next()
            g.op("act", actf(e_[:], srcp[:], AF.Exp, scale=it["scale"]), reads=(srcr,), writes=(er,))
            it["E"] = (e_, er)
            if it["Zb"] is None:
                i = it["i"]
                par, half = it["zpar"], i % 2
                rows = 16 if i == NKT - 1 else 128
                if i < 2:
                    g.op("dve", cpy(ZA[:rows, par, half], e_[:rows]), reads=(er, ZAr[par][half]), writes=(ZAr[par][half],))
                else:
                    g.op("dve", tt(ZA[:rows, par, half], ZA[:rows, par, half], e_[:rows], ALU.add), reads=(er, ZAr[par][half]),
                         writes=(ZAr[par][half],))

        def run_deferred(pidx):
            k = 0
            while k < len(deferred):
                if deferred[k][0] <= pidx:
                    deferred.pop(k)[1]()
                    k = 0
                else:
                    k += 1

        for pidx in range(min(LOOK, n_items)):
            issue_S(items[pidx])
        for pidx in range(min(AHEAD, n_items)):
            stage1(items[pidx])
        for pidx in range(n_items):
            it = items[pidx]
            if it["pre"]:
                prefetch(it["hh"] + 1)
            hb = it["hb"]
            i = it["i"]
            e_, er = it["E"]
            first, last = (i == 0), (i == NKT - 1)
            fns = [mm(ps_all[:, it["Ob"][x]], hb["V"][:, i, x * 128:(x + 1) * 128], e_[:], first, last) for x in range(it["ne"])]
            wb = list(it["Ob"][:it["ne"]])
            if it["Zb"] is not None:
                fns.append(mm(ps_all[:, it["Zb"]], (onesmeta if last else onesbf)[:], e_[:], first, last))
                wb.append(it["Zb"])
            g.op("pe", fns, reads=(er, hb["res"]), writes=tuple(PSr[k] for k in wb))
            if pidx + LOOK < n_items:
                issue_S(items[pidx + LOOK])
            if it["epi"] is not None:
                it["epi"](pidx)
            if pidx + AHEAD < n_items:
                stage1(items[pidx + AHEAD])
            run_deferred(pidx)
        run_deferred(10 ** 9)
        ar.reset(m0)

    def phaseC(st, tok0):
        N = 512
        XG, XGr, PS, TMP, H1, XIN = st["XG"], st["XGr"], st["PS"], st["TMP"], st["H1"], st["XIN"]
        OD, OM, MG, MGr, ODr = st["OD"], st["OM"], st["MG"], st["MGr"], st["ODr"]
        R2, R2r = st["R2"], st["R2r"]
        ss0, ss0r = ps_all[:, 0], PSr[0]
        ss1, ss1r = ps_all[:, 1], PSr[1]
        hres = [Res() for _ in range(KC)]
        c5_prev = st["c5"]
        st["c5"] = []
        bf = []
        for k0 in range(0, 2 * HD, 8):
            k1 = min(2 * HD, k0 + 8)
            bf.append(dmaf(OD[:, k0:k1, :], odT[k0:k1, :, tok0:tok0 + N].rearrange("c p t -> p c t")))
        for k0 in range(0, HM, 8):
            k1 = min(HM, k0 + 8)
            bf.append(dmaf(OM[:, k0:k1, :], omT[k0:k1, :, tok0:tok0 + N].rearrange("c p t -> p c t")))
        for k0 in range(0, KC, 8):
            k1 = min(KC, k0 + 8)
            bf.append(dmaf(XG[:, k0:k1, :], hgs[k0:k1, :, tok0:tok0 + N].rearrange("c p t -> p c t")))
        bf.append(dmaf(R2[:], r2s[:, tok0:tok0 + N]))
        g.dma("sp", st["bulks"], bf, writes=(ODr, R2r, *XGr))
        for j in range(KC):
            wa, war = wload(W["wgt"][j], KC * 128)
            wb, wbr_ = wload(W["wgt"][KC + j], KC * 128)
            wc, wcr = wload(W["wbr"][j], NE * 128)
            pgd, pgdr = PS.next()
            g.op("pe", [mm(pgd[:], wa[:, k * 128:(k + 1) * 128], XG[:, k], k == 0, k == KC - 1) for k in range(KC)],
                 reads=(war, *XGr), writes=(pgdr,))
            pgm, pgmr = PS.next()
            g.op("pe", [mm(pgm[:], wb[:, k * 128:(k + 1) * 128], XG[:, k], k == 0, k == KC - 1) for k in range(KC)],
                 reads=(wbr_, *XGr), writes=(pgmr,))
            pbd, pbdr = PS.next()
            g.op("pe", [mm(pbd[:], wc[:, k * 128:(k + 1) * 128], OD[:, k], k == 0, k == 2 * HD - 1) for k in range(2 * HD)],
                 reads=(wcr, ODr), writes=(pbdr,))
            pbm, pbmr = PS.next()
            g.op("pe", [mm(pbm[:], wc[:, (2 * HD + k) * 128:(2 * HD + k + 1) * 128], OM[:, k], k == 0, k == HM - 1) for k in range(HM)],
                 reads=(wcr, ODr), writes=(pbmr,))
            ms = []
            for (pgx, pgxr, pbx, pbxr, bcol) in ((pgd, pgdr, pbd, pbdr, V_BG + j), (pgm, pgmr, pbm, pbmr, V_BG + KC + j)):
                t1, t1r, _ = TMP.next()
                g.op("dve", tt(t1[:], pgx[:], R2[:], ALU.mult), reads=(pgxr, R2r), writes=(t1r,))
                t2, t2r, _ = TMP.next()
                g.op("act", actf(t2[:], t1[:], AF.Sigmoid, bias=vec[:, bcol:bcol + 1]), reads=(t1r,), writes=(t2r,))
                t3, t3r, _ = TMP.next()
                g.op("dve", tt(t3[:], pbx[:], t2[:], ALU.mult), reads=(pbxr, t2r), writes=(t3r,))
                ms.append((t3, t3r))
            g.op("dve", tt(MG[:, j], ms[0][0][:], ms[1][0][:], ALU.add), reads=(ms[0][1], ms[1][1]), writes=(MGr[j],))
            if c5_prev:
                c5_prev.pop(0)()
        while c5_prev:
            c5_prev.pop(0)()
        for j in range(KC):
            w, wr = wload(W["wo"][j], KC * 128)
            xin, xr, xs = XIN.next()
            g.dma("sp", xs, dmaf(xin[:], h1s[j][:, tok0:tok0 + N]), reads=(hres[j],), writes=(xr,))
            p, pr = PS.next()
            g.op("pe", [mm(p[:], w[:, k * 128:(k + 1) * 128], MG[:, k], k == 0, k == KC - 1) for k in range(KC)],
                 reads=(wr, *MGr), writes=(pr,))
            h2, h2r, h2s = H1.next()
            g.op("dve", tt(h2[:], p[:], xin[:], ALU.add), reads=(pr, xr), writes=(h2r,))
            g.dma("sp", h2s, dmaf(h1s[j][:, tok0:tok0 + N], h2[:]), reads=(h2r,), writes=(hres[j],))
            sq_acc(st, h2[:], h2r, N, ss0, ss0r, j == 0, j == KC - 1, which=0)
            g.op("dve", ts(XG[:, j], h2[:], vec[:, V_G2 + j:V_G2 + j + 1], ALU.mult), reads=(h2r,), writes=(XGr[j],))
        rstd(st, ss0, ss0r, N, D, st["R1"], st["R1r"])
        g.fence(("pe", "act", "dve"))

        def resid(j, xin, xr, xs):
            g.dma("sp", xs, dmaf(xin[:], h1s[j][:, tok0:tok0 + N]), reads=(hres[j],), writes=(xr,))

        def post(j, h3, h3r, h3s):
            g.dma("sp", h3s, dmaf(h1s[j][:, tok0:tok0 + N], h3[:]), reads=(h3r,), writes=(hres[j],))
            sq_acc(st, h3[:], h3r, N, ss1, ss1r, j == 0, j == KC - 1, which=1)

        ffn(st, "f2", N, resid, post)
        rstd(st, ss1, ss1r, N, D, st["RF"], st["RFr"])
        g.fence(("pe", "act", "dve", "sp"))

        def c5_step(j):
            xin, xr, xs = XIN.next()
            g.dma("sp", xs, dmaf(xin[:], h1s[j][:, tok0:tok0 + N]), reads=(hres[j],), writes=(xr,))
            y, yr, ys = H1.next()
            g.op("dve", stt(y[:], xin[:], vec[:, V_GF + j:V_GF + j + 1], st["RF"][:], ALU.mult, ALU.mult), reads=(xr, st["RFr"]), writes=(yr,))
            g.dma("sp", ys, dmaf(yT[j][:, tok0:tok0 + N], y[:]), reads=(yr,))
        for j in range(KC):
            st["c5"].append(lambda j=j: c5_step(j))

    st = alloc_AC()
    for t in range(c["NTO"]):
        phaseA(st, [("x", t * 512, 512, t * 512)], True)
    rest = TOWN + 16
    nto = -(-rest // 512)
    base = -(-(-(-rest // nto)) // 32) * 32
    pos = TOWN
    for k in range(nto):
        n = base if k < nto - 1 else (S - pos)
        seg = [("x", pos, n, pos)]
        if k == nto - 1:
            seg.append(("meta", 0, 16, S))
        assert 0 < sum(x[2] for x in seg) <= 512
        phaseA(st, seg, False)
        pos += n
    g.fence()
    if c.get("STOP") != "A":
        ar.reset(persist_mark)
        phaseB()
        g.fence()
        if c.get("STOP") != "B":
            ar.reset(ac_mark)
            st = alloc_AC()
            for t in range(c["NTO"]):
                phaseC(st, t * 512)
            while st["c5"]:
                st["c5"].pop(0)()
            g.fence()

    with nc.Block() as block:
        g.emit(block)
    return nc


def lay_cols_fm(Wm, kc):
    n = Wm.shape[1] // 128
    return np.ascontiguousarray(Wm.reshape(kc, 128, n, 128).transpose(2, 1, 0, 3).reshape(n, 128, kc * 128))


def vec_cols(v):
    v = np.asarray(v, np.float32).reshape(-1)
    return v.reshape(-1, 128).T


def swap_halves(Wp):
    h = Wp.shape[-1] // 2
    return np.concatenate([Wp[..., h:], Wp[..., :h]], axis=-1)


def prep_shared(cfg, inp):
    c = cfg
    D, F, HD, HM, QL, KVL = c["D"], c["F"], c["HD"], c["HM"], c["QL"], c["KVL"]
    KC, FC, QC, KVC, CG, NVG, MVG, MVW, HGQ, HGK = (c[k] for k in ("KC", "FC", "QC", "KVC", "CG", "NVG", "MVG", "MVW", "HGQ", "HGK"))
    f32 = np.float32
    sh = {}
    for nm, pre in (("f1", "ffn1"), ("f2", "ffn2")):
        sh[nm + "g"] = lay_cols_fm(np.asarray(inp[pre + "_w_gate"][0], f32), KC)
        sh[nm + "u"] = lay_cols_fm(np.asarray(inp[pre + "_w_up"][0], f32), KC)
        sh[nm + "d"] = lay_cols_fm(np.asarray(inp[pre + "_w_down"][0], f32), FC)
    w_in = np.asarray(inp["w_in"][0], f32)
    QKW = HD * 256
    o_dq, o_dk, o_dv = 0, QKW, 2 * QKW
    o_cq = 3 * QKW
    o_ckv = o_cq + QL
    o_kr = o_ckv + KVL
    sh["wi_dq"] = lay_cols_fm(w_in[:, o_dq:o_dq + QKW], KC)
    sh["wi_dk"] = lay_cols_fm(w_in[:, o_dk:o_dk + QKW], KC)
    sh["wi_cq"] = lay_cols_fm(w_in[:, o_cq:o_cq + QL], KC)
    sh["wi_ckv"] = lay_cols_fm(w_in[:, o_ckv:o_ckv + KVL], KC)
    wkr = w_in[:, o_kr:o_kr + 64]
    sh["wi_kr"] = lay_cols_fm(np.concatenate([wkr, swap_halves(wkr)], axis=1), KC)
    wv = w_in[:, o_dv:o_dv + QKW]
    sh["wi_dv"] = np.ascontiguousarray(
        wv.reshape(KC // CG, CG, 128, NVG, 512).transpose(3, 0, 2, 1, 4).reshape(NVG * (KC // CG), 128, CG * 512))
    wuq = np.asarray(inp["mla_w_uq"][0], f32).reshape(QL, HM, 192)
    wn = np.ascontiguousarray(wuq[:, :, :128]).reshape(QL, HM * 128)
    pn = lay_cols_fm(wn, QC)
    sh["uq_n"] = np.ascontiguousarray(pn.reshape(HM // HGQ, HGQ, 128, QC * 128).transpose(0, 2, 1, 3).reshape(HM // HGQ, 128, HGQ * QC * 128))
    wp = wuq[:, :, 128:192]
    wpp = np.concatenate([wp, swap_halves(wp)], axis=2).reshape(QL, HM * 128)
    pp = lay_cols_fm(np.ascontiguousarray(wpp), QC)
    sh["uq_p"] = np.ascontiguousarray(pp.reshape(HM // HGQ, HGQ, 128, QC * 128).transpose(0, 2, 1, 3).reshape(HM // HGQ, 128, HGQ * QC * 128))
    wukv = np.asarray(inp["mla_w_ukv"][0], f32).reshape(KVL, HM, 256)
    wkn = np.ascontiguousarray(wukv[:, :, :128]).reshape(KVL, HM * 128)
    pk = lay_cols_fm(wkn, KVC)
    sh["ukv_n"] = np.ascontiguousarray(pk.reshape(HM // HGK, HGK, 128, KVC * 128).transpose(0, 2, 1, 3).reshape(HM // HGK, 128, HGK * KVC * 128))
    wvv = np.ascontiguousarray(wukv[:, :, 128:]).reshape(KVC, 128, HM // MVG, MVW)
    sh["ukv_v"] = np.ascontiguousarray(wvv.transpose(2, 1, 0, 3).reshape(HM // MVG, 128, KVC * MVW))
    sh["wgt"] = lay_cols_fm(np.asarray(inp["w_gate"][0], f32), KC)
    bd = lay_cols_fm(np.asarray(inp["w_branch_diff"][0], f32), 2 * HD)
    bm = lay_cols_fm(np.asarray(inp["w_branch_mla"][0], f32), HM)
    sh["wbr"] = np.ascontiguousarray(np.concatenate([bd, bm], axis=2))
    sh["wo"] = lay_cols_fm(np.asarray(inp["w_out"][0], f32), KC)
    cols = [vec_cols(inp["ffn1_norm"][0]), vec_cols(inp["mix_norm"][0]), vec_cols(inp["ffn2_norm"][0]), vec_cols(inp["final_norm"]),
            vec_cols(inp["mla_q_norm"][0]), vec_cols(inp["mla_kv_norm"][0]), vec_cols(inp["diff_subln"][0]), vec_cols(inp["b_gate"][0]),
            vec_cols(inp["diff_lambda_q1"][0]), vec_cols(inp["diff_lambda_k1"][0]), vec_cols(inp["diff_lambda_q2"][0]),
            vec_cols(inp["diff_lambda_k2"][0])]
    sh["vecs"] = np.ascontiguousarray(np.concatenate(cols, axis=1).astype(f32))
    assert sh["vecs"].shape[1] == c["NV"]
    sh["ident"] = np.eye(128, dtype=f32)
    return sh


def prep_core(cfg, inp, core):
    c = cfg
    S, TOWN, KC, TK = c["S"], c["TOWN"], c["KC"], c["TK"]
    f32 = np.float32
    b, half = core // 2, core % 2
    x = np.asarray(inp["x"][b], f32)
    order = np.concatenate([np.arange(half * TOWN, (half + 1) * TOWN), np.arange((1 - half) * TOWN, (2 - half) * TOWN)])
    pc = {}
    pc["xT"] = np.ascontiguousarray(x[order].T.reshape(KC, 128, S))
    pc["metaT"] = np.ascontiguousarray(np.asarray(inp["meta_tokens"], f32).T.reshape(KC, 128, 16))
    pos = np.concatenate([16 + order, np.arange(16)]).astype(f32)
    inv_freq = (1.0 / (ROPE_THETA ** (np.arange(0, 64, 2, dtype=f32) / f32(64)))).astype(f32)
    ang = (pos[None, :] * inv_freq[:, None]).astype(f32)
    cs, sn = np.cos(ang).astype(f32), np.sin(ang).astype(f32)
    pc["ropeC"] = np.ascontiguousarray(np.concatenate([cs, cs], axis=0))
    pc["ropeS"] = np.ascontiguousarray(np.concatenate([-sn, sn], axis=0))
    kk = np.arange(128, dtype=np.int64)[:, None]
    u = np.arange(c["WOWN"], dtype=np.int64)[None, :] - (TOWN - 128)
    pc["mown"] = np.abs(u - kk).astype(f32)
    u = np.arange(c["WOTH"], dtype=np.int64)[None, :] - (S - 128)
    pc["moth"] = ((kk - u) if half == 0 else (S + u - kk)).astype(f32)
    return pc


def run_cfg(cfg, inp, trace=False):
    nc = build(cfg)
    sh = prep_shared(cfg, inp)
    in_maps = []
    for core in range(NCORES):
        m = dict(sh)
        m.update(prep_core(cfg, inp, core))
        in_maps.append(m)
    res = run_bass_kernel_spmd(nc, in_maps, core_ids=list(range(NCORES)), trace=trace)
    S, TOWN, D = cfg["S"], cfg["TOWN"], cfg["D"]
    out = np.empty((cfg["B"], S, D), np.float32)
    for core in range(NCORES):
        b, half = core // 2, core % 2
        y = np.asarray(res.results[core]["yT"]).reshape(D, TOWN)
        out[b, half * TOWN:(half + 1) * TOWN, :] = y.T
    return out, res


def kernel(**inputs):
    cfg = make_cfg()
    out, _ = run_cfg(cfg, inputs)
    return out
```

```python
import math
import numpy as np
import concourse.bass as bass
import concourse.mybir as mybir
from concourse.bass_utils import run_bass_kernel_spmd

F32 = mybir.dt.float32
BF16 = mybir.dt.bfloat16
AF = mybir.ActivationFunctionType
ALU = mybir.AluOpType
EPS = 1e-6
LAM_INIT = 0.2
OUT_SCALE = 0.8
ROPE_THETA = 10000.0
NCORES = 8


def make_cfg(D=4096, F=11008, HD=8, HM=16, QL=1024, KVL=512, S=4096, B=4, SLOT=4096, NW=5):
    c = dict(D=D, F=F, HD=HD, HM=HM, QL=QL, KVL=KVL, S=S, B=B, NMETA=16, SLOT=SLOT, NW=NW)
    c["KC"] = D // 128
    c["FC"] = F // 128
    c["QC"] = QL // 128
    c["KVC"] = KVL // 128
    c["TOWN"] = S // 2
    c["NTO"] = c["TOWN"] // 512
    c["NTA"] = S // 512
    c["TK"] = S + 16
    c["NKT"] = S // 128 + 1
    c["TKP"] = c["NKT"] * 128
    c["CG"] = min(8, c["KC"])
    c["NVG"] = (HD * 256) // 512
    c["MVG"] = min(4, HM)
    c["MVW"] = c["MVG"] * 128
    c["HGQ"] = max(1, min(HM, SLOT // (c["QC"] * 128)))
    c["HGK"] = max(1, min(HM, SLOT // (c["KVC"] * 128)))
    c["WOWN"] = 2 * c["TOWN"] - 128
    c["WOTH"] = S - 128
    c["NV"] = 4 * c["KC"] + c["QC"] + c["KVC"] + 2 + 2 * c["KC"] + 4
    assert D % 128 == 0 and F % 128 == 0 and c["TOWN"] % 512 == 0 and HD % 2 == 0
    assert HM % c["HGQ"] == 0 and HM % c["HGK"] == 0 and HM % c["MVG"] == 0 and c["KC"] % c["CG"] == 0
    return c


class Res:
    __slots__ = ("w", "r")

    def __init__(self):
        self.w = None
        self.r = {}


class Gen:
    ENG = ("pe", "act", "dve", "pool", "sp")

    def __init__(self, nc):
        self.nc = nc
        self.q = {e: [] for e in self.ENG}
        self.prog = {e: nc.alloc_semaphore("pg_" + e) for e in ("pe", "act", "dve", "pool")}
        self.cnt = {e: 0 for e in self.prog}
        self.waited = {e: {} for e in self.ENG}
        self.semtot = {}
        self.semobj = {}

    def dsem(self, name):
        s = self.nc.alloc_semaphore(f"{name}_{len(self.semtot)}")
        self.semtot[s.num] = 0
        self.semobj[s.num] = s
        return s

    def _wait(self, e, deps):
        best = {}
        for d in deps:
            if d is None:
                continue
            sem, val = d
            if sem.num not in best or best[sem.num][1] < val:
                best[sem.num] = (sem, val)
        for sem, val in best.values():
            if e == "pe" and sem.num == self.prog["pe"].num:
                continue
            if self.waited[e].get(sem.num, 0) >= val:
                continue
            self.waited[e][sem.num] = val
            self.q[e].append(("w", sem, val))

    @staticmethod
    def _deps(reads, writes):
        deps = []
        for r in reads:
            if r.w is not None:
                deps.append(r.w)
        for w in writes:
            if w.w is not None:
                deps.append(w.w)
            deps.extend(w.r.values())
        return deps

    @staticmethod
    def _commit(tok, reads, writes):
        sem, val = tok
        for r in reads:
            r.r[sem.num] = tok
        for w in writes:
            w.w = tok
            w.r = {}

    def op(self, e, fns, reads=(), writes=()):
        if callable(fns):
            fns = [fns]
        self._wait(e, self._deps(reads, writes))
        self.cnt[e] += 1
        tok = (self.prog[e], self.cnt[e])
        for f in fns[:-1]:
            self.q[e].append(("i", f, None, 0))
        self.q[e].append(("i", fns[-1], self.prog[e], 1))
        self._commit(tok, reads, writes)
        return tok

    def dma(self, e, sem, fns, reads=(), writes=()):
        if callable(fns):
            fns = [fns]
        self._wait(e, self._deps(reads, writes))
        for f in fns:
            self.semtot[sem.num] += 16
            self.q[e].append(("i", f, sem, 16))
        tok = (sem, self.semtot[sem.num])
        self._commit(tok, reads, writes)
        return tok

    def fence(self, engines=ENG):
        toks = [(self.prog[e], self.cnt[e]) for e in self.prog if self.cnt[e] > 0]
        toks += [(self.semobj[n], t) for n, t in self.semtot.items() if t > 0]
        for e in engines:
            self._wait(e, toks)

    def emit(self, block):
        nc = self.nc

        def run(name, h):
            for it in self.q[name]:
                if it[0] == "w":
                    h.wait_ge(it[1], it[2])
                else:
                    ins = it[1](h)
                    if it[2] is not None:
                        ins.then_inc(it[2], it[3])

        @block.tensor
        def _(h):
            run("pe", h)

        @block.scalar
        def _(h):
            run("act", h)

        @block.vector
        def _(h):
            run("dve", h)

        @block.gpsimd
        def _(h):
            run("pool", h)

        @block.sync
        def _(h):
            run("sp", h)


class Arena:
    def __init__(self, nc):
        self.nc = nc
        self.base = (nc.sbuf_base + 63) // 64 * 64
        self.top = nc.sbuf_top
        self.off = self.base
        self.n = 0

    def alloc(self, shape, dtype, name=None):
        esz = 4 if dtype == F32 else 2
        size = esz
        for s in shape[1:]:
            size *= s
        size = (size + 63) // 64 * 64
        assert self.off + size <= self.top, f"SBUF arena overflow {self.off + size - self.top} bytes ({name})"
        self.n += 1
        t = self.nc.alloc_sbuf_tensor_at(f"{name or 'a'}_{self.n}", list(shape), dtype, offset=self.off)
        self.off += size
        return t

    def mark(self):
        return self.off

    def reset(self, m):
        self.off = m


class Ring:
    def __init__(self, g, arena, name, shape, dtype, n, sem=False):
        self.n = n
        self.t = arena.alloc([shape[0], n] + list(shape[1:]), dtype, name)
        self.res = [Res() for _ in range(n)]
        self.sems = [g.dsem(f"{name}_s{k}") for k in range(n)] if sem else None
        self.i = 0

    def next(self):
        k = self.i % self.n
        self.i += 1
        return self.t[:, k], self.res[k], (self.sems[k] if self.sems else None)


def mm(out, lhsT, rhs, start, stop):
    return lambda e: e.matmul(out, lhsT=lhsT, rhs=rhs, start=start, stop=stop)


def actf(out, in_, func, bias=None, scale=None):
    kw = {}
    if bias is not None:
        kw["bias"] = bias
    if scale is not None:
        kw["scale"] = scale
    return lambda e: e.activation(out=out, in_=in_, func=func, **kw)


def tt(out, in0, in1, op):
    return lambda e: e.tensor_tensor(out=out, in0=in0, in1=in1, op=op)


def ts(out, in0, s1, op0):
    return lambda e: e.tensor_scalar(out=out, in0=in0, scalar1=s1, scalar2=None, op0=op0)


def stt(out, in0, scalar, in1, op0, op1):
    return lambda e: e.scalar_tensor_tensor(out=out, in0=in0, scalar=scalar, in1=in1, op0=op0, op1=op1)


def recip(out, in_):
    return lambda e: e.reciprocal(out=out, in_=in_)


def cpy(out, in_):
    return lambda e: e.tensor_copy(out=out, in_=in_)


def mset(ap, v):
    return lambda e: e.memset(ap, v)


def dmaf(out, in_, **kw):
    return lambda e: e.dma_start(out=out, in_=in_, **kw)


def alibi_slopes(n):
    return [2.0 ** (-8.0 * (h + 1) / n) for h in range(n)]


def build(cfg):
    c = cfg
    D, F, HD, HM, S = c["D"], c["F"], c["HD"], c["HM"], c["S"]
    KC, FC, QC, KVC = c["KC"], c["FC"], c["QC"], c["KVC"]
    TOWN, TK, NKT, TKP = c["TOWN"], c["TK"], c["NKT"], c["TKP"]
    SLOT, NW, CG, NVG, MVG, MVW, HGQ, HGK = c["SLOT"], c["NW"], c["CG"], c["NVG"], c["MVG"], c["MVW"], c["HGQ"], c["HGK"]
    NV = c["NV"]
    NE = 2 * HD + HM

    nc = bass.Bass("TRN2", target_bir_lowering=False)

    def din(name, shape, dt=F32):
        return nc.dram_tensor(name, list(shape), dt, kind="ExternalInput").ap()

    def dscr(name, shape, dt):
        return nc.dram_tensor(name, list(shape), dt, kind=("ExternalOutput" if c.get("DEBUG") else "Internal")).ap()

    xT = din("xT", [KC, 128, S])
    metaT = din("metaT", [KC, 128, 16])
    ropeC = din("ropeC", [64, TK])
    ropeS = din("ropeS", [64, TK])
    mown = din("mown", [128, c["WOWN"]])
    moth = din("moth", [128, c["WOTH"]])
    vecs = din("vecs", [128, NV])
    ident = din("ident", [128, 128])
    W = {}
    for nm in ("f1", "f2"):
        W[nm + "g"] = din(nm + "g", [FC, 128, KC * 128])
        W[nm + "u"] = din(nm + "u", [FC, 128, KC * 128])
        W[nm + "d"] = din(nm + "d", [KC, 128, FC * 128])
    W["wi_dq"] = din("wi_dq", [2 * HD, 128, KC * 128])
    W["wi_dk"] = din("wi_dk", [2 * HD, 128, KC * 128])
    W["wi_cq"] = din("wi_cq", [QC, 128, KC * 128])
    W["wi_ckv"] = din("wi_ckv", [KVC, 128, KC * 128])
    W["wi_kr"] = din("wi_kr", [1, 128, KC * 128])
    W["wi_dv"] = din("wi_dv", [NVG * (KC // CG), 128, CG * 512])
    W["uq_n"] = din("uq_n", [HM // HGQ, 128, HGQ * QC * 128])
    W["uq_p"] = din("uq_p", [HM // HGQ, 128, HGQ * QC * 128])
    W["ukv_n"] = din("ukv_n", [HM // HGK, 128, HGK * KVC * 128])
    W["ukv_v"] = din("ukv_v", [HM // MVG, 128, KVC * MVW])
    W["wgt"] = din("wgt", [2 * KC, 128, KC * 128])
    W["wbr"] = din("wbr", [KC, 128, NE * 128])
    W["wo"] = din("wo", [KC, 128, KC * 128])
    yT = nc.dram_tensor("yT", [KC, 128, TOWN], F32, kind="ExternalOutput").ap()

    h1s = dscr("h1s", [KC, 128, TOWN], F32)
    hgs = dscr("hgs", [KC, 128, TOWN], BF16)
    r2s = dscr("r2s", [128, TOWN], F32)
    qdT = dscr("qdT", [2 * HD, 128, TOWN], BF16)
    kdT = dscr("kdT", [2 * HD, 128, TK], BF16)
    vds = dscr("vds", [TK, HD * 256], BF16)
    knT = dscr("knT", [HM, 128, TK], BF16)
    kpT = dscr("kpT", [64, TK], BF16)
    mvs = dscr("mvs", [TK, HM * 128], BF16)
    qnT = dscr("qnT", [HM, 128, TOWN], BF16)
    qpT = dscr("qpT", [HM, 64, TOWN], BF16)
    odT = dscr("odT", [2 * HD, 128, TOWN], BF16)
    omT = dscr("omT", [HM, 128, TOWN], BF16)

    WS = {"f1g": dscr("ws_g", [FC, 128, KC * 128], BF16), "f1u": dscr("ws_u", [FC, 128, KC * 128], BF16),
          "f1d": dscr("ws_d", [KC, 128, FC * 128], BF16)}

    g = Gen(nc)
    ar = Arena(nc)
    ps_all = nc.alloc_psum_tensor("ps", [128, 8, 512], F32)
    PSr = [Res() for _ in range(8)]

    class PRing:
        def __init__(self, banks):
            self.banks = banks
            self.i = 0

        def next(self):
            k = self.banks[self.i % len(self.banks)]
            self.i += 1
            return ps_all[:, k], PSr[k]

    ones32 = ar.alloc([128, 128], F32, "ones32")
    onesbf = ar.alloc([128, 128], BF16, "onesbf")
    onesmeta = ar.alloc([128, 128], BF16, "onesmeta")
    id32 = ar.alloc([128, 128], F32, "id32")
    vec = ar.alloc([128, NV], F32, "vec")
    epsc = ar.alloc([128, 1], F32, "epsc")
    nlam = ar.alloc([128, 1], F32, "nlam")
    sublnS = ar.alloc([128, 2], F32, "sublnS")
    lamt = ar.alloc([128, 4], F32, "lamt")
    csem = g.dsem("csem")
    cres = Res()
    o = 0
    V_G1 = o; o += KC
    V_GM = o; o += KC
    V_G2 = o; o += KC
    V_GF = o; o += KC
    V_GQ = o; o += QC
    V_GKV = o; o += KVC
    V_SUB = o; o += 2
    V_BG = o; o += 2 * KC
    V_LAM = o; o += 4
    assert o == NV

    g.dma("sp", csem, [dmaf(vec[:], vecs), dmaf(id32[:], ident)], writes=(cres,))
    omr, l0, l1, l2, l3 = Res(), Res(), Res(), Res(), Res()
    g.op("dve", [mset(ones32[:], 1.0), mset(onesbf[:], 1.0), mset(onesmeta[:], 0.0), mset(epsc[:], EPS)], writes=(omr,))
    g.op("dve", mset(onesmeta[0:16, :], 1.0), writes=(omr,))
    g.op("dve", tt(lamt[:, 0:1], vec[:, V_LAM:V_LAM + 1], vec[:, V_LAM + 1:V_LAM + 2], ALU.mult), reads=(cres,), writes=(l0,))
    g.op("dve", tt(lamt[:, 1:2], vec[:, V_LAM + 2:V_LAM + 3], vec[:, V_LAM + 3:V_LAM + 4], ALU.mult), reads=(cres,), writes=(l1,))
    g.op("dve", ts(sublnS[:], vec[:, V_SUB:V_SUB + 2], OUT_SCALE, ALU.mult), reads=(cres,), writes=(l3,))
    g.op("pe", mm(ps_all[:, 0, 0:2], ones32[:], lamt[:, 0:2], True, True), reads=(omr, l0, l1), writes=(PSr[0],))
    g.op("act", actf(lamt[:, 2:4], ps_all[:, 0, 0:2], AF.Exp), reads=(PSr[0],), writes=(l2,))
    g.op("dve", tt(lamt[:, 0:1], lamt[:, 3:4], lamt[:, 2:3], ALU.subtract), reads=(l2,), writes=(l0,))
    g.op("dve", lambda e: e.tensor_scalar(out=nlam[:], in0=lamt[:, 0:1], scalar1=-LAM_INIT, scalar2=None, op0=ALU.add),
         reads=(l0,), writes=(l3,))
    g.fence(("pe", "act", "dve", "sp", "pool"))

    persist_mark = ar.mark()
    wt = ar.alloc([128, NW, SLOT], BF16, "wring")
    wres = [Res() for _ in range(NW)]
    wsem = [g.dsem(f"w{k}") for k in range(NW)]
    wi = [0]

    wbsem = [g.dsem(f"wb{k}") for k in range(NW)]

    def wload(src, L, wb=None):
        k = wi[0] % NW
        wi[0] += 1
        dst = wt[:, k, 0:L]
        g.dma("pool", wsem[k], dmaf(dst, src, max_dma_last_dim=8192), writes=(wres[k],))
        if wb is not None:
            g.dma("sp", wbsem[k], dmaf(wb, dst), reads=(wres[k],))
        return dst, wres[k]

    def wb_barrier():
        g._wait("pool", [(wbsem[k], g.semtot[wbsem[k].num]) for k in range(NW) if g.semtot[wbsem[k].num] > 0])

    ac_mark = ar.mark()

    def alloc_AC():
        st = {}
        st["XG"] = ar.alloc([128, KC, 512], BF16, "XG")
        st["XGr"] = [Res() for _ in range(KC)]
        bigsz = max(FC * 512 * 2, (QC + KVC) * 512 * 6, (NE + KC) * 512 * 2)
        big0 = ar.mark()
        st["HT"] = ar.alloc([128, FC, 512], BF16, "HT")
        st["HTr"] = [Res() for _ in range(FC)]
        ar.reset(big0)
        st["CQ"] = ar.alloc([128, QC, 512], F32, "CQ")
        st["CKV"] = ar.alloc([128, KVC, 512], F32, "CKV")
        st["CQN"] = ar.alloc([128, QC, 512], BF16, "CQN")
        st["CKVN"] = ar.alloc([128, KVC, 512], BF16, "CKVN")
        st["CQr"] = [Res() for _ in range(QC)]
        st["CKVr"] = [Res() for _ in range(KVC)]
        st["CQNr"] = [Res() for _ in range(QC)]
        st["CKVNr"] = [Res() for _ in range(KVC)]
        ar.reset(big0)
        st["OD"] = ar.alloc([128, 2 * HD, 512], BF16, "OD")
        st["OM"] = ar.alloc([128, HM, 512], BF16, "OM")
        st["MG"] = ar.alloc([128, KC, 512], BF16, "MG")
        st["MGr"] = [Res() for _ in range(KC)]
        st["ODr"] = Res()
        ar.reset(big0 + (bigsz + 63) // 64 * 64)
        st["XIN"] = Ring(g, ar, "xin", [128, 512], F32, 3, sem=True)
        st["SQ"] = Ring(g, ar, "sq", [128, 512], F32, 2)
        st["ACC"] = [ar.alloc([128, 512], F32, "acc0"), ar.alloc([128, 512], F32, "acc1")]
        st["ACCr"] = [Res(), Res()]
        st["TMP"] = Ring(g, ar, "tmp", [128, 512], F32, 5)
        st["H1"] = Ring(g, ar, "h1", [128, 512], F32, 3, sem=True)
        st["OUTB"] = Ring(g, ar, "outb", [128, 512], BF16, 3, sem=True)
        st["R1"] = ar.alloc([128, 512], F32, "R1"); st["R1r"] = Res()
        st["R2"] = ar.alloc([128, 512], F32, "R2"); st["R2r"] = Res()
        st["RQ"] = ar.alloc([128, 512], F32, "RQ"); st["RQr"] = Res()
        st["RKV"] = ar.alloc([128, 512], F32, "RKV"); st["RKVr"] = Res()
        st["R2T"] = ar.alloc([128, 4], F32, "R2T"); st["R2Tr"] = Res()
        st["RF"] = st["RQ"]; st["RFr"] = st["RQr"]
        st["c5"] = []
        st["ROPE"] = ar.alloc([64, 2, 512], F32, "ROPE"); st["ROPEr"] = Res(); st["ROPEs"] = g.dsem("ropes")
        st["bulks"] = g.dsem("bulks")
        st["PS"] = PRing([2, 3, 4, 5, 6, 7])
        return st

    def norm_stream(st, srcf, N, gcol, ssb, ssr):
        XG, XGr, XIN, SQ = st["XG"], st["XGr"], st["XIN"], st["SQ"]
        for cch in range(KC):
            xin, xr, xs = XIN.next()
            g.dma("sp", xs, srcf(cch, xin), writes=(xr,))
            sq, sqr, _ = SQ.next()
            g.op("act", actf(sq[:, :N], xin[:, :N], AF.Square), reads=(xr,), writes=(sqr,))
            g.op("pe", mm(ssb[:, :N], ones32[:], sq[:, :N], cch == 0, cch == KC - 1), reads=(sqr,), writes=(ssr,))
            g.op("dve", ts(XG[:, cch, :N], xin[:, :N], vec[:, gcol + cch:gcol + cch + 1], ALU.mult),
                 reads=(xr,), writes=(XGr[cch],))

    def rstd(st, ssb, ssr, N, dim, out, outr, P=128):
        tmp, tr, _ = st["TMP"].next()
        g.op("act", actf(tmp[:P, :N], ssb[:P, :N], AF.Ln, bias=epsc[:P, 0:1], scale=1.0 / dim), reads=(ssr,), writes=(tr,))
        g.op("act", actf(out[:P, :N], tmp[:P, :N], AF.Exp, scale=-0.5), reads=(tr,), writes=(outr,))

    def ffn(st, nm, N, resid, post, wmode="f32"):
        XG, XGr, HT, HTr, PS, TMP, H1, XIN = st["XG"], st["XGr"], st["HT"], st["HTr"], st["PS"], st["TMP"], st["H1"], st["XIN"]
        R1, R1r = st["R1"], st["R1r"]
        wg_, wu_, wd_ = W[nm + "g"], W[nm + "u"], W[nm + "d"]
        if wmode == "bf":
            wg_, wu_, wd_ = WS[nm + "g"], WS[nm + "u"], WS[nm + "d"]
        wbk = (wmode == "wb")
        for f in range(FC):
            wg, wgr = wload(wg_[f], KC * 128, wb=(WS[nm + "g"][f] if wbk else None))
            wu, wur = wload(wu_[f], KC * 128, wb=(WS[nm + "u"][f] if wbk else None))
            pg, pgr = PS.next()
            pu, pur = PS.next()
            g.op("pe", [mm(pg[:, :N], wg[:, k * 128:(k + 1) * 128], XG[:, k, :N], k == 0, k == KC - 1) for k in range(KC)],
                 reads=(wgr, *XGr), writes=(pgr,))
            g.op("pe", [mm(pu[:, :N], wu[:, k * 128:(k + 1) * 128], XG[:, k, :N], k == 0, k == KC - 1) for k in range(KC)],
                 reads=(wur, *XGr), writes=(pur,))
            t1, t1r, _ = TMP.next()
            g.op("dve", tt(t1[:, :N], pg[:, :N], R1[:, :N], ALU.mult), reads=(pgr, R1r), writes=(t1r,))
            t2, t2r, _ = TMP.next()
            g.op("act", actf(t2[:, :N], t1[:, :N], AF.Silu), reads=(t1r,), writes=(t2r,))
            t3, t3r, _ = TMP.next()
            g.op("dve", tt(t3[:, :N], pu[:, :N], R1[:, :N], ALU.mult), reads=(pur, R1r), writes=(t3r,))
            g.op("dve", tt(HT[:, f, :N], t3[:, :N], t2[:, :N], ALU.mult), reads=(t3r, t2r), writes=(HTr[f],))
        nt = SLOT // 128
        for j in range(KC):
            xin, xr, xs = XIN.next()
            resid(j, xin, xr, xs)
            pd, pdr = PS.next()
            f0 = 0
            while f0 < FC:
                f1 = min(FC, f0 + nt)
                w, wr = wload(wd_[j][:, f0 * 128:f1 * 128], (f1 - f0) * 128,
                              wb=(WS[nm + "d"][j][:, f0 * 128:f1 * 128] if wbk else None))
                g.op("pe", [mm(pd[:, :N], w[:, (f - f0) * 128:(f - f0 + 1) * 128], HT[:, f, :N], f == 0, f == FC - 1)
                            for f in range(f0, f1)], reads=(wr, *HTr[f0:f1]), writes=(pdr,))
                f0 = f1
            h1, h1r, h1sem = H1.next()
            g.op("dve", stt(h1[:, :N], pd[:, :N], 0.5, xin[:, :N], ALU.mult, ALU.add), reads=(pdr, xr), writes=(h1r,))
            post(j, h1, h1r, h1sem)

    def sq_acc(st, src, srcr, N, ssb, ssr, first, last, which=0):
        acc, accr = st["ACC"][which], st["ACCr"][which]
        if first:
            g.op("act", actf(acc[:, :N], src, AF.Square), reads=(srcr,), writes=(accr,))
        else:
            sq, sqr, _ = st["SQ"].next()
            g.op("act", actf(sq[:, :N], src, AF.Square), reads=(srcr,), writes=(sqr,))
            g.op("dve", tt(acc[:, :N], acc[:, :N], sq[:, :N], ALU.add), reads=(sqr, accr), writes=(accr,))
        if last:
            g.op("pe", mm(ssb[:, :N], ones32[:], acc[:, :N], True, True), reads=(accr,), writes=(ssr,))

    def phaseA(st, segs, own, wmode="f32"):
        XG, XGr, PS, TMP, OUTB = st["XG"], st["XGr"], st["PS"], st["TMP"], st["OUTB"]
        R2, R2r = st["R2"], st["R2r"]
        cols = []
        c0_ = 0
        for (kind, s0, n, key0) in segs:
            cols.append((c0_, kind, s0, n, key0))
            c0_ += n
        N = c0_
        tok0 = segs[0][1]
        ss0, ss0r = ps_all[:, 0], PSr[0]
        ss1, ss1r = ps_all[:, 1], PSr[1]

        def srcf(cch, dst):
            return [dmaf(dst[:, a:a + n], (metaT[cch] if kind == "meta" else xT[cch][:, s0:s0 + n])) for (a, kind, s0, n, key0) in cols]

        def fm_fns(dram2d, ob, P=128):
            return [dmaf(dram2d[:, key0:key0 + n], ob[:P, a:a + n]) for (a, kind, s0, n, key0) in cols]

        def tm_fns(dram, ob, b, nt_, d0, d1, w):
            fns = []
            lo_b, hi_b = b * 128, b * 128 + nt_
            for (a, kind, s0, n, key0) in cols:
                lo, hi = max(lo_b, a), min(hi_b, a + n)
                if lo < hi:
                    fns.append(dmaf(dram[key0 + lo - a:key0 + hi - a, d0:d1], ob[lo - lo_b:hi - lo_b, 0:w]))
            return fns

        ROPE, ROPEr = st["ROPE"], st["ROPEr"]
        rf = []
        for (a, kind, s0, n, key0) in cols:
            rf.append(dmaf(ROPE[:, 0, a:a + n], ropeC[:, key0:key0 + n]))
            rf.append(dmaf(ROPE[:, 1, a:a + n], ropeS[:, key0:key0 + n]))
        g.dma("sp", st["ROPEs"], rf, writes=(ROPEr,))
        norm_stream(st, srcf, N, V_G1, ss0, ss0r)
        rstd(st, ss0, ss0r, N, D, st["R1"], st["R1r"])

        def resid(j, xin, xr, xs):
            g.dma("sp", xs, srcf(j, xin), writes=(xr,))

        def post(j, h1, h1r, h1sem):
            if own:
                g.dma("sp", h1sem, dmaf(h1s[j][:, tok0:tok0 + N], h1[:, :N]), reads=(h1r,))
            sq_acc(st, h1[:, :N], h1r, N, ss1, ss1r, j == 0, j == KC - 1, which=1)
            g.op("dve", ts(XG[:, j, :N], h1[:, :N], vec[:, V_GM + j:V_GM + j + 1], ALU.mult), reads=(h1r,), writes=(XGr[j],))

        ffn(st, "f1", N, resid, post, wmode=wmode)
        rstd(st, ss1, ss1r, N, D, R2, R2r)
        if own:
            bf = [dmaf(hgs[k0:min(KC, k0 + 8), :, tok0:tok0 + N].rearrange("c p t -> p c t"), XG[:, k0:min(KC, k0 + 8), :N])
                  for k0 in range(0, KC, 8)]
            bf.append(dmaf(r2s[:, tok0:tok0 + N], R2[:, :N]))
            g.dma("sp", st["bulks"], bf, reads=(R2r, *XGr))
        R2T, R2Tr = st["R2T"], st["R2Tr"]
        nb = (N + 127) // 128
        for b in range(nb):
            nt_ = min(128, N - b * 128)
            pt, ptr = PS.next()
            g.op("pe", lambda e, pt=pt, b=b, nt_=nt_: e.transpose(out=pt[:nt_, 0:128], in_=R2[:, b * 128:b * 128 + nt_], identity=id32[:]),
                 reads=(R2r,), writes=(ptr,))
            g.op("dve", cpy(R2T[:nt_, b:b + 1], pt[:nt_, 0:1]), reads=(ptr,), writes=(R2Tr,))

        def proj_fm(w, wr, off, kc, rhs, rhsr, M=128, col0=0):
            p, pr = PS.next()
            g.op("pe", [mm(p[:M, :N], w[:, off + k * 128 + col0:off + k * 128 + col0 + M], rhs[:, k, :N], k == 0, k == kc - 1)
                        for k in range(kc)], reads=(wr, *rhsr), writes=(pr,))
            return p, pr

        def store_bf(p, pr, dst, mul=None, mulr=None, P=128, eng="dve"):
            ob, obr, obs = OUTB.next()
            if mul is not None:
                g.op("dve", tt(ob[:P, :N], p[:P, :N], mul[:P, :N], ALU.mult), reads=(pr, mulr), writes=(obr,))
            elif eng == "act":
                g.op("act", actf(ob[:P, :N], p[:P, :N], AF.Copy), reads=(pr,), writes=(obr,))
            else:
                g.op("dve", cpy(ob[:P, :N], p[:P, :N]), reads=(pr,), writes=(obr,))
            g.dma("sp", obs, dst(ob) if callable(dst) else dmaf(dst, ob[:P, :N]), reads=(obr,))

        if own:
            for hc in range(2 * HD):
                w, wr = wload(W["wi_dq"][hc], KC * 128)
                p, pr = proj_fm(w, wr, 0, KC, XG, XGr)
                store_bf(p, pr, qdT[hc][:, tok0:tok0 + N], R2, R2r)
        for hc in range(2 * HD):
            w, wr = wload(W["wi_dk"][hc], KC * 128)
            p, pr = proj_fm(w, wr, 0, KC, XG, XGr)
            store_bf(p, pr, (lambda ob, hc=hc: fm_fns(kdT[hc], ob)), R2, R2r)
        CKV, CKVr, CKVN, CKVNr = st["CKV"], st["CKVr"], st["CKVN"], st["CKVNr"]
        for k in range(KVC):
            w, wr = wload(W["wi_ckv"][k], KC * 128)
            p, pr = proj_fm(w, wr, 0, KC, XG, XGr)
            g.op("dve", tt(CKV[:, k, :N], p[:, :N], R2[:, :N], ALU.mult), reads=(pr, R2r), writes=(CKVr[k],))
            sq_acc(st, CKV[:, k, :N], CKVr[k], N, ss0, ss0r, k == 0, k == KVC - 1, which=0)
        w, wr = wload(W["wi_kr"][0], KC * 128)
        pa, par = proj_fm(w, wr, 0, KC, XG, XGr, M=64, col0=0)
        pb, pbr = proj_fm(w, wr, 0, KC, XG, XGr, M=64, col0=64)
        ta, tar, _ = TMP.next()
        g.op("dve", tt(ta[:64, :N], pa[:64, :N], R2[:64, :N], ALU.mult), reads=(par, R2r), writes=(tar,))
        tb, tbr, _ = TMP.next()
        g.op("dve", tt(tb[:64, :N], pb[:64, :N], R2[:64, :N], ALU.mult), reads=(pbr, R2r), writes=(tbr,))
        tc_, tcr, _ = TMP.next()
        g.op("dve", tt(tc_[:64, :N], ta[:64, :N], ROPE[:, 0, :N], ALU.mult), reads=(tar, ROPEr), writes=(tcr,))
        td, tdr, _ = TMP.next()
        g.op("dve", tt(td[:64, :N], tb[:64, :N], ROPE[:, 1, :N], ALU.mult), reads=(tbr, ROPEr), writes=(tdr,))
        ob, obr, obs = OUTB.next()
        g.op("dve", tt(ob[:64, :N], tc_[:64, :N], td[:64, :N], ALU.add), reads=(tcr, tdr), writes=(obr,))
        g.dma("sp", obs, fm_fns(kpT, ob, P=64), reads=(obr,))
        for gi in range(NVG):
            banks = [PS.next() for _ in range(nb)]
            for cg in range(KC // CG):
                w, wr = wload(W["wi_dv"][gi * (KC // CG) + cg], CG * 512)
                fns = []
                for cc in range(CG):
                    k = cg * CG + cc
                    for b in range(nb):
                        nt_ = min(128, N - b * 128)
                        fns.append(mm(banks[b][0][:nt_, :512], XG[:, k, b * 128:b * 128 + nt_], w[:, cc * 512:(cc + 1) * 512],
                                      k == 0, k == KC - 1))
                g.op("pe", fns, reads=(wr, *XGr), writes=tuple(bk[1] for bk in banks))
            for b in range(nb):
                nt_ = min(128, N - b * 128)
                ob, obr, obs = OUTB.next()
                g.op("act", actf(ob[:nt_, :512], banks[b][0][:nt_, :512], AF.Copy, scale=R2T[:nt_, b:b + 1]),
                     reads=(banks[b][1], R2Tr), writes=(obr,))
                g.dma("sp", obs, tm_fns(vds, ob, b, nt_, gi * 512, (gi + 1) * 512, 512), reads=(obr,))
        rstd(st, ss0, ss0r, N, c["KVL"], st["RKV"], st["RKVr"])
        for k in range(KVC):
            g.op("dve", stt(CKVN[:, k, :N], CKV[:, k, :N], vec[:, V_GKV + k:V_GKV + k + 1], st["RKV"][:, :N], ALU.mult, ALU.mult),
                 reads=(CKVr[k], st["RKVr"]), writes=(CKVNr[k],))
        for hg_ in range(HM // HGK):
            w, wr = wload(W["ukv_n"][hg_], HGK * KVC * 128)
            for hh in range(HGK):
                h = hg_ * HGK + hh
                p, pr = proj_fm(w, wr, hh * KVC * 128, KVC, CKVN, CKVNr)
                store_bf(p, pr, (lambda ob, h=h: fm_fns(knT[h], ob)), eng=("act" if hh % 2 else "dve"))
        for gi in range(HM // MVG):
            banks = [PS.next() for _ in range(nb)]
            w, wr = wload(W["ukv_v"][gi], KVC * MVW)
            fns = []
            for k in range(KVC):
                for b in range(nb):
                    nt_ = min(128, N - b * 128)
                    fns.append(mm(banks[b][0][:nt_, :MVW], CKVN[:, k, b * 128:b * 128 + nt_], w[:, k * MVW:(k + 1) * MVW],
                                  k == 0, k == KVC - 1))
            g.op("pe", fns, reads=(wr, *CKVNr), writes=tuple(bk[1] for bk in banks))
            for b in range(nb):
                nt_ = min(128, N - b * 128)
                ob, obr, obs = OUTB.next()
                g.op("act" if b % 2 else "dve",
                     (actf(ob[:nt_, :MVW], banks[b][0][:nt_, :MVW], AF.Copy) if b % 2 else cpy(ob[:nt_, :MVW], banks[b][0][:nt_, :MVW])),
                     reads=(banks[b][1],), writes=(obr,))
                g.dma("sp", obs, tm_fns(mvs, ob, b, nt_, gi * MVW, (gi + 1) * MVW, MVW), reads=(obr,))
        if own:
            CQ, CQr, CQN, CQNr = st["CQ"], st["CQr"], st["CQN"], st["CQNr"]
            for k in range(QC):
                w, wr = wload(W["wi_cq"][k], KC * 128)
                p, pr = proj_fm(w, wr, 0, KC, XG, XGr)
                g.op("dve", tt(CQ[:, k, :N], p[:, :N], R2[:, :N], ALU.mult), reads=(pr, R2r), writes=(CQr[k],))
                sq_acc(st, CQ[:, k, :N], CQr[k], N, ss1, ss1r, k == 0, k == QC - 1, which=1)
            rstd(st, ss1, ss1r, N, c["QL"], st["RQ"], st["RQr"])
            for k in range(QC):
                g.op("dve", stt(CQN[:, k, :N], CQ[:, k, :N], vec[:, V_GQ + k:V_GQ + k + 1], st["RQ"][:, :N], ALU.mult, ALU.mult),
                     reads=(CQr[k], st["RQr"]), writes=(CQNr[k],))
            for hg_ in range(HM // HGQ):
                w, wr = wload(W["uq_n"][hg_], HGQ * QC * 128)
                for hh in range(HGQ):
                    h = hg_ * HGQ + hh
                    p, pr = proj_fm(w, wr, hh * QC * 128, QC, CQN, CQNr)
                    store_bf(p, pr, qnT[h][:, tok0:tok0 + N], eng=("act" if hh % 2 else "dve"))
            for hg_ in range(HM // HGQ):
                w, wr = wload(W["uq_p"][hg_], HGQ * QC * 128)
                for hh in range(HGQ):
                    h = hg_ * HGQ + hh
                    pa, par = proj_fm(w, wr, hh * QC * 128, QC, CQN, CQNr, M=64, col0=0)
                    pb, pbr = proj_fm(w, wr, hh * QC * 128, QC, CQN, CQNr, M=64, col0=64)
                    tc_, tcr, _ = TMP.next()
                    g.op("dve", tt(tc_[:64, :N], pa[:64, :N], ROPE[:, 0, :N], ALU.mult), reads=(par, ROPEr), writes=(tcr,))
                    td, tdr, _ = TMP.next()
                    g.op("dve", tt(td[:64, :N], pb[:64, :N], ROPE[:, 1, :N], ALU.mult), reads=(pbr, ROPEr), writes=(tdr,))
                    ob, obr, obs = OUTB.next()
                    g.op("dve", tt(ob[:64, :N], tc_[:64, :N], td[:64, :N], ALU.add), reads=(tcr, tdr), writes=(obr,))
                    g.dma("sp", obs, dmaf(qpT[h][:, tok0:tok0 + N], ob[:64, :N]), reads=(obr,))
        g.fence(("pe", "act", "dve"))

    def phaseB():
        m0 = ar.mark()
        MOWN = ar.alloc([128, c["WOWN"]], F32, "MOWN")
        MOTH = ar.alloc([128, c["WOTH"]], F32, "MOTH")
        KPE = ar.alloc([128, TKP], BF16, "KPE")
        tabr = Res()
        tabs = g.dsem("tabs")
        HB = []
        for i in range(2):
            hb = dict(KT=ar.alloc([128, 2, TKP], BF16, f"KT{i}"), V=ar.alloc([128, NKT, 256], BF16, f"V{i}"),
                      QT=ar.alloc([128, 2, TOWN], BF16, f"QT{i}"), QP=ar.alloc([128, TOWN], BF16, f"QP{i}"),
                      res=Res(), sem=g.dsem(f"hb{i}"))
            HB.append(hb)
        LOOK = 4
        E = Ring(g, ar, "E", [128, 512], BF16, 4)
        T = Ring(g, ar, "T", [128, 512], F32, 3)
        TMP = Ring(g, ar, "tmpB", [128, 512], F32, 6)
        SQ = Ring(g, ar, "sqB", [128, 512], F32, 2)
        OUTB = Ring(g, ar, "outbB", [128, 512], BF16, 3, sem=True)
        OC = ar.alloc([128, 2, 2, 512], F32, "OC")
        OCr = [[Res(), Res()], [Res(), Res()]]
        ODt = ar.alloc([128, 2, 512], F32, "ODt")
        ODr = Res()
        SB = PRing([0, 1, 2, 3])
        ZA = ar.alloc([128, 2, 2, 512], F32, "ZA")
        ZAr = [[Res(), Res()], [Res(), Res()]]

        g.op("dve", mset(KPE[:], 0.0), writes=(tabr,))
        for hb in HB:
            g.op("dve", [mset(hb["KT"][:], 0.0), mset(hb["QP"][:], 0.0)], writes=(hb["res"],))
            g.op("pool", mset(hb["V"][:, NKT - 1, :], 0.0), writes=(hb["res"],))
        g.dma("sp", tabs, [dmaf(MOWN[:], mown), dmaf(MOTH[:], moth), dmaf(KPE[0:64, 0:TK], kpT)], writes=(tabr,))

        NOT = TOWN // 128
        NJ = TOWN // 512

        def load_diff(h, hb):
            fns = []
            for cc in range(2):
                fns.append(dmaf(hb["KT"][:, cc, 0:TK], kdT[2 * h + cc]))
                fns.append(dmaf(hb["QT"][:, cc, :], qdT[2 * h + cc]))
            t0 = 0
            while t0 < NKT - 1:
                t1 = min(NKT - 1, t0 + 8)
                fns.append(dmaf(hb["V"][:, t0:t1, :],
                                vds[t0 * 128:t1 * 128, h * 256:(h + 1) * 256].rearrange("(t p) e -> p t e", p=128)))
                t0 = t1
            fns.append(dmaf(hb["V"][0:16, NKT - 1, :], vds[S:S + 16, h * 256:(h + 1) * 256]))
            g.dma("sp", hb["sem"], fns, writes=(hb["res"],))

        def load_mla(h, hb):
            fns = [dmaf(hb["KT"][:, 0, 0:TK], knT[h]), dmaf(hb["QT"][:, 0, :], qnT[h]), dmaf(hb["QP"][0:64, :], qpT[h])]
            t0 = 0
            while t0 < NKT - 1:
                t1 = min(NKT - 1, t0 + 8)
                fns.append(dmaf(hb["V"][:, t0:t1, 0:128],
                                mvs[t0 * 128:t1 * 128, h * 128:(h + 1) * 128].rearrange("(t p) e -> p t e", p=128)))
                t0 = t1
            fns.append(dmaf(hb["V"][0:16, NKT - 1, 0:128], mvs[S:S + 16, h * 128:(h + 1) * 128]))
            g.dma("sp", hb["sem"], fns, writes=(hb["res"],))

        slopes = alibi_slopes(HD)
        dscale = 128 ** -0.5
        mscale = 192 ** -0.5
        total_heads = HD + HM
        items = []
        deferred = []

        def recip_act(src_ap, src_res, scale_in=None, bias_in=None, power=-1.0):
            lz, lzr, _ = TMP.next()
            g.op("act", actf(lz[:], src_ap, AF.Ln, bias=bias_in, scale=scale_in), reads=(src_res,), writes=(lzr,))
            rz, rzr, _ = TMP.next()
            g.op("act", actf(rz[:], lz[:], AF.Exp, scale=power), reads=(lzr,), writes=(rzr,))
            return rz, rzr

        def mk_diff_epi(h, j, cc, Ob, Zb):
            def epi(pidx):
                g.op("dve", cpy(OC[:, cc, 0], ps_all[:, Ob[0]]), reads=(PSr[Ob[0]],), writes=(OCr[cc][0],))
                g.op("act", actf(OC[:, cc, 1], ps_all[:, Ob[1]], AF.Copy), reads=(PSr[Ob[1]],), writes=(OCr[cc][1],))
                zc, zcr, _ = TMP.next()
                g.op("dve", cpy(zc[:], ps_all[:, Zb]), reads=(PSr[Zb],), writes=(zcr,))

                def part1():
                    rz, rzr = recip_act(zc[:], zcr)
                    for x in range(2):
                        g.op("dve", tt(OC[:, cc, x], OC[:, cc, x], rz[:], ALU.mult), reads=(rzr, OCr[cc][x]), writes=(OCr[cc][x],))
                    if cc == 0:
                        return
                    g.op("dve", stt(ODt[:].rearrange("p a b -> p (a b)"), OC[:, 1].rearrange("p a b -> p (a b)"), nlam[:, 0:1],
                                    OC[:, 0].rearrange("p a b -> p (a b)"), ALU.mult, ALU.add),
                         reads=(OCr[0][0], OCr[0][1], OCr[1][0], OCr[1][1]), writes=(ODr,))
                    sqs = []
                    for x in range(2):
                        sq, sqr, _ = SQ.next()
                        g.op("act", actf(sq[:], ODt[:, x], AF.Square), reads=(ODr,), writes=(sqr,))
                        sqs.append((sq, sqr))

                    def part2():
                        ssb, ssr = SB.next()
                        for x in range(2):
                            g.op("pe", mm(ssb[:], ones32[:], sqs[x][0][:], x == 0, x == 1), reads=(sqs[x][1],), writes=(ssr,))
                        rd, rdr = recip_act(ssb[:], ssr, scale_in=1.0 / 256, bias_in=epsc[:, 0:1], power=-0.5)
                        for x in range(2):
                            ob, obr, obs = OUTB.next()
                            g.op("dve", stt(ob[:], ODt[:, x], sublnS[:, x:x + 1], rd[:], ALU.mult, ALU.mult), reads=(ODr, rdr), writes=(obr,))
                            g.dma("sp", obs, dmaf(odT[2 * h + x][:, j * 512:(j + 1) * 512], ob[:]), reads=(obr,))
                    deferred.append([pidx + 5, part2])
                deferred.append([pidx + 2, part1])
            return epi

        def mk_mla_epi(h, j, Ob, par):
            def epi(pidx):
                raw, rawr, _ = TMP.next()
                g.op("dve", cpy(raw[:], ps_all[:, Ob[0]]), reads=(PSr[Ob[0]],), writes=(rawr,))

                def part1():
                    zb, zbr = SB.next()
                    g.op("pe", [mm(zb[:], ones32[:], ZA[:, par, 0], True, False), mm(zb[:], ones32[:], ZA[:, par, 1], False, True)],
                         reads=(ZAr[par][0], ZAr[par][1]), writes=(zbr,))
                    rz, rzr = recip_act(zb[:], zbr)
                    ob, obr, obs = OUTB.next()
                    g.op("dve", tt(ob[:], raw[:], rz[:], ALU.mult), reads=(rawr, rzr), writes=(obr,))
                    g.dma("sp", obs, dmaf(omT[h][:, j * 512:(j + 1) * 512], ob[:]), reads=(obr,))
                deferred.append([pidx + 2, part1])
            return epi

        mla_blk = 0
        for hh in range(total_heads):
            hb = HB[hh % 2]
            first_of_head = True
            if hh < HD:
                h = hh
                ch = -slopes[h] / dscale
                for j in range(NJ):
                    for cc in range(2):
                        Ob, Zb = [4, 5], 6
                        for i in range(NKT):
                            if i == NKT - 1:
                                bias = None
                            elif i < NOT:
                                s0 = 512 * j - 128 * i + (TOWN - 128)
                                bias = MOWN[:, s0:s0 + 512]
                            else:
                                s0 = 512 * j - 128 * i + (S - 128)
                                bias = MOTH[:, s0:s0 + 512]
                            it = dict(hb=hb, hh=hh, i=i, bias=bias, ch=ch, scale=dscale, ne=2, Ob=Ob, Zb=Zb,
                                      kq=[(hb["KT"][:, cc, i * 128:(i + 1) * 128], hb["QT"][:, cc, j * 512:(j + 1) * 512])],
                                      epi=(mk_diff_epi(h, j, cc, Ob, Zb) if i == NKT - 1 else None), pre=first_of_head)
                            first_of_head = False
                            items.append(it)
            else:
                h = hh - HD
                for j in range(NJ):
                    Ob, par = [4 + (mla_blk % 4)], mla_blk % 2
                    mla_blk += 1
                    for i in range(NKT):
                        it = dict(hb=hb, hh=hh, i=i, bias=None, ch=0.0, scale=mscale, ne=1, Ob=Ob, Zb=None, zpar=par,
                                  kq=[(hb["KT"][:, 0, i * 128:(i + 1) * 128], hb["QT"][:, 0, j * 512:(j + 1) * 512]),
                                      (KPE[:, i * 128:(i + 1) * 128], hb["QP"][:, j * 512:(j + 1) * 512])],
                                  epi=(mk_mla_epi(h, j, Ob, par) if i == NKT - 1 else None), pre=first_of_head)
                        first_of_head = False
                        items.append(it)

        def issue_S(it):
            p, pr = SB.next()
            nk = len(it["kq"])
            g.op("pe", [mm(p[:], k_, q_, x == 0, x == nk - 1) for x, (k_, q_) in enumerate(it["kq"])],
                 reads=(it["hb"]["res"], tabr), writes=(pr,))
            it["S"] = (p, pr)

        def prefetch(hh):
            if hh >= total_heads:
                return
            if hh < HD:
                load_diff(hh, HB[hh % 2])
            else:
                load_mla(hh - HD, HB[hh % 2])

        prefetch(0)
        n_items = len(items)
        AHEAD = 2

        def stage1(it):
            p, pr = it["S"]
            if it["bias"] is not None:
                t, tr, _ = T.next()
                g.op("dve", stt(t[:], it["bias"], it["ch"], p[:], ALU.mult, ALU.add), reads=(pr, tabr), writes=(tr,))
                srcp, srcr = t, tr
            else:
                srcp, srcr = p, pr
            e_, er, _ = E.next()
            g.op("act", actf(e_[:], srcp[:], AF.Exp, scale=it["scale"]), reads=(srcr,), writes=(er,))
            it["E"] = (e_, er)
            if it["Zb"] is None:
                i = it["i"]
                par, half = it["zpar"], i % 2
                rows = 16 if i == NKT - 1 else 128
                if i < 2:
                    g.op("dve", cpy(ZA[:rows, par, half], e_[:rows]), reads=(er, ZAr[par][half]), writes=(ZAr[par][half],))
                else:
                    g.op("dve", tt(ZA[:rows, par, half], ZA[:rows, par, half], e_[:rows], ALU.add), reads=(er, ZAr[par][half]),
                         writes=(ZAr[par][half],))

        def run_deferred(pidx):
            k = 0
            while k < len(deferred):
                if deferred[k][0] <= pidx:
                    deferred.pop(k)[1]()
                    k = 0
                else:
                    k += 1

        for pidx in range(min(LOOK, n_items)):
            issue_S(items[pidx])
        for pidx in range(min(AHEAD, n_items)):
            stage1(items[pidx])
        for pidx in range(n_items):
            it = items[pidx]
            if it["pre"]:
                prefetch(it["hh"] + 1)
            hb = it["hb"]
            i = it["i"]
            e_, er = it["E"]
            first, last = (i == 0), (i == NKT - 1)
            fns = [mm(ps_all[:, it["Ob"][x]], hb["V"][:, i, x * 128:(x + 1) * 128], e_[:], first, last) for x in range(it["ne"])]
            wb = list(it["Ob"][:it["ne"]])
            if it["Zb"] is not None:
                fns.append(mm(ps_all[:, it["Zb"]], (onesmeta if last else onesbf)[:], e_[:], first, last))
                wb.append(it["Zb"])
            g.op("pe", fns, reads=(er, hb["res"]), writes=tuple(PSr[k] for k in wb))
            if pidx + LOOK < n_items:
                issue_S(items[pidx + LOOK])
            if it["epi"] is not None:
                it["epi"](pidx)
            if pidx + AHEAD < n_items:
                stage1(items[pidx + AHEAD])
            run_deferred(pidx)
        run_deferred(10 ** 9)
        ar.reset(m0)

    def phaseC(st, tok0):
        N = 512
        XG, XGr, PS, TMP, H1, XIN = st["XG"], st["XGr"], st["PS"], st["TMP"], st["H1"], st["XIN"]
        OD, OM, MG, MGr, ODr = st["OD"], st["OM"], st["MG"], st["MGr"], st["ODr"]
        R2, R2r = st["R2"], st["R2r"]
        ss0, ss0r = ps_all[:, 0], PSr[0]
        ss1, ss1r = ps_all[:, 1], PSr[1]
        hres = [Res() for _ in range(KC)]
        c5_prev = st["c5"]
        st["c5"] = []
        bf = []
        for k0 in range(0, 2 * HD, 8):
            k1 = min(2 * HD, k0 + 8)
            bf.append(dmaf(OD[:, k0:k1, :], odT[k0:k1, :, tok0:tok0 + N].rearrange("c p t -> p c t")))
        for k0 in range(0, HM, 8):
            k1 = min(HM, k0 + 8)
            bf.append(dmaf(OM[:, k0:k1, :], omT[k0:k1, :, tok0:tok0 + N].rearrange("c p t -> p c t")))
        for k0 in range(0, KC, 8):
            k1 = min(KC, k0 + 8)
            bf.append(dmaf(XG[:, k0:k1, :], hgs[k0:k1, :, tok0:tok0 + N].rearrange("c p t -> p c t")))
        bf.append(dmaf(R2[:], r2s[:, tok0:tok0 + N]))
        g.dma("sp", st["bulks"], bf, writes=(ODr, R2r, *XGr))
        for j in range(KC):
            wa, war = wload(W["wgt"][j], KC * 128)
            wb, wbr_ = wload(W["wgt"][KC + j], KC * 128)
            wc, wcr = wload(W["wbr"][j], NE * 128)
            pgd, pgdr = PS.next()
            g.op("pe", [mm(pgd[:], wa[:, k * 128:(k + 1) * 128], XG[:, k], k == 0, k == KC - 1) for k in range(KC)],
                 reads=(war, *XGr), writes=(pgdr,))
            pgm, pgmr = PS.next()
            g.op("pe", [mm(pgm[:], wb[:, k * 128:(k + 1) * 128], XG[:, k], k == 0, k == KC - 1) for k in range(KC)],
                 reads=(wbr_, *XGr), writes=(pgmr,))
            pbd, pbdr = PS.next()
            g.op("pe", [mm(pbd[:], wc[:, k * 128:(k + 1) * 128], OD[:, k], k == 0, k == 2 * HD - 1) for k in range(2 * HD)],
                 reads=(wcr, ODr), writes=(pbdr,))
            pbm, pbmr = PS.next()
            g.op("pe", [mm(pbm[:], wc[:, (2 * HD + k) * 128:(2 * HD + k + 1) * 128], OM[:, k], k == 0, k == HM - 1) for k in range(HM)],
                 reads=(wcr, ODr), writes=(pbmr,))
            ms = []
            for (pgx, pgxr, pbx, pbxr, bcol) in ((pgd, pgdr, pbd, pbdr, V_BG + j), (pgm, pgmr, pbm, pbmr, V_BG + KC + j)):
                t1, t1r, _ = TMP.next()
                g.op("dve", tt(t1[:], pgx[:], R2[:], ALU.mult), reads=(pgxr, R2r), writes=(t1r,))
                t2, t2r, _ = TMP.next()
                g.op("act", actf(t2[:], t1[:], AF.Sigmoid, bias=vec[:, bcol:bcol + 1]), reads=(t1r,), writes=(t2r,))
                t3, t3r, _ = TMP.next()
                g.op("dve", tt(t3[:], pbx[:], t2[:], ALU.mult), reads=(pbxr, t2r), writes=(t3r,))
                ms.append((t3, t3r))
            g.op("dve", tt(MG[:, j], ms[0][0][:], ms[1][0][:], ALU.add), reads=(ms[0][1], ms[1][1]), writes=(MGr[j],))
            if c5_prev:
                c5_prev.pop(0)()
        while c5_prev:
            c5_prev.pop(0)()
        for j in range(KC):
            w, wr = wload(W["wo"][j], KC * 128)
            xin, xr, xs = XIN.next()
            g.dma("sp", xs, dmaf(xin[:], h1s[j][:, tok0:tok0 + N]), reads=(hres[j],), writes=(xr,))
            p, pr = PS.next()
            g.op("pe", [mm(p[:], w[:, k * 128:(k + 1) * 128], MG[:, k], k == 0, k == KC - 1) for k in range(KC)],
                 reads=(wr, *MGr), writes=(pr,))
            h2, h2r, h2s = H1.next()
            g.op("dve", tt(h2[:], p[:], xin[:], ALU.add), reads=(pr, xr), writes=(h2r,))
            g.dma("sp", h2s, dmaf(h1s[j][:, tok0:tok0 + N], h2[:]), reads=(h2r,), writes=(hres[j],))
            sq_acc(st, h2[:], h2r, N, ss0, ss0r, j == 0, j == KC - 1, which=0)
            g.op("dve", ts(XG[:, j], h2[:], vec[:, V_G2 + j:V_G2 + j + 1], ALU.mult), reads=(h2r,), writes=(XGr[j],))
        rstd(st, ss0, ss0r, N, D, st["R1"], st["R1r"])
        g.fence(("pe", "act", "dve"))

        def resid(j, xin, xr, xs):
            g.dma("sp", xs, dmaf(xin[:], h1s[j][:, tok0:tok0 + N]), reads=(hres[j],), writes=(xr,))

        def post(j, h3, h3r, h3s):
            g.dma("sp", h3s, dmaf(h1s[j][:, tok0:tok0 + N], h3[:]), reads=(h3r,), writes=(hres[j],))
            sq_acc(st, h3[:], h3r, N, ss1, ss1r, j == 0, j == KC - 1, which=1)

        ffn(st, "f2", N, resid, post)
        rstd(st, ss1, ss1r, N, D, st["RF"], st["RFr"])
        g.fence(("pe", "act", "dve", "sp"))

        def c5_step(j):
            xin, xr, xs = XIN.next()
            g.dma("sp", xs, dmaf(xin[:], h1s[j][:, tok0:tok0 + N]), reads=(hres[j],), writes=(xr,))
            y, yr, ys = H1.next()
            g.op("dve", stt(y[:], xin[:], vec[:, V_GF + j:V_GF + j + 1], st["RF"][:], ALU.mult, ALU.mult), reads=(xr, st["RFr"]), writes=(yr,))
            g.dma("sp", ys, dmaf(yT[j][:, tok0:tok0 + N], y[:]), reads=(yr,))
        for j in range(KC):
            st["c5"].append(lambda j=j: c5_step(j))

    st = alloc_AC()
    for t in range(c["NTO"]):
        phaseA(st, [("x", t * 512, 512, t * 512)], True, wmode=("wb" if t == 0 else "bf"))
        if t == 0:
            wb_barrier()
    rest = TOWN + 16
    nto = -(-rest // 512)
    base = -(-(-(-rest // nto)) // 32) * 32
    pos = TOWN
    for k in range(nto):
        n = base if k < nto - 1 else (S - pos)
        seg = [("x", pos, n, pos)]
        if k == nto - 1:
            seg.append(("meta", 0, 16, S))
        assert 0 < sum(x[2] for x in seg) <= 512
        phaseA(st, seg, False, wmode="bf")
        pos += n
    g.fence()
    if c.get("STOP") != "A":
        ar.reset(persist_mark)
        phaseB()
        g.fence()
        if c.get("STOP") != "B":
            ar.reset(ac_mark)
            st = alloc_AC()
            for t in range(c["NTO"]):
                phaseC(st, t * 512)
            while st["c5"]:
                st["c5"].pop(0)()
            g.fence()

    with nc.Block() as block:
        g.emit(block)
    return nc


def lay_cols_fm(Wm, kc):
    n = Wm.shape[1] // 128
    return np.ascontiguousarray(Wm.reshape(kc, 128, n, 128).transpose(2, 1, 0, 3).reshape(n, 128, kc * 128))


def vec_cols(v):
    v = np.asarray(v, np.float32).reshape(-1)
    return v.reshape(-1, 128).T


def swap_halves(Wp):
    h = Wp.shape[-1] // 2
    return np.concatenate([Wp[..., h:], Wp[..., :h]], axis=-1)


def prep_shared(cfg, inp):
    c = cfg
    D, F, HD, HM, QL, KVL = c["D"], c["F"], c["HD"], c["HM"], c["QL"], c["KVL"]
    KC, FC, QC, KVC, CG, NVG, MVG, MVW, HGQ, HGK = (c[k] for k in ("KC", "FC", "QC", "KVC", "CG", "NVG", "MVG", "MVW", "HGQ", "HGK"))
    f32 = np.float32
    sh = {}
    for nm, pre in (("f1", "ffn1"), ("f2", "ffn2")):
        sh[nm + "g"] = lay_cols_fm(np.asarray(inp[pre + "_w_gate"][0], f32), KC)
        sh[nm + "u"] = lay_cols_fm(np.asarray(inp[pre + "_w_up"][0], f32), KC)
        sh[nm + "d"] = lay_cols_fm(np.asarray(inp[pre + "_w_down"][0], f32), FC)
    w_in = np.asarray(inp["w_in"][0], f32)
    QKW = HD * 256
    o_dq, o_dk, o_dv = 0, QKW, 2 * QKW
    o_cq = 3 * QKW
    o_ckv = o_cq + QL
    o_kr = o_ckv + KVL
    sh["wi_dq"] = lay_cols_fm(w_in[:, o_dq:o_dq + QKW], KC)
    sh["wi_dk"] = lay_cols_fm(w_in[:, o_dk:o_dk + QKW], KC)
    sh["wi_cq"] = lay_cols_fm(w_in[:, o_cq:o_cq + QL], KC)
    sh["wi_ckv"] = lay_cols_fm(w_in[:, o_ckv:o_ckv + KVL], KC)
    wkr = w_in[:, o_kr:o_kr + 64]
    sh["wi_kr"] = lay_cols_fm(np.concatenate([wkr, swap_halves(wkr)], axis=1), KC)
    wv = w_in[:, o_dv:o_dv + QKW]
    sh["wi_dv"] = np.ascontiguousarray(
        wv.reshape(KC // CG, CG, 128, NVG, 512).transpose(3, 0, 2, 1, 4).reshape(NVG * (KC // CG), 128, CG * 512))
    wuq = np.asarray(inp["mla_w_uq"][0], f32).reshape(QL, HM, 192)
    wn = np.ascontiguousarray(wuq[:, :, :128]).reshape(QL, HM * 128)
    pn = lay_cols_fm(wn, QC)
    sh["uq_n"] = np.ascontiguousarray(pn.reshape(HM // HGQ, HGQ, 128, QC * 128).transpose(0, 2, 1, 3).reshape(HM // HGQ, 128, HGQ * QC * 128))
    wp = wuq[:, :, 128:192]
    wpp = np.concatenate([wp, swap_halves(wp)], axis=2).reshape(QL, HM * 128)
    pp = lay_cols_fm(np.ascontiguousarray(wpp), QC)
    sh["uq_p"] = np.ascontiguousarray(pp.reshape(HM // HGQ, HGQ, 128, QC * 128).transpose(0, 2, 1, 3).reshape(HM // HGQ, 128, HGQ * QC * 128))
    wukv = np.asarray(inp["mla_w_ukv"][0], f32).reshape(KVL, HM, 256)
    wkn = np.ascontiguousarray(wukv[:, :, :128]).reshape(KVL, HM * 128)
    pk = lay_cols_fm(wkn, KVC)
    sh["ukv_n"] = np.ascontiguousarray(pk.reshape(HM // HGK, HGK, 128, KVC * 128).transpose(0, 2, 1, 3).reshape(HM // HGK, 128, HGK * KVC * 128))
    wvv = np.ascontiguousarray(wukv[:, :, 128:]).reshape(KVC, 128, HM // MVG, MVW)
    sh["ukv_v"] = np.ascontiguousarray(wvv.transpose(2, 1, 0, 3).reshape(HM // MVG, 128, KVC * MVW))
    sh["wgt"] = lay_cols_fm(np.asarray(inp["w_gate"][0], f32), KC)
    bd = lay_cols_fm(np.asarray(inp["w_branch_diff"][0], f32), 2 * HD)
    bm = lay_cols_fm(np.asarray(inp["w_branch_mla"][0], f32), HM)
    sh["wbr"] = np.ascontiguousarray(np.concatenate([bd, bm], axis=2))
    sh["wo"] = lay_cols_fm(np.asarray(inp["w_out"][0], f32), KC)
    cols = [vec_cols(inp["ffn1_norm"][0]), vec_cols(inp["mix_norm"][0]), vec_cols(inp["ffn2_norm"][0]), vec_cols(inp["final_norm"]),
            vec_cols(inp["mla_q_norm"][0]), vec_cols(inp["mla_kv_norm"][0]), vec_cols(inp["diff_subln"][0]), vec_cols(inp["b_gate"][0]),
            vec_cols(inp["diff_lambda_q1"][0]), vec_cols(inp["diff_lambda_k1"][0]), vec_cols(inp["diff_lambda_q2"][0]),
            vec_cols(inp["diff_lambda_k2"][0])]
    sh["vecs"] = np.ascontiguousarray(np.concatenate(cols, axis=1).astype(f32))
    assert sh["vecs"].shape[1] == c["NV"]
    sh["ident"] = np.eye(128, dtype=f32)
    return sh


def prep_core(cfg, inp, core):
    c = cfg
    S, TOWN, KC, TK = c["S"], c["TOWN"], c["KC"], c["TK"]
    f32 = np.float32
    b, half = core // 2, core % 2
    x = np.asarray(inp["x"][b], f32)
    order = np.concatenate([np.arange(half * TOWN, (half + 1) * TOWN), np.arange((1 - half) * TOWN, (2 - half) * TOWN)])
    pc = {}
    pc["xT"] = np.ascontiguousarray(x[order].T.reshape(KC, 128, S))
    pc["metaT"] = np.ascontiguousarray(np.asarray(inp["meta_tokens"], f32).T.reshape(KC, 128, 16))
    pos = np.concatenate([16 + order, np.arange(16)]).astype(f32)
    inv_freq = (1.0 / (ROPE_THETA ** (np.arange(0, 64, 2, dtype=f32) / f32(64)))).astype(f32)
    ang = (pos[None, :] * inv_freq[:, None]).astype(f32)
    cs, sn = np.cos(ang).astype(f32), np.sin(ang).astype(f32)
    pc["ropeC"] = np.ascontiguousarray(np.concatenate([cs, cs], axis=0))
    pc["ropeS"] = np.ascontiguousarray(np.concatenate([-sn, sn], axis=0))
    kk = np.arange(128, dtype=np.int64)[:, None]
    u = np.arange(c["WOWN"], dtype=np.int64)[None, :] - (TOWN - 128)
    pc["mown"] = np.abs(u - kk).astype(f32)
    u = np.arange(c["WOTH"], dtype=np.int64)[None, :] - (S - 128)
    pc["moth"] = ((kk - u) if half == 0 else (S + u - kk)).astype(f32)
    return pc


def run_cfg(cfg, inp, trace=False):
    nc = build(cfg)
    sh = prep_shared(cfg, inp)
    in_maps = []
    for core in range(NCORES):
        m = dict(sh)
        m.update(prep_core(cfg, inp, core))
        in_maps.append(m)
    res = run_bass_kernel_spmd(nc, in_maps, core_ids=list(range(NCORES)), trace=trace)
    S, TOWN, D = cfg["S"], cfg["TOWN"], cfg["D"]
    out = np.empty((cfg["B"], S, D), np.float32)
    for core in range(NCORES):
        b, half = core // 2, core % 2
        y = np.asarray(res.results[core]["yT"]).reshape(D, TOWN)
        out[b, half * TOWN:(half + 1) * TOWN, :] = y.T
    return out, res


def kernel(**inputs):
    cfg = make_cfg()
    out, _ = run_cfg(cfg, inputs)
    return out
```
